# Optimizing a Trainium2 kernel written in Bass

```python
import math, functools
import jax, jax.numpy as jnp
from jax import lax
import numpy as np

D_MODEL = 2048
BATCH = 32
SEQ = 256
DEPTH = 2
DEC_BATCH = 2
DEC_SEQ = 4096
PAST_LEN = 256

GRID_W = 64
HEAD_DIM = 128
A_HEADS = 8
A_KV_HEADS = 2
A_GROUP = A_HEADS // A_KV_HEADS
B_HEADS = 4
B_V_DIM = 2 * HEAD_DIM
ATTN_SPLITS = [A_HEADS * HEAD_DIM, A_KV_HEADS * HEAD_DIM, A_KV_HEADS * HEAD_DIM,
               B_HEADS * 2 * HEAD_DIM, B_HEADS * 2 * HEAD_DIM, B_HEADS * B_V_DIM]
ATTN_IN = sum(ATTN_SPLITS)
Q_BLOCK = 128
ROPE_BASE = 10000.0
ROPE_PAIRS_PER_AXIS = HEAD_DIM // 4
S5_GROUP_CH = 16
S5_GROUPS = D_MODEL // S5_GROUP_CH
S5_STATE = 64
D_FF = 4 * D_MODEL
N_ATTN_LAYERS = (DEPTH + 1) // 2
N_SSM_LAYERS = DEPTH // 2
N_MOD = 6
EPS = 1e-6
F32 = jnp.float32

kernel_name = "hybrid_diffusion_gqa_diffattn_s5_step"


def _rms(x, g):
    xf = x.astype(F32)
    y = xf * lax.rsqrt(jnp.mean(xf * xf, axis=-1, keepdims=True) + EPS)
    return (y * g.astype(F32)).astype(x.dtype)


def _modulation(cond, ada_w, ada_b):
    m = jax.nn.silu(cond) @ ada_w + ada_b
    return [t[:, None, :] for t in jnp.split(m, N_MOD, axis=-1)]


def _modulate(x, g, shift, scale):
    return _rms(x, g) * (1.0 + scale) + shift


def _mlp(n, w1, w2):
    h = jax.nn.relu(n @ w1)
    return (h * h) @ w2


def _axial_rope(L):
    rows = L // GRID_W
    row = jnp.repeat(jnp.arange(rows, dtype=F32), GRID_W)
    col = jnp.tile(jnp.arange(GRID_W, dtype=F32), rows)
    inv = ROPE_BASE ** (-jnp.arange(ROPE_PAIRS_PER_AXIS, dtype=F32) / ROPE_PAIRS_PER_AXIS)
    ang = jnp.concatenate([row[:, None] * inv, col[:, None] * inv], axis=-1)
    return jnp.cos(ang), jnp.sin(ang)


def _apply_rope(x, rope):
    cos, sin = rope
    shp = (1, x.shape[1]) + (1,) * (x.ndim - 3) + (HEAD_DIM // 2,)
    cos, sin = cos.reshape(shp), sin.reshape(shp)
    xf = x.astype(F32)
    x1, x2 = xf[..., :HEAD_DIM // 2], xf[..., HEAD_DIM // 2:]
    return jnp.concatenate([x1 * cos - x2 * sin, x1 * sin + x2 * cos], axis=-1).astype(x.dtype)


def _attn_qkv(n, w_in, qk_g, rope):
    B, L, _ = n.shape
    idx = [int(i) for i in np.cumsum(ATTN_SPLITS)[:-1]]
    qa, ka, va, qb, kb, vb = jnp.split(n @ w_in, idx, axis=-1)
    qa = _rms(qa.reshape(B, L, A_HEADS, HEAD_DIM), qk_g[0])
    ka = _rms(ka.reshape(B, L, A_KV_HEADS, HEAD_DIM), qk_g[1])
    va = va.reshape(B, L, A_KV_HEADS, HEAD_DIM)
    qb = qb.reshape(B, L, B_HEADS, 2, HEAD_DIM)
    kb = kb.reshape(B, L, B_HEADS, 2, HEAD_DIM)
    vb = vb.reshape(B, L, B_HEADS, B_V_DIM)
    if rope is not None:
        qa, ka, qb, kb = (_apply_rope(t, rope) for t in (qa, ka, qb, kb))
    return qa, ka, va, qb, kb, vb


def _gqa_block(q, k, v):
    B, Q = q.shape[:2]
    qg = q.reshape(B, Q, A_KV_HEADS, A_GROUP, HEAD_DIM)
    s = jnp.einsum("bqhgd,bkhd->bhgqk", qg, k).astype(F32) * (HEAD_DIM ** -0.5)
    p = jax.nn.softmax(s, axis=-1).astype(v.dtype)
    o = jnp.einsum("bhgqk,bkhd->bqhgd", p, v)
    return o.reshape(B, Q, A_HEADS * HEAD_DIM)


def _diff_block(q, k, v, lam):
    s = jnp.einsum("bqhmd,bkhmd->bhmqk", q, k).astype(F32) * (HEAD_DIM ** -0.5)
    p = jax.nn.softmax(s, axis=-1)
    w = (p[:, :, 0] - lam * p[:, :, 1]).astype(v.dtype)
    return jnp.einsum("bhqk,bkhe->bqhe", w, v)


def _sweep(fn, q):
    B, L = q.shape[:2]
    nb = L // Q_BLOCK
    qb = jnp.moveaxis(q.reshape((B, nb, Q_BLOCK) + q.shape[2:]), 1, 0)
    o = lax.map(fn, qb)
    return jnp.moveaxis(o, 0, 1).reshape((B, L) + o.shape[3:])


def _attn_mix(qa, ka, va, qb, kb, vb, lam_p, lam_init, subln_g, w_out):
    lp = lam_p.astype(F32)
    lam = jnp.exp(jnp.sum(lp[0] * lp[1])) - jnp.exp(jnp.sum(lp[2] * lp[3])) + lam_init
    oa = _sweep(lambda q: _gqa_block(q, ka, va), qa)
    ob = _sweep(lambda q: _diff_block(q, kb, vb, lam), qb)
    ob = _rms(ob, subln_g) * (1.0 - lam_init)
    B, L = oa.shape[:2]
    o = jnp.concatenate([oa, ob.reshape(B, L, B_HEADS * B_V_DIM)], axis=-1)
    return o @ w_out


def _ssm_combine(x, y):
    a1, b1 = x
    a2, b2 = y
    return a1 * a2, a2 * b1 + b2


def _s5_scan(u, a_re, a_im, log_dt, b_re, b_im, c_re, c_im, h0):
    Bsz, L, _ = u.shape
    lam = lax.complex(a_re.astype(F32), a_im.astype(F32))
    dt = jnp.exp(log_dt.astype(F32))[:, None]
    a_bar = jnp.exp(lam * dt)
    b_bar = ((a_bar - 1.0) / lam)[..., None] * lax.complex(b_re.astype(F32), b_im.astype(F32))
    ug = u.astype(F32).reshape(Bsz, L, S5_GROUPS, S5_GROUP_CH).astype(jnp.complex64)
    bu = jnp.einsum("gnc,blgc->blgn", b_bar, ug)
    if h0 is not None:
        bu = bu.at[:, 0].add(a_bar * h0)
    a_seq = jnp.broadcast_to(a_bar, bu.shape)
    _, h = lax.associative_scan(_ssm_combine, (a_seq, bu), axis=1)
    cc = lax.complex(c_re.astype(F32), c_im.astype(F32))
    y = jnp.einsum("gcn,blgn->blgc", cc, h).real
    return y.reshape(Bsz, L, D_MODEL), h[:, -1]


def _ssm_mix(n, sp, h0):
    w_in, a_re, a_im, log_dt, b_re, b_im, c_re, c_im, d, glu_w, w_out = sp
    u = n @ w_in
    per_dir = lambda k: (a_re[k], a_im[k], log_dt[k], b_re[k], b_im[k], c_re[k], c_im[k])
    y_f, h_f = _s5_scan(u, *per_dir(0), None if h0 is None else h0[:, 0])
    y_b, h_b = _s5_scan(u[:, ::-1], *per_dir(1), None if h0 is None else h0[:, 1])
    y = y_f + y_b[:, ::-1] + d.astype(F32) * u.astype(F32)
    z = jax.nn.gelu(y).astype(n.dtype)
    z = z * jax.nn.sigmoid(z @ glu_w)
    return z @ w_out, jnp.stack([h_f, h_b], axis=1)


def setup_inputs(seed: int = 0) -> dict:
    key = jax.random.key(seed)
    ks = iter(jax.random.split(key, 40))
    nrm = lambda shape, s: jax.random.normal(next(ks), shape, F32) * s
    D = D_MODEL
    G, N, C = S5_GROUPS, S5_STATE, S5_GROUP_CH
    return {
        "x_prompt": nrm((BATCH, SEQ, D), 1.0),
        "x_sample": nrm((DEC_BATCH, DEC_SEQ, D), 1.0),
        "c": nrm((DEC_BATCH, D), 1.0),
        "cache_a_k": nrm((DEC_BATCH, N_ATTN_LAYERS, PAST_LEN, A_KV_HEADS, HEAD_DIM), 1.0),
        "cache_a_v": nrm((DEC_BATCH, N_ATTN_LAYERS, PAST_LEN, A_KV_HEADS, HEAD_DIM), 1.0),
        "cache_b_k": nrm((DEC_BATCH, N_ATTN_LAYERS, PAST_LEN, B_HEADS, 2, HEAD_DIM), 1.0),
        "cache_b_v": nrm((DEC_BATCH, N_ATTN_LAYERS, PAST_LEN, B_HEADS, B_V_DIM), 1.0),
        "state_ssm": nrm((DEC_BATCH, N_SSM_LAYERS, 2, G, N, 2), 0.3),
        "c_ctx": nrm((D,), 1.0),
        "ada_w": nrm((DEPTH, D, N_MOD * D), D ** -0.5),
        "ada_b": nrm((DEPTH, N_MOD * D), 0.02),
        "norm_g": 1.0 + nrm((DEPTH, 4, D), 0.02),
        "mlp_w1": nrm((DEPTH, D, D_FF), D ** -0.5),
        "mlp_w2": nrm((DEPTH, D_FF, D), D_FF ** -0.5),
        "attn_w_in": nrm((N_ATTN_LAYERS, D, ATTN_IN), D ** -0.5),
        "attn_w_out": nrm((N_ATTN_LAYERS, D, D), D ** -0.5),
        "attn_qk_norm": 1.0 + nrm((N_ATTN_LAYERS, 2, HEAD_DIM), 0.02),
        "diff_lambda": nrm((N_ATTN_LAYERS, 4, HEAD_DIM), 0.1),
        "diff_subln": 1.0 + nrm((N_ATTN_LAYERS, B_V_DIM), 0.02),
        "ssm_w_in": nrm((N_SSM_LAYERS, D, D), D ** -0.5),
        "ssm_a_re": -0.5 + nrm((N_SSM_LAYERS, 2, G, N), 0.01),
        "ssm_a_im": math.pi * jnp.arange(N, dtype=F32) + nrm((N_SSM_LAYERS, 2, G, N), 0.01),
        "ssm_log_dt": jax.random.uniform(next(ks), (N_SSM_LAYERS, 2, G), F32, math.log(1e-3), math.log(1e-1)),
        "ssm_b_re": nrm((N_SSM_LAYERS, 2, G, N, C), (2 * C) ** -0.5),
        "ssm_b_im": nrm((N_SSM_LAYERS, 2, G, N, C), (2 * C) ** -0.5),
        "ssm_c_re": nrm((N_SSM_LAYERS, 2, G, C, N), (2 * N) ** -0.5),
        "ssm_c_im": nrm((N_SSM_LAYERS, 2, G, C, N), (2 * N) ** -0.5),
        "ssm_d": nrm((N_SSM_LAYERS, D), 1.0),
        "ssm_glu_w": nrm((N_SSM_LAYERS, D, D), D ** -0.5),
        "ssm_w_out": nrm((N_SSM_LAYERS, D, D), D ** -0.5),
    }


def reference(x_prompt, x_sample, c, cache_a_k, cache_a_v, cache_b_k, cache_b_v, state_ssm, c_ctx,
              ada_w, ada_b, norm_g, mlp_w1, mlp_w2, attn_w_in, attn_w_out, attn_qk_norm, diff_lambda,
              diff_subln, ssm_w_in, ssm_a_re, ssm_a_im, ssm_log_dt, ssm_b_re, ssm_b_im, ssm_c_re,
              ssm_c_im, ssm_d, ssm_glu_w, ssm_w_out):
    rope = _axial_rope(x_sample.shape[1])
    xp, xs = x_prompt, x_sample
    new_ak, new_av, new_bk, new_bv, new_ssm = [], [], [], [], []
    cat = lambda lat, ctx: jnp.concatenate([lat, ctx.astype(lat.dtype)], axis=1)
    for l in range(DEPTH):
        g = norm_g[l]
        mp = _modulation(c_ctx[None, :], ada_w[l], ada_b[l])
        ms = _modulation(c, ada_w[l], ada_b[l])
        n_p = _modulate(xp, g[0], mp[0], mp[1])
        n_s = _modulate(xs, g[0], ms[0], ms[1])
        i = l // 2
        if l % 2 == 0:
            lam_init = 0.8 - 0.6 * math.exp(-0.3 * l)
            mix = functools.partial(_attn_mix, lam_p=diff_lambda[i], lam_init=lam_init,
                                    subln_g=diff_subln[i], w_out=attn_w_out[i])
            qa, ka, va, qb, kb, vb = _attn_qkv(n_p, attn_w_in[i], attn_qk_norm[i], None)
            o_p = mix(qa, ka, va, qb, kb, vb)
            new_ak.append(ka)
            new_av.append(va)
            new_bk.append(kb)
            new_bv.append(vb)
            qa, ka, va, qb, kb, vb = _attn_qkv(n_s, attn_w_in[i], attn_qk_norm[i], rope)
            o_s = mix(qa, cat(ka, cache_a_k[:, i]), cat(va, cache_a_v[:, i]),
                      qb, cat(kb, cache_b_k[:, i]), cat(vb, cache_b_v[:, i]))
        else:
            sp = (ssm_w_in[i], ssm_a_re[i], ssm_a_im[i], ssm_log_dt[i], ssm_b_re[i], ssm_b_im[i],
                  ssm_c_re[i], ssm_c_im[i], ssm_d[i], ssm_glu_w[i], ssm_w_out[i])
            o_p, h_ctx = _ssm_mix(n_p, sp, None)
            new_ssm.append(jnp.stack([h_ctx.real, h_ctx.imag], axis=-1))
            st = state_ssm[:, i].astype(F32)
            o_s, _ = _ssm_mix(n_s, sp, lax.complex(st[..., 0], st[..., 1]))
        xp = xp + mp[2] * _rms(o_p, g[1])
        xs = xs + ms[2] * _rms(o_s, g[1])
        xp = xp + mp[5] * _rms(_mlp(_modulate(xp, g[2], mp[3], mp[4]), mlp_w1[l], mlp_w2[l]), g[3])
        xs = xs + ms[5] * _rms(_mlp(_modulate(xs, g[2], ms[3], ms[4]), mlp_w1[l], mlp_w2[l]), g[3])
    new_cache_a_k = jnp.stack(new_ak, axis=1)
    new_cache_a_v = jnp.stack(new_av, axis=1)
    new_cache_b_k = jnp.stack(new_bk, axis=1)
    new_cache_b_v = jnp.stack(new_bv, axis=1)
    new_state_ssm = jnp.stack(new_ssm, axis=1)
    return (xp, xs, new_cache_a_k, new_cache_a_v, new_cache_b_k, new_cache_b_v, new_state_ssm)
```

```python
import math
from contextlib import ExitStack

import numpy as np
import concourse.bass as bass
import concourse.mybir as mybir
from concourse.bass_utils import run_bass_kernel_spmd

F32 = mybir.dt.float32
BF16 = mybir.dt.bfloat16
I32 = mybir.dt.int32
AF = mybir.ActivationFunctionType
ALU = mybir.AluOpType
AX = mybir.AxisListType

D = 2048
KC = 16
TT = 512
NPT = 1024
NTILE = NPT // TT
EPS = 1e-6
ATTN_IN = 4608

COMPUTE = ("pe", "act", "dve", "pool")
QUEUES = ("sp", "act", "pool")


class Sched:
    def __init__(self, nc, dma_pool=8):
        self.nc = nc
        self.ops = []
        self.last_writer = {}
        self.readers = {}
        self.dma_pool = dma_pool
        self.fences = []

    def fence(self):
        self.fences.append(len(self.ops))
        self.last_writer.clear()
        self.readers.clear()

    def add(self, eng, fn, reads=(), writes=(), dma=False):
        idx = len(self.ops)
        deps = set()
        for r in reads:
            w = self.last_writer.get(r)
            if w is not None:
                deps.add(w)
        for w_ in writes:
            w = self.last_writer.get(w_)
            if w is not None:
                deps.add(w)
            for rd in self.readers.get(w_, ()):
                deps.add(rd)
        deps.discard(idx)
        self.ops.append(dict(eng=eng, fn=fn, deps=deps, dma=dma, need_inc=dma, idx=idx))
        for r in reads:
            lst = self.readers.setdefault(r, [])
            if not dma:
                lst[:] = [x for x in lst if self.ops[x]["dma"] or self.ops[x]["eng"] != eng]
            lst.append(idx)
        for w_ in writes:
            self.last_writer[w_] = idx
            self.readers[w_] = []
        return idx

    def pe(self, fn, reads=(), writes=()):
        return self.add("pe", fn, reads, writes)

    def act(self, fn, reads=(), writes=()):
        return self.add("act", fn, reads, writes)

    def dve(self, fn, reads=(), writes=()):
        return self.add("dve", fn, reads, writes)

    def pool(self, fn, reads=(), writes=()):
        return self.add("pool", fn, reads, writes)

    def dma(self, q, fn, reads=(), writes=()):
        return self.add(q, fn, reads, writes, dma=True)

    def _skip(self, dop, op):
        return (not dop["dma"]) and (not op["dma"]) and dop["eng"] == op["eng"] == "pe"

    def emit(self, ctx):
        nc = self.nc
        import os as _os
        mx = int(_os.environ.get("KMAXOPS", "0"))
        if mx > 0:
            self.ops = self.ops[:mx]
            self.fences = [F for F in self.fences if F <= mx]
        ops = self.ops
        for op in ops:
            for d in op["deps"]:
                dop = ops[d]
                if dop["dma"] or self._skip(dop, op):
                    continue
                dop["need_inc"] = True
        for F in self.fences:
            last = {}
            for op in ops[:F]:
                if not op["dma"]:
                    last[op["eng"]] = op
            for op in last.values():
                op["need_inc"] = True
        sems = {e: ctx.enter_context(nc.semaphore("s_" + e)) for e in COMPUTE}
        dsems = {q: [ctx.enter_context(nc.semaphore("d_%s%d" % (q, i))) for i in range(self.dma_pool)]
                 for q in QUEUES}
        tick = {e: 0 for e in COMPUTE}
        dcount = {q: 0 for q in QUEUES}
        duse = {q: [0] * self.dma_pool for q in QUEUES}
        for op in ops:
            if op["dma"]:
                q = op["eng"]
                s = dcount[q] % self.dma_pool
                dcount[q] += 1
                op["slot_prev"] = duse[q][s] * 16
                duse[q][s] += 1
                op["sem"] = dsems[q][s]
                op["val"] = duse[q][s] * 16
            elif op["need_inc"]:
                tick[op["eng"]] += 1
                op["sem"] = sems[op["eng"]]
                op["val"] = tick[op["eng"]]
        by_eng = {}
        for op in ops:
            by_eng.setdefault(op["eng"], []).append(op)
        fence_waits = []
        for F in self.fences:
            fw = {}
            for op in ops[:F]:
                if op["dma"] or op["need_inc"]:
                    k = id(op["sem"])
                    if k not in fw or fw[k][1] < op["val"]:
                        fw[k] = (op["sem"], op["val"])
            fence_waits.append(list(fw.values()))
        final = []
        for q in QUEUES:
            for s in range(self.dma_pool):
                if duse[q][s]:
                    final.append((dsems[q][s], duse[q][s] * 16))

        def run_engine(eng_name, handle):
            waited = {}
            fi = 0
            for op in by_eng.get(eng_name, []):
                waits = []
                while fi < len(self.fences) and op["idx"] >= self.fences[fi]:
                    waits.extend(fence_waits[fi])
                    fi += 1
                for d in sorted(op["deps"]):
                    dop = ops[d]
                    if self._skip(dop, op):
                        continue
                    waits.append((dop["sem"], dop["val"]))
                if op["dma"] and op["slot_prev"] > 0:
                    waits.append((op["sem"], op["slot_prev"]))
                best = {}
                for sem, val in waits:
                    k = id(sem)
                    if k not in best or best[k][1] < val:
                        best[k] = (sem, val)
                for k, (sem, val) in best.items():
                    if waited.get(k, 0) >= val:
                        continue
                    waited[k] = val
                    handle.wait_ge(sem, val)
                ins = op["fn"](handle)
                if op["dma"]:
                    ins.then_inc(op["sem"], 16)
                elif op["need_inc"]:
                    ins.then_inc(op["sem"], 1)
            if eng_name == "sp":
                for sem, val in final:
                    handle.wait_ge(sem, val)

        with nc.Block() as block:
            @block.sync
            def _(e):
                run_engine("sp", e)

            @block.tensor
            def _(e):
                run_engine("pe", e)

            @block.scalar
            def _(e):
                run_engine("act", e)

            @block.vector
            def _(e):
                run_engine("dve", e)

            @block.gpsimd
            def _(e):
                run_engine("pool", e)


import os
SAMPLE = os.environ.get('KSAMPLE', '1') == '1'
STAGE = int(os.environ.get('KSTAGE', '9'))
NTOK = 5120 if SAMPLE else 1024
NT_ALL = NTOK // TT
NMAIN = 2048 if SAMPLE else 1024
MAIN_TILES = [0, 1, 2, 3] if SAMPLE else [0, 1]
NKEY = 4096 + 256
QSCALE = 128.0 ** -0.5
LAM_INIT0 = 0.8 - 0.6 * math.exp(-0.3 * 0)
SEG = 256


class Prog:
    def __init__(self):
        self.nc = bass.Bass("TRN2", target_bir_lowering=False)
        self.ctx = None
        self.S = Sched(self.nc)
        self.nbank = 0
        self.nslab = 0
        self.cp = 0
        self.rot = {}

    def din(self, name, shape, dt=F32):
        return self.nc.dram_tensor(name, list(shape), dt, kind="ExternalInput").ap()

    def dout(self, name, shape):
        return self.nc.dram_tensor(name, list(shape), F32, kind="ExternalOutput").ap()

    def dscr(self, name, shape, dt):
        return self.nc.dram_tensor(name, list(shape), dt).ap()

    def sb(self, name, shape, dt=F32):
        return self.ctx.enter_context(self.nc.sbuf_tensor(name, list(shape), dt))

    def bank(self):
        i = self.nbank % 8
        self.nbank += 1
        return i

    def rr(self, name, n):
        i = self.rot.get(name, 0)
        self.rot[name] = i + 1
        return i % n

    def copy(self, out, in_, reads, writes, eng=None):
        S = self.S
        if eng is None:
            eng = "dve" if (self.cp % 2 == 0) else "act"
            self.cp += 1
        if eng == "dve":
            S.dve(lambda e: e.tensor_copy(out, in_), reads=reads, writes=writes)
        elif eng == "pool":
            S.pool(lambda e: e.tensor_copy(out, in_), reads=reads, writes=writes)
        else:
            S.act(lambda e: e.activation(out, in_, AF.Copy), reads=reads, writes=writes)

    def build(self):
        nc = self.nc
        S = self.S
        P = self
        TT_ = TT
        xall = P.din("xall", [NTOK, D])
        cond = P.din("cond", [2, D])
        ident_d = P.din("ident", [128, 128])
        jmat_d = P.din("jmat", [128, 128])
        maskB_d = P.din("maskB", [4, 128, 8])
        maskC_d = P.din("maskC", [4, 128, 2])
        ada_w = P.din("ada_w", [2, D, 6 * D])
        ada_b = P.din("ada_b", [2, 6 * D])
        norm_g = P.din("norm_g", [2, 4, D])
        mlp_w1 = P.din("mlp_w1", [2, D, 4 * D])
        mlp_w2 = P.din("mlp_w2", [2, 4 * D, D])
        w_in = P.din("attn_w_in", [D, ATTN_IN])
        w_out = P.din("attn_w_out", [D, D])
        qkn = P.din("attn_qk_norm", [2, 128])
        dlam = P.din("diff_lambda", [4, 128])
        dsub = P.din("diff_subln", [256])
        s_w_in = P.din("ssm_w_in", [D, D])
        s_are = P.din("ssm_a_re", [2, 128, 64])
        s_aim = P.din("ssm_a_im", [2, 128, 64])
        s_ldt = P.din("ssm_log_dt", [2, 128])
        s_bre = P.din("ssm_b_re", [2, 128, 64, 16])
        s_bim = P.din("ssm_b_im", [2, 128, 64, 16])
        s_cre = P.din("ssm_c_re", [2, 128, 16, 64])
        s_cim = P.din("ssm_c_im", [2, 128, 16, 64])
        s_d = P.din("ssm_d", [D])
        s_glu = P.din("ssm_glu_w", [D, D])
        s_wout = P.din("ssm_w_out", [D, D])
        if SAMPLE:
            rope_d = P.din("rope", [2, 4096, 64])
            c_ak = P.din("cache_ak", [256, 256])
            c_av = P.din("cache_av", [256, 256])
            c_bk = P.din("cache_bk", [256, 1024])
            c_bv = P.din("cache_bv", [256, 1024])
            h0_d = P.din("h0", [2, 128, 64, 2])
            sel_d = P.din("sel", [1, 32])
        o_yp = P.dout("o_yp", [1024, D])
        o_ys = P.dout("o_ys", [1024, D])
        o_ak = P.dout("o_ak", [NPT, 256])
        o_av = P.dout("o_av", [NPT, 256])
        o_bk = P.dout("o_bk", [NPT, 1024])
        o_bv = P.dout("o_bv", [NPT, 1024])
        o_st = P.dout("o_st", [4, 2, 64, 256])
        XTs = P.dscr("XTs", [NT_ALL, 128, KC, TT], F32)
        UTs = P.dscr("UTs", [2, KC, 128, NTOK], BF16)
        UTF = P.dscr("UTF", [KC, 128, NMAIN], F32)
        Ys = P.dscr("Ys", [KC, 128, NMAIN], F32)
        if SAMPLE:
            QTs = P.dscr("QTs", [16, 128, 4096], BF16)
            KTs = P.dscr("KTs", [10, 128, NKEY], BF16)
            Vs = P.dscr("Vs", [NKEY, 1280], BF16)
            OTs = P.dscr("OTs", [8, 128, KC, TT], BF16)

        DBG = os.environ.get("KDBG", "0") == "1"
        P.dbg_names = []

        def dbg(name, ap, shape, dt, reads):
            if not DBG:
                return
            d = nc.dram_tensor("dbg_" + name, list(shape), dt, kind="ExternalOutput").ap()
            P.dbg_names.append("dbg_" + name)
            S.dma("sp", ncd(d, ap), reads=reads, writes=[("dbg", name)])
        P.dbg = dbg

        def ncd(out, in_, q="sp"):
            def fn(e):
                with nc.allow_non_contiguous_dma(reason="small/strided layout DMA"):
                    return e.dma_start(out=out, in_=in_)
            return fn

        with ExitStack() as ctx:
            P.ctx = ctx
            ps = [ctx.enter_context(nc.psum_tensor("ps%d" % i, [128, 512], F32)) for i in range(8)]
            psb = [p[:].bitcast(BF16) for p in ps]
            ident = P.sb("ident_sb", [128, 128])
            identb = P.sb("identb", [128, 128], BF16)
            jmat = P.sb("jmat_sb", [128, 128])
            jmatb = P.sb("jmatb", [128, 128], BF16)
            ones_b = P.sb("ones_b", [128, 128], BF16)
            eps_c = P.sb("eps_c", [128, 1])
            condT = P.sb("condT", [128, KC, 2])
            sc = P.sb("sc", [128, KC, 2], BF16)
            adab = P.sb("adab", [128, 2, 96])
            gam = P.sb("gam", [128, 2, 4, KC])
            mod = P.sb("mod", [128, 2, 96, 2])
            DER = P.sb("DER", [128, 2, 2, 4, KC])
            gq = P.sb("gq", [128, 2, 128])
            dl = P.sb("dl", [128, 4, 128])
            dlp = P.sb("dlp", [128, 2, 128])
            lam = P.sb("lam", [128, 4])
            sg = P.sb("sg", [128, 2])
            rstd = P.sb("rstd", [128, TT])
            rc2 = P.sb("rc2", [128, 2, TT])
            hss = P.sb("hss", [128, 8])
            hrs = P.sb("hrs", [128, 8])
            Dcol = P.sb("Dcol", [128, KC])
            MEM = P.sb("MEM", [128, 188 * 256])

            def V(off_kb, shape, dt):
                n = 1
                for s_ in shape[1:]:
                    n *= s_
                nb = n * (4 if dt == F32 else 2)
                a = int(off_kb * 256)
                ap = MEM[:, a:a + nb // 4]
                if dt != F32:
                    ap = ap.bitcast(dt)
                if len(shape) > 2:
                    names = ["d%d" % i for i in range(len(shape) - 1)]
                    pat = "p (" + " ".join(names) + ") -> p " + " ".join(names)
                    kw = {names[i]: shape[i + 1] for i in range(1, len(names))}
                    ap = ap.rearrange(pat, **kw)
                return ap

            X = V(0, [128, KC, TT], F32)
            BF = V(32, [128, KC, TT], F32)
            xtm = V(32, [128, 4, D], F32)
            NTb = V(64, [128, KC, TT], BF16)
            QTt = V(80, [128, 16, TT], BF16)
            KTt = V(96, [128, 10, TT], BF16)
            Vtt = V(106, [128, 4, 1280], BF16)
            OTt = V(116, [128, KC, TT], BF16)
            hT = V(80, [128, 64, TT], BF16)
            gtmp = V(80, [128, KC, TT], F32)
            z2 = V(112, [128, KC, TT], BF16)
            slabs = [V(144 + 16 * i, [128, KC, 512], BF16) for i in range(2)]
            blk32 = [V(176 + 2 * i, [128, 512], F32) for i in range(2)]
            blk16 = [V(180 + i, [128, 512], BF16) for i in range(2)]
            EB = [V(182 + i, [128, 512], BF16) for i in range(2)]
            rtmp = [V(132 + i, [128, 4, 64], F32) for i in range(4)]
            cs_t = V(136, [128, 4, 2, 64], F32)
            hsq = V(187, [128, 512], BF16)

            def bfk(k):
                return ("BF", k)
            BFALL = [bfk(k) for k in range(KC)]

            def load_w(w_ap, r0, c0, ncols=512, kc=KC):
                i = P.nslab % 2
                P.nslab += 1
                sl = slabs[i]
                src = w_ap[r0:r0 + kc * 128, c0:c0 + ncols].rearrange("(k p) n -> p k n", p=128)
                S.dma("pool", lambda e: e.dma_start(out=sl[:, 0:kc, 0:ncols], in_=src), writes=[("slab", i)])
                return sl, ("slab", i)

            S.dma("sp", lambda e: e.dma_start(out=ident[:], in_=ident_d[:, :]), writes=["ident"])
            S.dma("sp", lambda e: e.dma_start(out=jmat[:], in_=jmat_d[:, :]), writes=["jmat"])
            S.dve(lambda e: e.tensor_copy(identb[:], ident[:]), reads=["ident"], writes=["identb"])
            S.dve(lambda e: e.tensor_copy(jmatb[:], jmat[:]), reads=["jmat"], writes=["jmatb"])
            S.pool(lambda e: e.memset(ones_b[:], 1.0), writes=["ones_b"])
            S.pool(lambda e: e.memset(eps_c[:], EPS), writes=["eps_c"])
            for r in range(2):
                S.dma("sp", ncd(condT[:, :, r], cond[r].rearrange("(c p) -> p c", p=128)), writes=["condT"])
            for l in range(2):
                S.dma("sp", ncd(adab[:, l, :], ada_b[l].rearrange("(j p) -> p j", p=128)), writes=["adab"])
                for f_ in range(4):
                    S.dma("sp", ncd(gam[:, l, f_, :], norm_g[l, f_].rearrange("(c p) -> p c", p=128)), writes=["gam"])
            for r in range(2):
                S.dma("sp", ncd(gq[:, r, :], qkn[r:r + 1, :].partition_broadcast(128)[:, 0, :]), writes=["gq"])
            for r in range(4):
                S.dma("sp", ncd(dl[:, r, :], dlam[r:r + 1, :].partition_broadcast(128)[:, 0, :]), writes=["dl"])
            S.dma("sp", ncd(sg[:], dsub.rearrange("(h p) -> p h", p=128)), writes=["sg"])
            S.dma("sp", ncd(Dcol[:], s_d.rearrange("(c p) -> p c", p=128)), writes=["Dcol"])
            S.act(lambda e: e.activation(sc[:], condT[:], AF.Silu), reads=["condT"], writes=["sc"])
            S.dve(lambda e: e.tensor_tensor(dlp[:, 0, :], dl[:, 0, :], dl[:, 1, :], op=ALU.mult), reads=["dl"], writes=["dlp"])
            S.dve(lambda e: e.tensor_tensor(dlp[:, 1, :], dl[:, 2, :], dl[:, 3, :], op=ALU.mult), reads=["dl"], writes=["dlp"])
            S.dve(lambda e: e.tensor_reduce(out=lam[:, 0:2], in_=dlp[:], op=ALU.add, axis=AX.X), reads=["dlp"], writes=["lam"])
            S.act(lambda e: e.activation(lam[:, 0:2], lam[:, 0:2], AF.Exp), reads=["lam"], writes=["lam"])
            S.dve(lambda e: e.tensor_tensor(lam[:, 2:3], lam[:, 0:1], lam[:, 1:2], op=ALU.subtract), reads=["lam"], writes=["lam"])
            S.dve(lambda e: e.tensor_scalar(lam[:, 3:4], lam[:, 2:3], LAM_INIT0, -1.0, op0=ALU.add, op1=ALU.mult), reads=["lam"], writes=["lam"])
            S.dve(lambda e: e.tensor_scalar(sg[:], sg[:], 1.0 - LAM_INIT0, None, op0=ALU.mult), reads=["sg"], writes=["sg"])

            for l in range(2):
                for sl_i in range(24):
                    sl, sk = load_w(ada_w[l], 0, sl_i * 512)
                    for j in range(4):
                        b = P.bank()
                        for k in range(KC):
                            S.pe(lambda e, b=b, k=k, j=j, sl=sl: e.matmul(ps[b][:, 0:2], sl[:, k, j * 128:(j + 1) * 128], sc[:, k, :],
                                                                       start=(k == 0), stop=(k == KC - 1)),
                                 reads=[sk, "sc"], writes=[("ps", b)])
                        ch = sl_i * 4 + j
                        S.dve(lambda e, b=b, ch=ch, l=l: e.tensor_scalar(mod[:, l, ch, :], ps[b][:, 0:2], adab[:, l, ch:ch + 1], None, op0=ALU.add),
                              reads=[("ps", b), "adab"], writes=["mod"])
                for r in range(2):
                    S.dve(lambda e, l=l, r=r: e.scalar_tensor_tensor(DER[:, l, r, 0, :], mod[:, l, 16:32, r], 1.0, gam[:, l, 0, :], op0=ALU.add, op1=ALU.mult),
                          reads=["mod", "gam"], writes=["DER"])
                    S.dve(lambda e, l=l, r=r: e.tensor_tensor(DER[:, l, r, 1, :], mod[:, l, 32:48, r], gam[:, l, 1, :], op=ALU.mult),
                          reads=["mod", "gam"], writes=["DER"])
                    S.dve(lambda e, l=l, r=r: e.scalar_tensor_tensor(DER[:, l, r, 2, :], mod[:, l, 64:80, r], 1.0, gam[:, l, 2, :], op0=ALU.add, op1=ALU.mult),
                          reads=["mod", "gam"], writes=["DER"])
                    S.dve(lambda e, l=l, r=r: e.tensor_tensor(DER[:, l, r, 3, :], mod[:, l, 80:96, r], gam[:, l, 3, :], op=ALU.mult),
                          reads=["mod", "gam"], writes=["DER"])

            def stats_rstd(src, skeys, kc=KC, dim=D):
                for k in range(kc):
                    S.act(lambda e, k=k: e.activation(NTb[:, k, :], src[:, k, :], AF.Square), reads=[skeys[k]], writes=[("NT", k)])
                b = P.bank()
                for k in range(kc):
                    S.pe(lambda e, k=k, b=b: e.matmul(ps[b][:, :], ones_b[:], NTb[:, k, :], start=(k == 0), stop=(k == kc - 1)),
                         reads=[("NT", k), "ones_b"], writes=[("ps", b)])
                S.act(lambda e, b=b: e.activation(rstd[:], ps[b][:, :], AF.Sqrt, bias=eps_c[:, 0:1], scale=1.0 / dim),
                      reads=[("ps", b), "eps_c"], writes=["rstd"])
                S.dve(lambda e: e.reciprocal(rstd[:], rstd[:]), reads=["rstd"], writes=["rstd"])

            def modulate(l, r, kind, shift_lo):
                for k in range(KC):
                    S.dve(lambda e, k=k: e.tensor_tensor(BF[:, k, :], X[:, k, :], rstd[:], op=ALU.mult),
                          reads=[("X", k), "rstd"], writes=[bfk(k)])
                    S.act(lambda e, k=k: e.activation(NTb[:, k, :], BF[:, k, :], AF.Identity,
                                                      bias=mod[:, l, shift_lo + k, r:r + 1], scale=DER[:, l, r, kind, k:k + 1]),
                          reads=[bfk(k), "DER", "mod"], writes=[("NT", k)])

            def residual(l, r, kind):
                for k in range(KC):
                    S.pool(lambda e, k=k: e.tensor_tensor(BF[:, k, :], BF[:, k, :], rstd[:], op=ALU.mult),
                           reads=[bfk(k), "rstd"], writes=[bfk(k)])
                    S.dve(lambda e, k=k: e.scalar_tensor_tensor(X[:, k, :], BF[:, k, :], DER[:, l, r, kind, k:k + 1], X[:, k, :],
                                                                op0=ALU.mult, op1=ALU.add),
                          reads=[bfk(k), "DER", ("X", k)], writes=[("X", k)])

            def linear_fm(in_get, in_keys, kcin, w_ap, n_out, evac):
                for cg in range(n_out // 512):
                    banks = [P.bank() for _ in range(4)]
                    nkg = kcin // 16
                    for kg in range(nkg):
                        sl, sk = load_w(w_ap, kg * 2048, cg * 512)
                        for j in range(4):
                            for k in range(16):
                                kk = kg * 16 + k
                                S.pe(lambda e, b=banks[j], k=k, j=j, kk=kk, sl=sl, kg=kg: e.matmul(
                                    ps[b][:, :], sl[:, k, j * 128:(j + 1) * 128], in_get(kk),
                                    start=(kg == 0 and k == 0), stop=(kg == nkg - 1 and k == 15)),
                                    reads=[sk, in_keys[kk]], writes=[("ps", banks[j])])
                    for j in range(4):
                        evac(cg * 4 + j, banks[j])

            def mlp(l, r):
                stats_rstd(X, [("X", k) for k in range(KC)])
                modulate(l, r, 2, 48)

                def ev1(j, b):
                    i = P.rr("relu", 2)
                    S.act(lambda e: e.activation(blk32[i][:, :], ps[b][:, :], AF.Relu), reads=[("ps", b)], writes=[("blk32", i)])
                    S.pool(lambda e: e.tensor_tensor(hT[:, j, :], blk32[i][:, :], blk32[i][:, :], op=ALU.mult),
                           reads=[("blk32", i)], writes=[("hT", j)])
                linear_fm(lambda kk: NTb[:, kk, :], [("NT", k) for k in range(KC)], KC, mlp_w1[l], 4 * D, ev1)

                def ev2(j, b):
                    P.copy(BF[:, j, :], ps[b][:, :], reads=[("ps", b)], writes=[bfk(j)])
                linear_fm(lambda kk: hT[:, kk, :], [("hT", k) for k in range(64)], 64, mlp_w2[l], D, ev2)
                stats_rstd(BF, BFALL)
                residual(l, r, 3)
            def tile_r(t):
                return 0 if t < 2 else 1

            def passA(t):
                r = tile_r(t)
                samp = t >= 2
                for sub in range(4):
                    r0 = t * TT + sub * 128
                    S.dma("sp", lambda e, sub=sub, r0=r0: e.dma_start(out=xtm[:, sub, :], in_=xall[r0:r0 + 128, :]),
                          writes=[bfk(4 * sub + i) for i in range(4)])
                    if samp:
                        tr0 = (t - 2) * TT + sub * 128
                        for cs_i in range(2):
                            S.dma("sp", lambda e, sub=sub, tr0=tr0, cs_i=cs_i: e.dma_start(out=cs_t[:, sub, cs_i, :], in_=rope_d[cs_i, tr0:tr0 + 128, :]),
                                  writes=[("cs", sub)])
                for k in range(KC):
                    b = P.bank()
                    for sub in range(4):
                        S.pe(lambda e, b=b, k=k, sub=sub: e.transpose(ps[b][:, sub * 128:(sub + 1) * 128], xtm[:, sub, k * 128:(k + 1) * 128], ident[:]),
                             reads=[bfk(4 * sub + k // 4), "ident"], writes=[("ps", b)])
                    P.copy(X[:, k, :], ps[b][:, :], reads=[("ps", b)], writes=[("X", k)])
                stats_rstd(X, [("X", k) for k in range(KC)])
                modulate(0, r, 0, 0)
                if samp:
                    S.dma("sp", lambda e: e.dma_start(out=XTs[t], in_=X), reads=[("X", k) for k in range(KC)], writes=[("XTs", t)])

                def rope(src32, nmap, sub, dst16):
                    s3 = src32.rearrange("p (h d) -> p h d", d=128)
                    d3 = dst16.rearrange("p (h d) -> p h d", d=128)
                    cosb = cs_t[:, sub, 0, :].unsqueeze(1).to_broadcast([128, nmap, 64])
                    sinb = cs_t[:, sub, 1, :].unsqueeze(1).to_broadcast([128, nmap, 64])
                    x1 = s3[:, :, 0:64]
                    x2 = s3[:, :, 64:128]
                    T = [rt[:, 0:nmap, :] for rt in rtmp]
                    rk = [("rtmp", i) for i in range(4)]
                    S.dve(lambda e: e.tensor_tensor(T[0], x1, cosb, op=ALU.mult), reads=[skey_cur[0], ("cs", sub)], writes=[rk[0]])
                    S.pool(lambda e: e.tensor_tensor(T[1], x2, sinb, op=ALU.mult), reads=[skey_cur[0], ("cs", sub)], writes=[rk[1]])
                    S.pool(lambda e: e.tensor_tensor(T[2], x1, sinb, op=ALU.mult), reads=[skey_cur[0], ("cs", sub)], writes=[rk[2]])
                    S.dve(lambda e: e.tensor_tensor(T[3], x2, cosb, op=ALU.mult), reads=[skey_cur[0], ("cs", sub)], writes=[rk[3]])
                    S.dve(lambda e: e.tensor_tensor(d3[:, :, 0:64], T[0], T[1], op=ALU.subtract), reads=[rk[0], rk[1]], writes=[dkey_cur[0]])
                    S.pool(lambda e: e.tensor_tensor(d3[:, :, 64:128], T[2], T[3], op=ALU.add), reads=[rk[2], rk[3]], writes=[dkey_cur[0]])

                skey_cur = [None]
                dkey_cur = [None]

                def to_T(src16, skey, nmap, dst, dkeys):
                    b = P.bank()
                    for i in range(nmap):
                        S.pe(lambda e, b=b, i=i: e.transpose(psb[b][:, i * 128:(i + 1) * 128], src16[:, i * 128:(i + 1) * 128], identb[:]),
                             reads=[skey, "identb"], writes=[("ps", b)])
                    P.copy(dst, psb[b][:, 0:nmap * 128].rearrange("p (i t) -> p i t", t=128), reads=[("ps", b)], writes=dkeys)

                for s_i in range(9):
                    sl, sk = load_w(w_in, 0, s_i * 512)
                    for sub in range(4):
                        b = P.bank()
                        for k in range(KC):
                            S.pe(lambda e, b=b, k=k, sub=sub, sl=sl: e.matmul(ps[b][:, :], NTb[:, k, sub * 128:(sub + 1) * 128], sl[:, k, :],
                                                                         start=(k == 0), stop=(k == KC - 1)),
                                 reads=[("NT", k), sk], writes=[("ps", b)])
                        r0 = t * TT + sub * 128
                        i32 = P.rr("blk32", 2)
                        i16 = P.rr("blk16", 2)
                        b32 = blk32[i32]
                        b16 = blk16[i16]
                        k32 = ("blk32", i32)
                        k16 = ("blk16", i16)
                        skey_cur[0] = k32
                        dkey_cur[0] = k16
                        tsl = slice(sub * 128, (sub + 1) * 128)
                        if s_i in (0, 1, 2):
                            nh = 4 if s_i < 2 else 2
                            gi = 0 if s_i < 2 else 1
                            S.act(lambda e, b=b, nh=nh: e.activation(hsq[:, 0:nh * 128], ps[b][:, 0:nh * 128], AF.Square), reads=[("ps", b)], writes=["hsq"])
                            S.dve(lambda e, nh=nh: e.tensor_reduce(out=hss[:, 0:nh], in_=hsq[:, 0:nh * 128].rearrange("p (h d) -> p h d", h=nh), op=ALU.add, axis=AX.X),
                                  reads=["hsq"], writes=["hss"])
                            S.act(lambda e, nh=nh: e.activation(hrs[:, 0:nh], hss[:, 0:nh], AF.Sqrt, bias=eps_c[:, 0:1], scale=1.0 / 128), reads=["hss", "eps_c"], writes=["hrs"])
                            S.dve(lambda e, nh=nh: e.reciprocal(hrs[:, 0:nh], hrs[:, 0:nh]), reads=["hrs"], writes=["hrs"])
                            for h in range(nh):
                                S.dve(lambda e, b=b, h=h, b32=b32, gi=gi: e.scalar_tensor_tensor(b32[:, h * 128:(h + 1) * 128], ps[b][:, h * 128:(h + 1) * 128],
                                                                                               hrs[:, h:h + 1], gq[:, gi, :], op0=ALU.mult, op1=ALU.mult),
                                      reads=[("ps", b), "hrs", "gq"], writes=[k32])
                            if s_i == 2:
                                S.dve(lambda e, b=b, b32=b32: e.tensor_copy(b32[:, 256:512], ps[b][:, 256:512]), reads=[("ps", b), "hrs"], writes=[k32])
                            if samp:
                                rope(b32[:, 0:nh * 128], nh, sub, b16[:, 0:nh * 128])
                            else:
                                P.copy(b16[:, 0:nh * 128], b32[:, 0:nh * 128], reads=[k32], writes=[k16], eng="pool")
                            if s_i < 2:
                                to_T(b16, k16, 4, QTt[:, 4 * s_i:4 * s_i + 4, tsl], [("QT", 4 * s_i + i) for i in range(4)])
                            else:
                                to_T(b16, k16, 2, KTt[:, 0:2, tsl], [("KT", 0), ("KT", 1)])
                                P.copy(Vtt[:, sub, 0:256], b32[:, 256:512], reads=[k32], writes=[("Vt", sub)], eng="pool")
                                if not samp:
                                    S.dma("sp", lambda e, b32=b32, r0=r0: e.dma_start(out=o_ak[r0:r0 + 128, :], in_=b32[:, 0:256]), reads=[k32], writes=[("o_ak", r0)])
                                    S.dma("sp", lambda e, b32=b32, r0=r0: e.dma_start(out=o_av[r0:r0 + 128, :], in_=b32[:, 256:512]), reads=[k32], writes=[("o_av", r0)])
                        else:
                            P.copy(b32[:, :], ps[b][:, :], reads=[("ps", b)], writes=[k32])
                            if s_i in (3, 4, 5, 6):
                                if samp:
                                    rope(b32[:, :], 4, sub, b16[:, :])
                                else:
                                    P.copy(b16[:, :], b32[:, :], reads=[k32], writes=[k16], eng="pool")
                                if s_i in (3, 4):
                                    hm0 = 8 + 4 * (s_i - 3)
                                    to_T(b16, k16, 4, QTt[:, hm0:hm0 + 4, tsl], [("QT", hm0 + i) for i in range(4)])
                                else:
                                    hm0 = 2 + 4 * (s_i - 5)
                                    to_T(b16, k16, 4, KTt[:, hm0:hm0 + 4, tsl], [("KT", hm0 + i) for i in range(4)])
                                    if not samp:
                                        c0 = (s_i - 5) * 512
                                        S.dma("sp", lambda e, b32=b32, r0=r0, c0=c0: e.dma_start(out=o_bk[r0:r0 + 128, c0:c0 + 512], in_=b32[:, :]),
                                              reads=[k32], writes=[("o_bk", s_i, r0)])
                            else:
                                c0 = (s_i - 7) * 512
                                P.copy(Vtt[:, sub, 256 + c0:256 + c0 + 512], b32[:, :], reads=[k32], writes=[("Vt", sub)], eng="pool")
                                if not samp:
                                    S.dma("sp", lambda e, b32=b32, r0=r0, c0=c0: e.dma_start(out=o_bv[r0:r0 + 128, c0:c0 + 512], in_=b32[:, :]),
                                          reads=[k32], writes=[("o_bv", s_i, r0)])
                if samp:
                    tk0 = (t - 2) * TT
                    S.dma("sp", ncd(QTs[:, :, tk0:tk0 + TT].rearrange("h p t -> p h t"), QTt), reads=[("QT", i) for i in range(16)], writes=[("QTs", t)])
                    S.dma("sp", ncd(KTs[:, :, tk0:tk0 + TT].rearrange("h p t -> p h t"), KTt), reads=[("KT", i) for i in range(10)], writes=[("KTs", t)])
                    S.dma("sp", ncd(Vs[tk0:tk0 + TT, :].rearrange("(s p) c -> p s c", p=128), Vtt), reads=[("Vt", i) for i in range(4)], writes=[("Vs", t)])

            ATT_S = [0, 1]

            def att_gqa(qT, qkeys, kT_get, kkeys_get, v_get, vkeys_get, nkt, nq, out_ap, okeys):
                ob, db = 2, 6
                for kt in range(nkt):
                    sbk = ATT_S[P.rr("attS", 2)]
                    ei = P.rr("EB", 2)
                    S.pe(lambda e, sbk=sbk, kt=kt: e.matmul(ps[sbk][:, 0:nq], kT_get(kt), qT, start=True, stop=True),
                         reads=list(qkeys) + list(kkeys_get(kt)), writes=[("ps", sbk)])
                    S.act(lambda e, sbk=sbk, ei=ei: e.activation(EB[ei][:, 0:nq], ps[sbk][:, 0:nq], AF.Exp, scale=QSCALE),
                          reads=[("ps", sbk)], writes=[("EB", ei)])
                    S.pe(lambda e, kt=kt, ei=ei: e.matmul(ps[ob][:, 0:nq], v_get(kt), EB[ei][:, 0:nq], start=(kt == 0), stop=(kt == nkt - 1)),
                         reads=[("EB", ei)] + list(vkeys_get(kt)), writes=[("ps", ob)])
                    S.pe(lambda e, kt=kt, ei=ei: e.matmul(ps[db][:, 0:nq], ones_b[:], EB[ei][:, 0:nq], start=(kt == 0), stop=(kt == nkt - 1)),
                         reads=[("EB", ei), "ones_b"], writes=[("ps", db)])
                S.dve(lambda e: e.reciprocal(rc2[:, 0, 0:nq], ps[db][:, 0:nq]), reads=[("ps", db)], writes=["rc2"])
                S.dve(lambda e: e.tensor_tensor(out_ap, ps[ob][:, 0:nq], rc2[:, 0, 0:nq], op=ALU.mult), reads=[("ps", ob), "rc2"], writes=okeys)

            def att_diff(qT2, qkeys, kT_get, kkeys_get, v_get, vkeys_get, nkt, nq, out_get, okeys):
                for kt in range(nkt):
                    for m in range(2):
                        sbk = ATT_S[P.rr("attS", 2)]
                        ei = P.rr("EB", 2)
                        S.pe(lambda e, sbk=sbk, kt=kt, m=m: e.matmul(ps[sbk][:, 0:nq], kT_get(m, kt), qT2(m), start=True, stop=True),
                             reads=list(qkeys) + list(kkeys_get(kt)), writes=[("ps", sbk)])
                        S.act(lambda e, sbk=sbk, ei=ei: e.activation(EB[ei][:, 0:nq], ps[sbk][:, 0:nq], AF.Exp, scale=QSCALE),
                              reads=[("ps", sbk)], writes=[("EB", ei)])
                        for half in range(2):
                            ob = 2 + 2 * m + half
                            S.pe(lambda e, kt=kt, ei=ei, ob=ob, half=half: e.matmul(ps[ob][:, 0:nq], v_get(kt, half), EB[ei][:, 0:nq], start=(kt == 0), stop=(kt == nkt - 1)),
                                 reads=[("EB", ei)] + list(vkeys_get(kt)), writes=[("ps", ob)])
                        S.pe(lambda e, kt=kt, ei=ei, m=m: e.matmul(ps[6 + m][:, 0:nq], ones_b[:], EB[ei][:, 0:nq], start=(kt == 0), stop=(kt == nkt - 1)),
                             reads=[("EB", ei), "ones_b"], writes=[("ps", 6 + m)])
                S.dve(lambda e: e.reciprocal(rc2[:, 0, 0:nq], ps[6][:, 0:nq]), reads=[("ps", 6)], writes=["rc2"])
                S.dve(lambda e: e.reciprocal(rc2[:, 1, 0:nq], ps[7][:, 0:nq]), reads=[("ps", 7)], writes=["rc2"])
                S.dve(lambda e: e.tensor_scalar(rc2[:, 1, 0:nq], rc2[:, 1, 0:nq], lam[:, 3:4], None, op0=ALU.mult), reads=["rc2", "lam"], writes=["rc2"])
                dfo = [blk32[0], blk32[1]]
                for half in range(2):
                    S.dve(lambda e, half=half: e.tensor_tensor(dfo[half][:, 0:nq], ps[2 + half][:, 0:nq], rc2[:, 0, 0:nq], op=ALU.mult),
                          reads=[("ps", 2 + half), "rc2"], writes=[("blk32", half)])
                    S.dve(lambda e, half=half: e.tensor_tensor(EB[half][:, 0:nq], ps[4 + half][:, 0:nq], rc2[:, 1, 0:nq], op=ALU.mult),
                          reads=[("ps", 4 + half), "rc2"], writes=[("EB", half)])
                    S.pool(lambda e, half=half: e.tensor_tensor(dfo[half][:, 0:nq], dfo[half][:, 0:nq], EB[half][:, 0:nq], op=ALU.add),
                           reads=[("blk32", half), ("EB", half)], writes=[("blk32", half)])
                for half in range(2):
                    S.act(lambda e, half=half: e.activation(blk16[half][:, 0:nq], dfo[half][:, 0:nq], AF.Square), reads=[("blk32", half)], writes=[("blk16", half)])
                sbk = ATT_S[P.rr("attS", 2)]
                for half in range(2):
                    S.pe(lambda e, half=half, sbk=sbk: e.matmul(ps[sbk][:, 0:nq], ones_b[:], blk16[half][:, 0:nq], start=(half == 0), stop=(half == 1)),
                         reads=[("blk16", half), "ones_b"], writes=[("ps", sbk)])
                S.act(lambda e, sbk=sbk: e.activation(rc2[:, 0, 0:nq], ps[sbk][:, 0:nq], AF.Sqrt, bias=eps_c[:, 0:1], scale=1.0 / 256), reads=[("ps", sbk), "eps_c"], writes=["rc2"])
                S.dve(lambda e: e.reciprocal(rc2[:, 0, 0:nq], rc2[:, 0, 0:nq]), reads=["rc2"], writes=["rc2"])
                for half in range(2):
                    S.dve(lambda e, half=half: e.scalar_tensor_tensor(out_get(half), dfo[half][:, 0:nq], sg[:, half:half + 1], rc2[:, 0, 0:nq], op0=ALU.mult, op1=ALU.mult),
                          reads=[("blk32", half), "rc2", "sg"], writes=okeys)

            def attention_prompt_tile():
                for sq_i in range(2):
                    cs = slice(sq_i * 256, (sq_i + 1) * 256)
                    for h in range(8):
                        g = h // 4
                        att_gqa(QTt[:, h, cs], [("QT", h)],
                                lambda kt, g=g, sq_i=sq_i: KTt[:, g, sq_i * 256 + kt * 128: sq_i * 256 + (kt + 1) * 128], lambda kt, g=g: [("KT", g)],
                                lambda kt, g=g, sq_i=sq_i: Vtt[:, 2 * sq_i + kt, g * 128:(g + 1) * 128], lambda kt, sq_i=sq_i: [("Vt", 2 * sq_i + kt)],
                                2, 256, OTt[:, h, cs], [("OT", h)])
                    for h in range(4):
                        att_diff(lambda m, h=h, cs=cs: QTt[:, 8 + 2 * h + m, cs], [("QT", 8 + 2 * h), ("QT", 9 + 2 * h)],
                                 lambda m, kt, h=h, sq_i=sq_i: KTt[:, 2 + 2 * h + m, sq_i * 256 + kt * 128: sq_i * 256 + (kt + 1) * 128],
                                 lambda kt, h=h: [("KT", 2 + 2 * h), ("KT", 3 + 2 * h)],
                                 lambda kt, half, h=h, sq_i=sq_i: Vtt[:, 2 * sq_i + kt, 256 + h * 256 + half * 128: 256 + h * 256 + (half + 1) * 128],
                                 lambda kt, sq_i=sq_i: [("Vt", 2 * sq_i + kt)],
                                 2, 256, lambda half, h=h, cs=cs: OTt[:, 8 + 2 * h + half, cs], [("OT", 8 + 2 * h), ("OT", 9 + 2 * h)])

            def rev_pos(t, sub):
                if t < 2:
                    pr, pos = sub // 2, sub % 2
                    return t * TT + (2 * pr + (1 - pos)) * 128
                slot = (t - 2) // 2
                s8 = ((t - 2) % 2) * 4 + sub
                s8r = 7 - s8
                return 1024 + slot * 1024 + s8r * 128

            def main_index(t):
                return MAIN_TILES.index(t) if t in MAIN_TILES else None

            def passC(t):
                r = tile_r(t)

                def evo(j, b):
                    P.copy(BF[:, j, :], ps[b][:, :], reads=[("ps", b)], writes=[bfk(j)])
                linear_fm(lambda kk: OTt[:, kk, :], [("OT", k) for k in range(KC)], KC, w_out, D, evo)
                stats_rstd(BF, BFALL)
                residual(0, r, 1)
                P.mark("mlp")
                switch(K_ATT, K_HT)
                mlp(0, r)
                switch(K_HT, K_UST)
                P.mark("l1inproj")
                stats_rstd(X, [("X", k) for k in range(KC)])
                modulate(1, r, 0, 0)
                mi = main_index(t)
                if mi is not None:
                    S.dma("sp", lambda e: e.dma_start(out=XTs[t], in_=X), reads=[("X", k) for k in range(KC)], writes=[("XTs", t)])
                for cg in range(4):
                    sl, sk = load_w(s_w_in, 0, cg * 512)
                    for sub in range(4):
                        b = P.bank()
                        for k in range(KC):
                            S.pe(lambda e, b=b, k=k, sub=sub, sl=sl: e.matmul(ps[b][:, :], NTb[:, k, sub * 128:(sub + 1) * 128], sl[:, k, :],
                                                                         start=(k == 0), stop=(k == KC - 1)),
                                 reads=[("NT", k), sk], writes=[("ps", b)])
                        i32 = P.rr("blk32", 2)
                        i16 = P.rr("blk16", 2)
                        b32, k32 = blk32[i32], ("blk32", i32)
                        b16, k16 = blk16[i16], ("blk16", i16)
                        S.act(lambda e, b=b, b32=b32: e.activation(b32[:, :], ps[b][:, :], AF.Copy), reads=[("ps", b)], writes=[k32])
                        S.pool(lambda e, b32=b32, b16=b16: e.tensor_copy(b16[:, :], b32[:, :]), reads=[k32], writes=[k16])
                        gpos = t * TT + sub * 128
                        rpos = rev_pos(t, sub)
                        bt = P.bank()
                        for i in range(4):
                            S.pe(lambda e, bt=bt, i=i, b32=b32: e.transpose(ps[bt][:, i * 128:(i + 1) * 128], b32[:, i * 128:(i + 1) * 128], ident[:]),
                                 reads=[k32, "ident"], writes=[("ps", bt)])
                        ui = P.rr("ust", 2)
                        S.act(lambda e, bt=bt, ui=ui: e.activation(ust32[ui][:, :, :], ps[bt][:, :].rearrange("p (i t) -> p i t", t=128), AF.Copy),
                              reads=[("ps", bt)], writes=[("ust32", ui)])
                        S.dve(lambda e, ui=ui: e.tensor_copy(ust16[ui][:, 0, :, :], ust32[ui][:, :, :]),
                              reads=[("ust32", ui)], writes=[("ust16", ui, 0)])
                        S.dma("sp", ncd(UTs[0, 4 * cg:4 * cg + 4, :, gpos:gpos + 128].rearrange("c p t -> p c t"), ust16[ui][:, 0, :, :]),
                              reads=[("ust16", ui, 0)], writes=[("UTs", 0, cg, gpos)])
                        if mi is not None:
                            mpos = mi * TT + sub * 128
                            S.dma("sp", ncd(UTF[4 * cg:4 * cg + 4, :, mpos:mpos + 128].rearrange("c p t -> p c t"), ust32[ui][:, :, :]),
                                  reads=[("ust32", ui)], writes=[("UTF", cg, mpos)])
                        br = P.bank()
                        for i in range(4):
                            S.pe(lambda e, br=br, i=i, b16=b16: e.matmul(ps[br][:, i * 128:(i + 1) * 128], b16[:, i * 128:(i + 1) * 128], jmatb[:], start=True, stop=True),
                                 reads=[k16, "jmatb"], writes=[("ps", br)])
                        S.act(lambda e, br=br, ui=ui: e.activation(ust16[ui][:, 1, :, :], ps[br][:, :].rearrange("p (i t) -> p i t", t=128), AF.Copy),
                              reads=[("ps", br)], writes=[("ust16", ui, 1)])
                        S.dma("sp", ncd(UTs[1, 4 * cg:4 * cg + 4, :, rpos:rpos + 128].rearrange("c p t -> p c t"), ust16[ui][:, 1, :, :]),
                              reads=[("ust16", ui, 1)], writes=[("UTs", 1, cg, rpos)])

            ust16 = [V(80 + 2 * i, [128, 2, 4, 128], BF16) for i in range(2)]
            ust32 = [V(84 + 2 * i, [128, 4, 128], F32) for i in range(2)]

            def passE(t):
                r = tile_r(t)
                mi = main_index(t)
                switch(K_HT, K_GT)
                S.dma("sp", lambda e: e.dma_start(out=X, in_=XTs[t]), reads=[("XTs", t)], writes=[("X", k) for k in range(KC)])
                S.dma("sp", ncd(BF, Ys[:, :, mi * TT:(mi + 1) * TT].rearrange("c p t -> p c t")), reads=["Ys"], writes=BFALL)
                for k in range(KC):
                    gk = ("gtmp", k)
                    S.pool(lambda e, k=k: e.tensor_tensor(gtmp[:, k, :], BF[:, k, :], BF[:, k, :], op=ALU.mult), reads=[bfk(k)], writes=[gk])
                    S.dve(lambda e, k=k: e.tensor_scalar(gtmp[:, k, :], gtmp[:, k, :], 0.044715, 1.0, op0=ALU.mult, op1=ALU.add), reads=[gk], writes=[gk])
                    S.pool(lambda e, k=k: e.tensor_tensor(gtmp[:, k, :], gtmp[:, k, :], BF[:, k, :], op=ALU.mult), reads=[gk, bfk(k)], writes=[gk])
                    S.act(lambda e, k=k: e.activation(gtmp[:, k, :], gtmp[:, k, :], AF.Sigmoid, scale=2.0 * math.sqrt(2.0 / math.pi)), reads=[gk], writes=[gk])
                    S.dve(lambda e, k=k: e.tensor_tensor(BF[:, k, :], BF[:, k, :], gtmp[:, k, :], op=ALU.mult), reads=[gk, bfk(k)], writes=[bfk(k)])
                    S.act(lambda e, k=k: e.activation(NTb[:, k, :], BF[:, k, :], AF.Copy), reads=[bfk(k)], writes=[("NT", k)])

                def evg(j, b):
                    i = P.rr("relu", 2)
                    S.act(lambda e: e.activation(blk32[i][:, :], ps[b][:, :], AF.Sigmoid), reads=[("ps", b)], writes=[("blk32", i)])
                    S.dve(lambda e: e.tensor_tensor(z2[:, j, :], BF[:, j, :], blk32[i][:, :], op=ALU.mult), reads=[("blk32", i), bfk(j)], writes=[("z2", j)])
                linear_fm(lambda kk: NTb[:, kk, :], [("NT", k) for k in range(KC)], KC, s_glu, D, evg)

                def evo(j, b):
                    P.copy(BF[:, j, :], ps[b][:, :], reads=[("ps", b)], writes=[bfk(j)])
                linear_fm(lambda kk: z2[:, kk, :], [("z2", k) for k in range(KC)], KC, s_wout, D, evo)
                stats_rstd(BF, BFALL)
                residual(1, r, 1)
                switch(K_GT, K_HT)
                mlp(1, r)
                dst = o_yp if t < 2 else o_ys
                base = t * TT if t < 2 else (t - 2) * TT
                for sub in range(4):
                    for kq in range(4):
                        b = P.bank()
                        for i in range(4):
                            k = kq * 4 + i
                            S.pe(lambda e, b=b, i=i, k=k, sub=sub: e.transpose(ps[b][:, i * 128:(i + 1) * 128], X[:, k, sub * 128:(sub + 1) * 128], ident[:]),
                                 reads=[("X", k), "ident"], writes=[("ps", b)])
                        oi = P.rr("blk32", 2)
                        P.copy(blk32[oi][:, :], ps[b][:, :], reads=[("ps", b)], writes=[("blk32", oi)])
                        r0 = base + sub * 128
                        S.dma("sp", lambda e, oi=oi, r0=r0, kq=kq, dst=dst: e.dma_start(out=dst[r0:r0 + 128, kq * 512:(kq + 1) * 512], in_=blk32[oi][:, :]),
                              reads=[("blk32", oi)], writes=[("oy", t, sub, kq)])

            def passD():
                PI = math.pi
                BnT = [V(4 * i, [128, 64, 16], F32) for i in range(4)]
                btmp = [V(16 + 4 * i, [128, 64, 16], F32) for i in range(2)]
                SO = V(24, [128, 4, 2, 64, 2], F32)
                H0 = V(28, [128, 2, 64, 2], F32)
                CnT = [V(32 + 4 * i, [128, 16, 64], F32) for i in range(4)]
                nslot = [0]

                def slot():
                    i = nslot[0]
                    nslot[0] += 1
                    assert i < 120
                    return V(48 + 0.25 * i, [128, 64], F32), ("prm", i)
                Ec = V(78, [128, 8, 256], F32)
                Es = V(86, [128, 8, 256], F32)
                U = V(94, [128, NTOK], BF16)
                HT = V(104, [128, 4, 2, NMAIN], BF16)
                Y0 = V(136, [128, 2, NMAIN], F32)
                UTFt = V(152, [128, NMAIN], F32)
                bri = [[V(160 + 4 * i + 2 * j, [128, 512], F32) for j in range(2)] for i in range(2)]
                tb = [V(168 + 2 * i, [128, 512], F32) for i in range(4)]
                tt1 = V(168, [128, 8, 128], F32)
                tt2 = V(172, [128, 8, 128], F32)
                wre = V(176, [128, 512], F32)
                wim = V(178, [128, 512], F32)
                qre = V(180, [128, 512], F32)
                qim = V(182, [128, 512], F32)
                bex = V(184, [128, 4, 128], F32)
                pads = V(186, [128, 4, 128], BF16)
                ytm = V(187, [128, 2, 128], F32)
                maskB = P.sb("maskB_sb", [128, 4, 8])
                maskC = P.sb("maskC_sb", [128, 4, 2])
                cst = P.sb("cst", [128, 4])
                selt = P.sb("selt", [128, 32])
                SL = P.sb("SL", [128, 4, 2])
                ini = P.sb("ini", [128, 4, 2])
                acc = P.sb("acc", [128, 4])
                S.dma("sp", ncd(maskB[:], maskB_d.rearrange("q p g -> p q g")), writes=["maskB"])
                S.dma("sp", ncd(maskC[:], maskC_d.rearrange("q p g -> p q g")), writes=["maskC"])
                S.pool(lambda e: e.memset(cst[:, 0:1], PI / 2), writes=["cst"])
                if SAMPLE:
                    S.dma("sp", ncd(selt[:], sel_d[0:1, :].partition_broadcast(128)[:, 0, :]), writes=["selt"])
                    for d_ in range(2):
                        S.dma("sp", ncd(H0[:, d_, :, :], h0_d[d_].rearrange("(st gl) n r -> (gl n) st r", gl=2)), writes=["H0"])
                prm = {}
                for d_ in range(2):
                    for ri, src in enumerate((s_bre, s_bim)):
                        S.dma("sp", ncd(BnT[d_ * 2 + ri], src[d_].rearrange("(st gl) n c -> (gl n) st c", gl=2)), writes=[("Bn", d_, ri)])
                    for ri, src in enumerate((s_cre, s_cim)):
                        S.dma("sp", ncd(CnT[d_ * 2 + ri], src[d_].rearrange("(ct g) c n -> (g c) ct n", g=8)), writes=[("Cn", d_, ri)])

                def ew(eng, fn, rk, wk):
                    getattr(S, eng)(fn, reads=rk, writes=wk)

                def tt(out, a, b, op, eng="dve"):
                    ew(eng, lambda e: e.tensor_tensor(out[0], a[0], b[0], op=op), [a[1], b[1]], [out[1]])

                def csq(cr, ci, nr, ni, tmp):
                    tt(tmp, ci, ci, ALU.mult)
                    tt(ni, cr, ci, ALU.mult)
                    tt(nr, cr, cr, ALU.mult)
                    tt(nr, nr, tmp, ALU.subtract)
                    ew("dve", lambda e: e.tensor_scalar(ni[0], ni[0], 2.0, None, op0=ALU.mult), [ni[1]], [ni[1]])

                for d_ in range(2):
                    are, aim, ldt = slot(), slot(), slot()
                    S.dma("sp", ncd(are[0], s_are[d_].rearrange("(st gl) n -> (gl n) st", gl=2)), writes=[are[1]])
                    S.dma("sp", ncd(aim[0], s_aim[d_].rearrange("(st gl) n -> (gl n) st", gl=2)), writes=[aim[1]])
                    for gl in range(2):
                        S.dma("sp", ncd(ldt[0][gl * 64:(gl + 1) * 64, :], s_ldt[d_].rearrange("(st gl) -> gl st", gl=2)[gl:gl + 1, :].partition_broadcast(64)[:, 0, :]),
                              writes=[ldt[1]])
                    dt_ = slot()
                    ew("act", lambda e, dt_=dt_, ldt=ldt: e.activation(dt_[0], ldt[0], AF.Exp), [ldt[1]], [dt_[1]])
                    rr_, th = slot(), slot()
                    tt(rr_, are, dt_, ALU.mult)
                    ew("act", lambda e, rr_=rr_: e.activation(rr_[0], rr_[0], AF.Exp), [rr_[1]], [rr_[1]])
                    tt(th, aim, dt_, ALU.mult)
                    kf, ki, tmp, xk = slot(), slot(), slot(), slot()
                    Pc, Ps = [], []
                    for k in range(9):
                        pc_, ps_ = slot(), slot()
                        Pc.append(pc_)
                        Ps.append(ps_)
                        sc2 = float(1 << k)
                        ew("dve", lambda e, xk=xk, th=th, sc2=sc2: e.tensor_scalar(xk[0], th[0], sc2, None, op0=ALU.mult), [th[1]], [xk[1]])
                        ew("dve", lambda e, kf=kf, xk=xk: e.tensor_scalar(kf[0], xk[0], 1.0 / (2 * PI), None, op0=ALU.mult), [xk[1]], [kf[1]])
                        ew("dve", lambda e, kf=kf, ki=ki: e.tensor_copy(ki[0].bitcast(I32), kf[0]), [kf[1]], [ki[1]])
                        ew("dve", lambda e, kf=kf, ki=ki: e.tensor_copy(kf[0], ki[0].bitcast(I32)), [ki[1]], [kf[1]])
                        ew("dve", lambda e, kf=kf, xk=xk: e.scalar_tensor_tensor(xk[0], kf[0], -2 * PI, xk[0], op0=ALU.mult, op1=ALU.add), [kf[1], xk[1]], [xk[1]])
                        ew("dve", lambda e, xk=xk: e.tensor_scalar(xk[0], xk[0], PI, -PI, op0=ALU.min, op1=ALU.max), [xk[1]], [xk[1]])
                        ew("act", lambda e, ps_=ps_, xk=xk: e.activation(ps_[0], xk[0], AF.Sin), [xk[1]], [ps_[1]])
                        ew("act", lambda e, tmp=tmp, xk=xk: e.activation(tmp[0], xk[0], AF.Abs), [xk[1]], [tmp[1]])
                        ew("act", lambda e, pc_=pc_, tmp=tmp: e.activation(pc_[0], tmp[0], AF.Sin, bias=cst[:, 0:1], scale=-1.0), [tmp[1], "cst"], [pc_[1]])
                    abr, abi = slot(), slot()
                    tt(abr, rr_, Pc[0], ALU.mult)
                    tt(abi, rr_, Ps[0], ALU.mult)
                    nr, den, t2, fr, fi = slot(), slot(), slot(), slot(), slot()
                    ew("dve", lambda e, nr=nr, abr=abr: e.tensor_scalar(nr[0], abr[0], -1.0, None, op0=ALU.add), [abr[1]], [nr[1]])
                    tt(den, are, are, ALU.mult)
                    tt(t2, aim, aim, ALU.mult)
                    tt(den, den, t2, ALU.add)
                    ew("dve", lambda e, den=den: e.reciprocal(den[0], den[0]), [den[1]], [den[1]])
                    tt(fr, nr, are, ALU.mult)
                    tt(t2, abi, aim, ALU.mult)
                    tt(fr, fr, t2, ALU.add)
                    tt(fr, fr, den, ALU.mult)
                    tt(fi, abi, are, ALU.mult)
                    tt(t2, nr, aim, ALU.mult)
                    tt(fi, fi, t2, ALU.subtract)
                    tt(fi, fi, den, ALU.mult)
                    bre_, bim_ = BnT[d_ * 2], BnT[d_ * 2 + 1]
                    kbr, kbi = ("Bn", d_, 0), ("Bn", d_, 1)
                    frb = fr[0].unsqueeze(2).to_broadcast([128, 64, 16])
                    fib = fi[0].unsqueeze(2).to_broadcast([128, 64, 16])
                    ew("pool", lambda e, bre_=bre_: e.tensor_copy(btmp[0], bre_), [kbr], ["btmp0"])
                    ew("dve", lambda e, bre_=bre_, frb=frb: e.tensor_tensor(bre_, bre_, frb, op=ALU.mult), [kbr, fr[1], "btmp0"], [kbr])
                    ew("dve", lambda e, bim_=bim_, fib=fib: e.tensor_tensor(btmp[1], bim_, fib, op=ALU.mult), [kbi, fi[1]], ["btmp1"])
                    ew("dve", lambda e, bre_=bre_: e.tensor_tensor(bre_, bre_, btmp[1], op=ALU.subtract), [kbr, "btmp1"], [kbr])
                    ew("dve", lambda e, bim_=bim_, frb=frb: e.tensor_tensor(bim_, bim_, frb, op=ALU.mult), [kbi, fr[1], "btmp1"], [kbi])
                    ew("dve", lambda e, fib=fib: e.tensor_tensor(btmp[1], btmp[0], fib, op=ALU.mult), ["btmp0", fi[1], kbi], ["btmp1"])
                    ew("dve", lambda e, bim_=bim_: e.tensor_tensor(bim_, bim_, btmp[1], op=ALU.add), [kbi, "btmp1"], [kbi])
                    prm[d_] = dict(r=rr_, Pc=Pc, Ps=Ps)
                    if d_ == 0:
                        for nm, sl_ in (("r", rr_), ("pc0", Pc[0]), ("ps0", Ps[0]), ("fr", fr), ("fi", fi), ("dt", dt_), ("th", th), ("are", are), ("aim", aim)):
                            P.dbg(nm, sl_[0], [128, 64], F32, [sl_[1]])
                        P.dbg("bbr", BnT[0], [128, 64, 16], F32, [("Bn", 0, 0)])
                    if SAMPLE:
                        cur = (abr, abi)
                        pp = [(slot(), slot()), (slot(), slot())]
                        for k in range(10):
                            nx = pp[k % 2]
                            csq(cur[0], cur[1], nx[0], nx[1], tmp)
                            cur = nx
                        A1 = cur
                        A2 = (slot(), slot())
                        csq(A1[0], A1[1], A2[0], A2[1], tmp)
                        A3 = (slot(), slot())
                        tt(A3[0], A2[0], A1[0], ALU.mult)
                        tt(tmp, A2[1], A1[1], ALU.mult)
                        tt(A3[0], A3[0], tmp, ALU.subtract)
                        tt(A3[1], A2[0], A1[1], ALU.mult)
                        tt(tmp, A2[1], A1[0], ALU.mult)
                        tt(A3[1], A3[1], tmp, ALU.add)
                        coef = []
                        for src in range(4):
                            cr_, ci_, nci_ = slot(), slot(), slot()
                            base = d_ * 16 + src * 4
                            sc_ = lambda e_: selt[:, base + e_: base + e_ + 1]
                            for (o_, parts) in ((cr_, (A1[0], A2[0], A3[0])), (ci_, (A1[1], A2[1], A3[1]))):
                                ew("dve", lambda e, o_=o_, p0=parts[0], s1=sc_(1): e.tensor_scalar(o_[0], p0[0], s1, None, op0=ALU.mult), [parts[0][1], "selt"], [o_[1]])
                                ew("dve", lambda e, o_=o_, p1=parts[1], s2_=sc_(2): e.scalar_tensor_tensor(o_[0], p1[0], s2_, o_[0], op0=ALU.mult, op1=ALU.add), [parts[1][1], "selt", o_[1]], [o_[1]])
                                ew("dve", lambda e, o_=o_, p2=parts[2], s3=sc_(3): e.scalar_tensor_tensor(o_[0], p2[0], s3, o_[0], op0=ALU.mult, op1=ALU.add), [parts[2][1], "selt", o_[1]], [o_[1]])
                            ew("dve", lambda e, cr_=cr_, s0=sc_(0): e.tensor_scalar(cr_[0], cr_[0], s0, None, op0=ALU.add), [cr_[1], "selt"], [cr_[1]])
                            ew("dve", lambda e, ci_=ci_, nci_=nci_: e.tensor_scalar(nci_[0], ci_[0], -1.0, None, op0=ALU.mult), [ci_[1]], [nci_[1]])
                            coef.append((cr_, ci_, nci_))
                        prm[d_]["coef"] = coef

                order = [0, 1] + ([4, 5, 6, 7, 8, 9, 2, 3] if SAMPLE else [])
                TK = [("t", i) for i in range(4)]

                for ctp in range(8):
                    for d_ in range(2):
                        pr = prm[d_]
                        st0 = ctp * 8
                        S.pool(lambda e: e.memset(Ec[:, :, 0:1], 1.0), writes=["Ec"])
                        S.pool(lambda e: e.memset(Es[:, :, 0:1], 0.0), writes=["Es"])
                        for k in range(8):
                            n = 1 << k
                            pcb = pr["Pc"][k][0][:, st0:st0 + 8].unsqueeze(2).to_broadcast([128, 8, n])
                            psb_ = pr["Ps"][k][0][:, st0:st0 + 8].unsqueeze(2).to_broadcast([128, 8, n])
                            pk = [pr["Pc"][k][1], pr["Ps"][k][1]]
                            S.pool(lambda e, n=n, pcb=pcb: e.tensor_tensor(tt1[:, :, 0:n], Ec[:, :, 0:n], pcb, op=ALU.mult), reads=["Ec"] + pk, writes=TK[0:2])
                            S.pool(lambda e, n=n, psb_=psb_: e.tensor_tensor(tt2[:, :, 0:n], Es[:, :, 0:n], psb_, op=ALU.mult), reads=["Es"] + pk, writes=TK[2:4])
                            S.dve(lambda e, n=n: e.tensor_tensor(Ec[:, :, n:2 * n], tt1[:, :, 0:n], tt2[:, :, 0:n], op=ALU.subtract), reads=TK, writes=["Ec"])
                            S.pool(lambda e, n=n, psb_=psb_: e.tensor_tensor(tt1[:, :, 0:n], Ec[:, :, 0:n], psb_, op=ALU.mult), reads=["Ec"] + pk, writes=TK[0:2])
                            S.pool(lambda e, n=n, pcb=pcb: e.tensor_tensor(tt2[:, :, 0:n], Es[:, :, 0:n], pcb, op=ALU.mult), reads=["Es"] + pk, writes=TK[2:4])
                            S.dve(lambda e, n=n: e.tensor_tensor(Es[:, :, n:2 * n], tt1[:, :, 0:n], tt2[:, :, 0:n], op=ALU.add), reads=TK, writes=["Es"])
                        if ctp == 0 and d_ == 0:
                            P.dbg("Ec", Ec, [128, 8, 256], F32, ["Ec"])
                            P.dbg("Es", Es, [128, 8, 256], F32, ["Es"])
                        for ci in range(2):
                            ct = 2 * ctp + ci
                            S.dma("sp", lambda e, d_=d_, ct=ct: e.dma_start(out=U, in_=UTs[d_, ct]), reads=[("UTs", d_)], writes=["U"])
                            if ct == 0:
                                P.dbg("U%d" % d_, U, [128, NTOK], BF16, ["U"])
                            for q in range(4):
                                st = 4 * ct + q
                                j8 = 4 * ci + q
                                for ri in range(2):
                                    bn = BnT[d_ * 2 + ri]
                                    S.dve(lambda e, bn=bn, ri=ri, st=st, q=q: e.tensor_tensor(
                                        bex[:, ri, :].rearrange("p (g c) -> p g c", c=16),
                                        bn[:, st, :].unsqueeze(1).to_broadcast([128, 8, 16]),
                                        maskB[:, q, :].unsqueeze(2).to_broadcast([128, 8, 16]), op=ALU.mult),
                                        reads=[("Bn", d_, ri), "maskB"], writes=[("bex", ri)])
                                    cn = CnT[d_ * 2 + ri]
                                    S.dve(lambda e, cn=cn, ri=ri, ct=ct, q=q: e.tensor_tensor(
                                        bex[:, 2 + ri, :].rearrange("p (g n) -> p g n", n=64),
                                        cn[:, ct, :].unsqueeze(1).to_broadcast([128, 2, 64]),
                                        maskC[:, q, :].unsqueeze(2).to_broadcast([128, 2, 64]), op=ALU.mult),
                                        reads=[("Cn", d_, ri), "maskC"], writes=[("bex", 2 + ri)])
                                bp = P.bank()
                                for i4 in range(4):
                                    S.pe(lambda e, bp=bp, i4=i4: e.transpose(ps[bp][:, i4 * 128:(i4 + 1) * 128], bex[:, i4, :], ident[:]),
                                         reads=[("bex", i4), "ident"], writes=[("ps", bp)])
                                S.act(lambda e, bp=bp: e.activation(pads[:, 0:3, :], ps[bp][:, 0:384].rearrange("p (a b) -> p a b", b=128), AF.Copy),
                                      reads=[("ps", bp)], writes=["pads"])
                                S.act(lambda e, bp=bp: e.activation(pads[:, 3, :], ps[bp][:, 384:512], AF.Copy, scale=-1.0),
                                      reads=[("ps", bp)], writes=["pads"])
                                if st == 0 and d_ == 0:
                                    P.dbg("pads", pads, [128, 4, 128], BF16, ["pads"])
                                rcol = pr["r"][0][:, st:st + 1]
                                rkey = pr["r"][1]
                                cos255 = Ec[:, j8, 255:256]
                                sin255 = Es[:, j8, 255:256]
                                cosT = Ec[:, j8, :].unsqueeze(1).to_broadcast([128, 2, 256])
                                sinT = Es[:, j8, :].unsqueeze(1).to_broadcast([128, 2, 256])
                                p8c = pr["Pc"][8][0][:, st:st + 1]
                                p8s = pr["Ps"][8][0][:, st:st + 1]
                                p8k = [pr["Pc"][8][1], pr["Ps"][8][1]]
                                v3 = lambda ap: ap.rearrange("p (s t) -> p s t", t=256)
                                prev_last = [None]

                                def cmul_to(o_r, o_i, a_r, a_i, b_r, b_i, rk, wk):
                                    S.dve(lambda e: e.tensor_scalar(acc[:, 0:1], a_i, b_i, None, op0=ALU.mult), reads=rk, writes=["acc0"])
                                    S.dve(lambda e: e.tensor_scalar(acc[:, 1:2], a_i, b_r, None, op0=ALU.mult), reads=rk, writes=["acc1"])
                                    S.dve(lambda e: e.scalar_tensor_tensor(o_r, a_r, b_r, acc[:, 0:1], op0=ALU.mult, op1=ALU.subtract), reads=rk + ["acc0"], writes=wk)
                                    S.dve(lambda e: e.scalar_tensor_tensor(o_i, a_r, b_i, acc[:, 1:2], op0=ALU.mult, op1=ALU.add), reads=rk + ["acc1"], writes=wk)

                                for t in order:
                                    col0 = t * TT
                                    mi = main_index(t)
                                    samp = t >= 2
                                    slot_i = (t - 2) // 2 if samp else None
                                    half_i = (t - 2) % 2 if samp else None
                                    bi = P.rr("bri", 2)
                                    b_re, b_im = P.bank(), P.bank()
                                    for ri, bb_ in enumerate((b_re, b_im)):
                                        S.pe(lambda e, ri=ri, bb_=bb_, col0=col0: e.matmul(ps[bb_][:, :], pads[:, ri, :], U[:, col0:col0 + TT], start=True, stop=True),
                                             reads=["pads", "U"], writes=[("ps", bb_)])
                                        S.act(lambda e, ri=ri, bb_=bb_, bi=bi: e.activation(bri[bi][ri], ps[bb_][:, :], AF.Copy), reads=[("ps", bb_)], writes=[("bri", bi, ri)])
                                    bre_, bim_ = bri[bi][0], bri[bi][1]
                                    kre, kim = ("bri", bi, 0), ("bri", bi, 1)
                                    EK = ["Ec", "Es"]
                                    cos2 = Ec[:, j8, :]
                                    sin2 = Es[:, j8, :]
                                    for sg_i in range(2):
                                        cs_ = slice(sg_i * 256, (sg_i + 1) * 256)
                                        S.pool(lambda e, bre_=bre_, cs_=cs_, cos2=cos2: e.tensor_tensor(tb[0][:, cs_], bre_[:, cs_], cos2, op=ALU.mult), reads=[kre] + EK, writes=[TK[0]])
                                        S.pool(lambda e, bim_=bim_, cs_=cs_, sin2=sin2: e.tensor_tensor(tb[1][:, cs_], bim_[:, cs_], sin2, op=ALU.mult), reads=[kim] + EK, writes=[TK[1]])
                                        S.pool(lambda e, bim_=bim_, cs_=cs_, cos2=cos2: e.tensor_tensor(tb[2][:, cs_], bim_[:, cs_], cos2, op=ALU.mult), reads=[kim] + EK, writes=[TK[2]])
                                        S.pool(lambda e, bre_=bre_, cs_=cs_, sin2=sin2: e.tensor_tensor(tb[3][:, cs_], bre_[:, cs_], sin2, op=ALU.mult), reads=[kre] + EK, writes=[TK[3]])
                                    S.dve(lambda e: e.tensor_tensor(wre, tb[0], tb[1], op=ALU.add), reads=TK[0:2], writes=["wre"])
                                    S.dve(lambda e: e.tensor_tensor(wim, tb[2], tb[3], op=ALU.subtract), reads=TK[2:4], writes=["wim"])
                                    for sg_i in range(2):
                                        cs_ = slice(sg_i * 256, (sg_i + 1) * 256)
                                        init_r, init_i = 0.0, 0.0
                                        ik = []
                                        if samp:
                                            gseg = half_i * 2 + sg_i
                                            ii = P.rr("ini", 2)
                                            if gseg > 0:
                                                pl = prev_last[0]
                                                cmul_to(ini[:, ii, 0:1], ini[:, ii, 1:2], pl[0], pl[1], p8c, p8s, [pl[2]] + p8k, [("ini", ii)])
                                                init_r, init_i = ini[:, ii, 0:1], ini[:, ii, 1:2]
                                                ik = [("ini", ii)]
                                            elif slot_i == 0:
                                                S.dve(lambda e: e.memset(acc[:, 2:4], 0.0), writes=["acc23"])
                                                for src in range(4):
                                                    cr_, ci_, nci_ = pr["coef"][src]
                                                    if src == 0:
                                                        vr, vi, vk = H0[:, d_, st, 0:1], H0[:, d_, st, 1:2], "H0"
                                                    else:
                                                        vr, vi, vk = SL[:, src, 0:1], SL[:, src, 1:2], ("SL", src)
                                                    ck = [cr_[1], ci_[1], nci_[1], vk, "acc23"]
                                                    S.dve(lambda e, vr=vr, cr_=cr_, st=st: e.scalar_tensor_tensor(acc[:, 2:3], vr, cr_[0][:, st:st + 1], acc[:, 2:3], op0=ALU.mult, op1=ALU.add), reads=ck, writes=["acc23"])
                                                    S.dve(lambda e, vi=vi, nci_=nci_, st=st: e.scalar_tensor_tensor(acc[:, 2:3], vi, nci_[0][:, st:st + 1], acc[:, 2:3], op0=ALU.mult, op1=ALU.add), reads=ck, writes=["acc23"])
                                                    S.dve(lambda e, vr=vr, ci_=ci_, st=st: e.scalar_tensor_tensor(acc[:, 3:4], vr, ci_[0][:, st:st + 1], acc[:, 3:4], op0=ALU.mult, op1=ALU.add), reads=ck, writes=["acc23"])
                                                    S.dve(lambda e, vi=vi, cr_=cr_, st=st: e.scalar_tensor_tensor(acc[:, 3:4], vi, cr_[0][:, st:st + 1], acc[:, 3:4], op0=ALU.mult, op1=ALU.add), reads=ck, writes=["acc23"])
                                                cmul_to(ini[:, ii, 0:1], ini[:, ii, 1:2], acc[:, 2:3], acc[:, 3:4],
                                                        pr["Pc"][0][0][:, st:st + 1], pr["Ps"][0][0][:, st:st + 1], ["acc23", pr["Pc"][0][1], pr["Ps"][0][1]], [("ini", ii)])
                                                init_r, init_i = ini[:, ii, 0:1], ini[:, ii, 1:2]
                                                ik = [("ini", ii)]
                                        S.dve(lambda e, cs_=cs_, init_r=init_r, rcol=rcol: e.tensor_tensor_scan(qre[:, cs_], rcol.to_broadcast([128, 256]), wre[:, cs_], init_r, ALU.mult, ALU.add),
                                              reads=["wre", rkey] + ik, writes=["qre"])
                                        S.dve(lambda e, cs_=cs_, init_i=init_i, rcol=rcol: e.tensor_tensor_scan(qim[:, cs_], rcol.to_broadcast([128, 256]), wim[:, cs_], init_i, ALU.mult, ALU.add),
                                              reads=["wim", rkey] + ik, writes=["qim"])
                                        if st == 0 and d_ == 0 and t == 0 and sg_i == 1:
                                            P.dbg("wre", wre, [128, 512], F32, ["wre"])
                                            P.dbg("qre", qre, [128, 512], F32, ["qre"])
                                            P.dbg("bre", bre_, [128, 512], F32, [kre])
                                            P.dbg("bim", bim_, [128, 512], F32, [kim])
                                            P.dbg("wim", wim, [128, 512], F32, ["wim"])
                                            P.dbg("qim", qim, [128, 512], F32, ["qim"])
                                            P.dbg("t0", tb[0], [128, 512], F32, [TK[0]])
                                            P.dbg("t1", tb[1], [128, 512], F32, [TK[1]])
                                        lastc = sg_i * 256 + 255
                                        if samp:
                                            li = P.rr("lastq", 2)
                                            S.dve(lambda e, li=li, lastc=lastc: e.tensor_copy(ini[:, 2 + li, 0:1], qre[:, lastc:lastc + 1]), reads=["qre"], writes=[("lastq", li)])
                                            S.dve(lambda e, li=li, lastc=lastc: e.tensor_copy(ini[:, 2 + li, 1:2], qim[:, lastc:lastc + 1]), reads=["qim"], writes=[("lastq", li)])
                                            prev_last[0] = (ini[:, 2 + li, 0:1], ini[:, 2 + li, 1:2], ("lastq", li))
                                            if slot_i != 0 and gseg == 3:
                                                cmul_to(SL[:, slot_i, 0:1], SL[:, slot_i, 1:2], ini[:, 2 + li, 0:1], ini[:, 2 + li, 1:2], cos255, sin255,
                                                        [("lastq", li), "Ec", "Es"], [("SL", slot_i)])
                                        else:
                                            seq = t * 2 + sg_i
                                            cmul_to(SO[:, seq, d_, st, 0:1], SO[:, seq, d_, st, 1:2], qre[:, lastc:lastc + 1], qim[:, lastc:lastc + 1], cos255, sin255,
                                                    ["qre", "qim", "Ec", "Es"], ["SO"])
                                    if mi is not None:
                                        mc = slice(mi * TT, (mi + 1) * TT)
                                        for sg_i in range(2):
                                            cs_ = slice(sg_i * 256, (sg_i + 1) * 256)
                                            S.pool(lambda e, cs_=cs_, cos2=cos2: e.tensor_tensor(tb[0][:, cs_], qre[:, cs_], cos2, op=ALU.mult), reads=["qre"] + EK, writes=[TK[0]])
                                            S.pool(lambda e, cs_=cs_, sin2=sin2: e.tensor_tensor(tb[1][:, cs_], qim[:, cs_], sin2, op=ALU.mult), reads=["qim"] + EK, writes=[TK[1]])
                                            S.pool(lambda e, cs_=cs_, sin2=sin2: e.tensor_tensor(tb[2][:, cs_], qre[:, cs_], sin2, op=ALU.mult), reads=["qre"] + EK, writes=[TK[2]])
                                            S.pool(lambda e, cs_=cs_, cos2=cos2: e.tensor_tensor(tb[3][:, cs_], qim[:, cs_], cos2, op=ALU.mult), reads=["qim"] + EK, writes=[TK[3]])
                                        S.dve(lambda e, mc=mc, q=q: e.tensor_tensor(HT[:, q, 0, mc], tb[0], tb[1], op=ALU.subtract), reads=TK[0:2], writes=[("HT", q)])
                                        S.dve(lambda e, mc=mc, q=q: e.tensor_tensor(HT[:, q, 1, mc], tb[2], tb[3], op=ALU.add), reads=TK[2:4], writes=[("HT", q)])
                                S.pool(lambda e, q=q: e.tensor_copy(cpads[:, q, :, :], pads[:, 2:4, :]), reads=["pads"], writes=[("cpads", q)])
                            HK = [("HT", q) for q in range(4)]
                            CK = [("cpads", q) for q in range(4)]
                            for mi, t in enumerate(MAIN_TILES):
                                if d_ == 0:
                                    b = P.bank()
                                    n_ = 0
                                    for q in range(4):
                                        for ri in range(2):
                                            S.pe(lambda e, b=b, q=q, ri=ri, mi=mi, n_=n_: e.matmul(ps[b][:, :], cpads[:, q, ri, :], HT[:, q, ri, mi * TT:(mi + 1) * TT],
                                                                                            start=(n_ == 0), stop=(n_ == 7)),
                                                 reads=HK + CK, writes=[("ps", b)])
                                            n_ += 1
                                    P.copy(Y0[:, ci, mi * TT:(mi + 1) * TT], ps[b][:, :], reads=[("ps", b)], writes=[("Y0", ci, mi)])
                                else:
                                    for sub in range(4):
                                        pos = mi * TT + sub * 128
                                        gp = rev_pos(t, sub)
                                        gt = gp // TT
                                        fpos = main_index(gt) * TT + gp % TT
                                        b = P.bank()
                                        n_ = 0
                                        for q in range(4):
                                            for ri in range(2):
                                                S.pe(lambda e, b=b, q=q, ri=ri, pos=pos, n_=n_: e.matmul(ps[b][:, 0:128], HT[:, q, ri, pos:pos + 128], cpads[:, q, ri, :],
                                                                                                  start=(n_ == 0), stop=(n_ == 7)),
                                                     reads=HK + CK, writes=[("ps", b)])
                                                n_ += 1
                                        yi = P.rr("ytm", 2)
                                        S.act(lambda e, b=b, yi=yi: e.activation(ytm[:, yi, :], ps[b][:, 0:128], AF.Copy), reads=[("ps", b)], writes=[("ytm", yi)])
                                        b2 = P.bank()
                                        S.pe(lambda e, b2=b2, yi=yi: e.matmul(ps[b2][:, 0:128], ytm[:, yi, :], jmat[:], start=True, stop=True),
                                             reads=[("ytm", yi), "jmat"], writes=[("ps", b2)])
                                        S.dve(lambda e, b2=b2, fpos=fpos, ci=ci: e.tensor_tensor(Y0[:, ci, fpos:fpos + 128], Y0[:, ci, fpos:fpos + 128], ps[b2][:, 0:128], op=ALU.add),
                                              reads=[("ps", b2), ("Y0", ci, fpos // TT)], writes=[("Y0", ci, fpos // TT)])
                            if d_ == 1:
                                S.dma("sp", lambda e, ct=ct: e.dma_start(out=UTFt, in_=UTF[ct]), reads=["UTF"], writes=["UTFt"])
                                yk = [("Y0", ci, m_) for m_ in range(len(MAIN_TILES))]
                                S.dve(lambda e, ci=ci, ct=ct: e.scalar_tensor_tensor(Y0[:, ci, :], UTFt, Dcol[:, ct:ct + 1], Y0[:, ci, :], op0=ALU.mult, op1=ALU.add),
                                      reads=["UTFt", "Dcol"] + yk, writes=yk)
                                S.dma("sp", lambda e, ci=ci, ct=ct: e.dma_start(out=Ys[ct], in_=Y0[:, ci, :]), reads=yk, writes=["Ys"])
                for seq in range(4):
                    for d_ in range(2):
                        b = P.bank()
                        for ri in range(2):
                            S.pe(lambda e, b=b, ri=ri, seq=seq, d_=d_: e.transpose(ps[b][0:64, ri * 128:(ri + 1) * 128], SO[:, seq, d_, :, ri], ident[:]),
                                 reads=["SO", "ident"], writes=[("ps", b)])
                        oi = 0
                        S.dve(lambda e, b=b, oi=oi: e.tensor_copy(sost[oi][0:64, :].rearrange("p (a r) -> p a r", r=2),
                                                                   ps[b][0:64, 0:256].rearrange("p (r a) -> p a r", r=2)), reads=[("ps", b)], writes=[("sost", oi)])
                        S.dma("sp", lambda e, oi=oi, seq=seq, d_=d_: e.dma_start(out=o_st[seq, d_], in_=sost[oi][0:64, :]), reads=[("sost", oi)], writes=[("o_st", seq, d_)])

            cpads = V(29, [128, 4, 2, 128], BF16)
            sost = [V(31, [128, 256], F32)]

            dummy = P.sb("dummy_bar", [128, 8])
            K_ATT = [("QT", i) for i in range(16)] + [("KT", i) for i in range(10)] + [("Vt", i) for i in range(4)] + [("OT", i) for i in range(16)] + [("rtmp", i) for i in range(4)] + [("cs", i) for i in range(4)]
            K_HT = [("hT", i) for i in range(64)]
            K_UST = [("ust16", i, j) for i in range(2) for j in range(2)] + [("ust32", i) for i in range(2)]
            K_GT = [("gtmp", i) for i in range(16)] + [("z2", i) for i in range(16)]
            K_X = [("X", k) for k in range(KC)]
            K_B = ["KTu", "Vu"] + [("Qb", i) for i in range(2)] + [("Ob", i) for i in range(2)]

            def switch(*groups):
                keys = []
                for g in groups:
                    keys += list(g)
                n = P.rr("dummy", 8)
                S.dve(lambda e, n=n: e.memset(dummy[:, n:n + 1], 0.0), writes=keys + [("dummy", n)])

            def passB():
                switch(K_X, BFALL, K_B)
                KTu = V(0, [128, 2, NKEY], BF16)
                Vu = V(17, [128, 34, 256], BF16)
                Qb = [V(34 + 4 * i, [128, 4, TT], BF16) for i in range(2)]
                Ob = [V(42 + 4 * i, [128, 4, TT], BF16) for i in range(2)]
                S.dma("pool", lambda e: e.dma_start(out=Vs[4096:4352, 0:256], in_=c_av[:, :]), writes=["Vs_c"])
                S.dma("pool", lambda e: e.dma_start(out=Vs[4096:4352, 256:1280], in_=c_bv[:, :]), writes=["Vs_c"])
                ck = Vu[:, 0:10, :].rearrange("p a b -> p (a b)").rearrange("p (k c) -> p k c", c=1280)
                S.dma("pool", lambda e: e.dma_start(out=ck[:, :, 0:256], in_=c_ak.rearrange("(k p) c -> p k c", p=128)), writes=["Vu"])
                S.dma("pool", lambda e: e.dma_start(out=ck[:, :, 256:1280], in_=c_bk.rearrange("(k p) c -> p k c", p=128)), writes=["Vu"])
                for kt in range(2):
                    for grp in range(3):
                        m0 = grp * 4
                        nm = 4 if grp < 2 else 2
                        b = P.bank()
                        for i in range(nm):
                            S.pe(lambda e, b=b, i=i, kt=kt, m0=m0: e.transpose(psb[b][:, i * 128:(i + 1) * 128], ck[:, kt, (m0 + i) * 128:(m0 + i + 1) * 128], identb[:]),
                                 reads=["Vu", "identb"], writes=[("ps", b)])
                        oi = P.rr("Ob", 2)
                        P.copy(Ob[oi][:, 0:nm, 0:128], psb[b][:, 0:nm * 128].rearrange("p (i t) -> p i t", t=128), reads=[("ps", b)], writes=[("Ob", oi)])
                        S.dma("sp", ncd(KTs[m0:m0 + nm, :, 4096 + kt * 128:4096 + (kt + 1) * 128].rearrange("h p t -> p h t"), Ob[oi][:, 0:nm, 0:128]),
                              reads=[("Ob", oi)], writes=["KTs_c"])
                NKT = NKEY // 128
                for unit in range(6):
                    gqa = unit < 2
                    if gqa:
                        g = unit
                        S.dma("sp", lambda e, g=g: e.dma_start(out=KTu[:, 0, :], in_=KTs[g]), reads=["KTs_c"], writes=["KTu"])
                        S.dma("sp", ncd(Vu[:, :, 0:128], Vs[:, g * 128:(g + 1) * 128].rearrange("(kt p) c -> p kt c", p=128)), reads=["Vs_c"], writes=["Vu"])
                        hm0, nm = 4 * g, 4
                    else:
                        h = unit - 2
                        for m in range(2):
                            S.dma("sp", lambda e, h=h, m=m: e.dma_start(out=KTu[:, m, :], in_=KTs[2 + 2 * h + m]), reads=["KTs_c"], writes=["KTu"])
                        S.dma("sp", ncd(Vu[:, :, :], Vs[:, 256 + 256 * h:512 + 256 * h].rearrange("(kt p) c -> p kt c", p=128)), reads=["Vs_c"], writes=["Vu"])
                        hm0, nm = 8 + 2 * h, 2
                    for qb in range(8):
                        qi = P.rr("Qb", 2)
                        oi = P.rr("Ob", 2)
                        S.dma("sp", ncd(Qb[qi][:, 0:nm, :], QTs[hm0:hm0 + nm, :, qb * TT:(qb + 1) * TT].rearrange("h p t -> p h t")), writes=[("Qb", qi)])
                        if gqa:
                            for h4 in range(4):
                                att_gqa(Qb[qi][:, h4, :], [("Qb", qi)],
                                        lambda kt: KTu[:, 0, kt * 128:(kt + 1) * 128], lambda kt: ["KTu"],
                                        lambda kt: Vu[:, kt, 0:128], lambda kt: ["Vu"],
                                        NKT, TT, Ob[oi][:, h4, :], [("Ob", oi)])
                        else:
                            att_diff(lambda m, qi=qi: Qb[qi][:, m, :], [("Qb", qi)],
                                     lambda m, kt: KTu[:, m, kt * 128:(kt + 1) * 128], lambda kt: ["KTu"],
                                     lambda kt, half: Vu[:, kt, half * 128:(half + 1) * 128], lambda kt: ["Vu"],
                                     NKT, TT, lambda half, oi=oi: Ob[oi][:, half, :], [("Ob", oi)])
                        S.dma("sp", lambda e, oi=oi, qb=qb, hm0=hm0, nm=nm: e.dma_start(out=OTs[qb][:, hm0:hm0 + nm, :], in_=Ob[oi][:, 0:nm, :]),
                              reads=[("Ob", oi)], writes=[("OTs", qb, unit)])
                switch(K_B, K_X, BFALL)

            def mark(name):
                if os.environ.get("KMARK"):
                    print("MARK", name, len(S.ops), flush=True)
            P.mark = mark
            mark("start_tiles")
            for t in (0, 1):
                switch(K_UST, K_HT, K_ATT)
                mark("passA%d" % t)
                passA(t)
                mark("att%d" % t)
                attention_prompt_tile()
                mark("passC%d" % t)
                passC(t)
            mark("end_prompt_l0")
            if SAMPLE:
                for t in range(2, NT_ALL):
                    switch(K_UST, K_HT, K_ATT)
                    passA(t)
                passB()
                for t in range(2, NT_ALL):
                    switch(K_UST, K_HT, K_ATT)
                    S.dma("sp", lambda e, t=t: e.dma_start(out=X, in_=XTs[t]), reads=[("XTs", t)], writes=K_X)
                    S.dma("sp", lambda e, t=t: e.dma_start(out=OTt, in_=OTs[t - 2]), reads=[("OTs", t - 2, u) for u in range(6)], writes=[("OT", k) for k in range(KC)])
                    passC(t)
            if STAGE >= 2:
                S.fence()
                passD()
            if STAGE >= 3:
                S.fence()
                for t in MAIN_TILES:
                    passE(t)
            S.emit(ctx)
        return nc


_CACHE = {}


def _get_prog():
    if "nc" not in _CACHE:
        _CACHE["nc"] = Prog().build()
    return _CACHE["nc"]


def _rope_tables():
    pos = np.arange(4096)
    row = (pos // 64).astype(np.float32)
    col = (pos % 64).astype(np.float32)
    inv = (np.float32(10000.0) ** (-np.arange(32, dtype=np.float32) / np.float32(32))).astype(np.float32)
    ang = np.concatenate([row[:, None] * inv[None, :], col[:, None] * inv[None, :]], axis=-1).astype(np.float32)
    return np.cos(ang).astype(np.float32), np.sin(ang).astype(np.float32)


def kernel(x_prompt, x_sample, c, cache_a_k, cache_a_v, cache_b_k, cache_b_v, state_ssm, c_ctx,
           ada_w, ada_b, norm_g, mlp_w1, mlp_w2, attn_w_in, attn_w_out, attn_qk_norm, diff_lambda,
           diff_subln, ssm_w_in, ssm_a_re, ssm_a_im, ssm_log_dt, ssm_b_re, ssm_b_im, ssm_c_re,
           ssm_c_im, ssm_d, ssm_glu_w, ssm_w_out):
    f = lambda a: np.ascontiguousarray(np.asarray(a, dtype=np.float32))
    nc = _get_prog()
    x_prompt = f(x_prompt)
    x_sample = f(x_sample)
    c_ctx = f(c_ctx)
    c = f(c)
    ident = np.eye(128, dtype=np.float32)
    jmat = np.ascontiguousarray(ident[::-1])
    maskB = np.zeros((4, 128, 8), np.float32)
    maskC = np.zeros((4, 128, 2), np.float32)
    for q in range(4):
        for gl in range(2):
            maskB[q, gl * 64:(gl + 1) * 64, 2 * q + gl] = 1.0
            g8 = 2 * q + gl
            maskC[q, g8 * 16:(g8 + 1) * 16, gl] = 1.0
    shared = dict(
        ident=ident, jmat=jmat, maskB=maskB, maskC=maskC,
        ada_w=f(ada_w), ada_b=f(ada_b), norm_g=f(norm_g), mlp_w1=f(mlp_w1), mlp_w2=f(mlp_w2),
        attn_w_in=f(attn_w_in).reshape(D, ATTN_IN), attn_w_out=f(attn_w_out).reshape(D, D),
        attn_qk_norm=f(attn_qk_norm).reshape(2, 128), diff_lambda=f(diff_lambda).reshape(4, 128),
        diff_subln=f(diff_subln).reshape(256), ssm_w_in=f(ssm_w_in).reshape(D, D),
        ssm_a_re=f(ssm_a_re).reshape(2, 128, 64), ssm_a_im=f(ssm_a_im).reshape(2, 128, 64),
        ssm_log_dt=f(ssm_log_dt).reshape(2, 128), ssm_b_re=f(ssm_b_re).reshape(2, 128, 64, 16),
        ssm_b_im=f(ssm_b_im).reshape(2, 128, 64, 16), ssm_c_re=f(ssm_c_re).reshape(2, 128, 16, 64),
        ssm_c_im=f(ssm_c_im).reshape(2, 128, 16, 64), ssm_d=f(ssm_d).reshape(D),
        ssm_glu_w=f(ssm_glu_w).reshape(D, D), ssm_w_out=f(ssm_w_out).reshape(D, D))
    cos_t, sin_t = _rope_tables()
    in_maps = []
    orders = []
    for core in range(8):
        m = dict(shared)
        b, j = core // 4, core % 4
        xp = x_prompt[core * 4:(core + 1) * 4].reshape(NPT, D)
        m["cond"] = np.stack([c_ctx, c[b]], 0)
        if SAMPLE:
            others = [i for i in range(4) if i != j]
            order = [j] + others
            orders.append(order)
            idx = np.concatenate([np.arange(ch * 1024, (ch + 1) * 1024) for ch in order])
            m["xall"] = np.ascontiguousarray(np.concatenate([xp, x_sample[b][idx]], 0))
            m["rope"] = np.ascontiguousarray(np.stack([cos_t[idx], sin_t[idx]], 0))
            m["cache_ak"] = f(cache_a_k)[b, 0].reshape(256, 256)
            m["cache_av"] = f(cache_a_v)[b, 0].reshape(256, 256)
            m["cache_bk"] = f(cache_b_k)[b, 0].reshape(256, 1024)
            m["cache_bv"] = f(cache_b_v)[b, 0].reshape(256, 1024)
            m["h0"] = np.ascontiguousarray(f(state_ssm)[b, 0])
            sel = np.zeros((2, 4, 4), np.float32)
            sel[0, 0, j] = 1.0
            sel[1, 0, 3 - j] = 1.0
            for s_, i in enumerate(others, start=1):
                if i < j:
                    sel[0, s_, j - 1 - i] = 1.0
                if i > j:
                    sel[1, s_, i - j - 1] = 1.0
            m["sel"] = sel.reshape(1, 32)
        else:
            m["xall"] = xp
        in_maps.append(m)
    res = run_bass_kernel_spmd(nc, in_maps, core_ids=list(range(8)))
    R = res.results
    _CACHE["dbg"] = {k: v for k, v in R[0].items() if k.startswith("dbg_")}
    cat = lambda name: np.concatenate([R[i][name] for i in range(8)], 0)
    y_p = cat("o_yp").reshape(32, 256, D)
    ak = cat("o_ak").reshape(32, 1, 256, 2, 128)
    av = cat("o_av").reshape(32, 1, 256, 2, 128)
    bk = cat("o_bk").reshape(32, 1, 256, 4, 2, 128)
    bv = cat("o_bv").reshape(32, 1, 256, 4, 256)
    st = cat("o_st").reshape(32, 1, 2, 128, 64, 2)
    y_s = np.zeros((2, 4096, D), np.float32)
    for core in range(8):
        b, j = core // 4, core % 4
        y_s[b, j * 1024:(j + 1) * 1024] = R[core]["o_ys"]
    return (y_p, y_s, ak, av, bk, bv, st)
```

```python
import math
from contextlib import ExitStack

import numpy as np
import concourse.bass as bass
import concourse.mybir as mybir
from concourse.bass_utils import run_bass_kernel_spmd

F32 = mybir.dt.float32
BF16 = mybir.dt.bfloat16
I32 = mybir.dt.int32
AF = mybir.ActivationFunctionType
ALU = mybir.AluOpType
AX = mybir.AxisListType

D = 2048
KC = 16
TT = 512
NPT = 1024
NTILE = NPT // TT
EPS = 1e-6
ATTN_IN = 4608

COMPUTE = ("pe", "act", "dve", "pool")
QUEUES = ("sp", "act", "pool")


class Sched:
    def __init__(self, nc, dma_pool=8):
        self.nc = nc
        self.ops = []
        self.last_writer = {}
        self.readers = {}
        self.dma_pool = dma_pool
        self.fences = []

    def fence(self):
        self.fences.append(len(self.ops))
        self.last_writer.clear()
        self.readers.clear()

    def add(self, eng, fn, reads=(), writes=(), dma=False):
        idx = len(self.ops)
        deps = set()
        for r in reads:
            w = self.last_writer.get(r)
            if w is not None:
                deps.add(w)
        for w_ in writes:
            w = self.last_writer.get(w_)
            if w is not None:
                deps.add(w)
            for rd in self.readers.get(w_, ()):
                deps.add(rd)
        deps.discard(idx)
        self.ops.append(dict(eng=eng, fn=fn, deps=deps, dma=dma, need_inc=dma, idx=idx))
        for r in reads:
            lst = self.readers.setdefault(r, [])
            if not dma:
                lst[:] = [x for x in lst if self.ops[x]["dma"] or self.ops[x]["eng"] != eng]
            lst.append(idx)
        for w_ in writes:
            self.last_writer[w_] = idx
            self.readers[w_] = []
        return idx

    def pe(self, fn, reads=(), writes=()):
        return self.add("pe", fn, reads, writes)

    def act(self, fn, reads=(), writes=()):
        return self.add("act", fn, reads, writes)

    def dve(self, fn, reads=(), writes=()):
        return self.add("dve", fn, reads, writes)

    def pool(self, fn, reads=(), writes=()):
        return self.add("pool", fn, reads, writes)

    def dma(self, q, fn, reads=(), writes=()):
        return self.add(q, fn, reads, writes, dma=True)

    def _skip(self, dop, op):
        return (not dop["dma"]) and (not op["dma"]) and dop["eng"] == op["eng"] == "pe"

    def emit(self, ctx):
        nc = self.nc
        import os as _os
        mx = int(_os.environ.get("KMAXOPS", "0"))
        if mx > 0:
            self.ops = self.ops[:mx]
            self.fences = [F for F in self.fences if F <= mx]
        ops = self.ops
        for op in ops:
            for d in op["deps"]:
                dop = ops[d]
                if dop["dma"] or self._skip(dop, op):
                    continue
                dop["need_inc"] = True
        for F in self.fences:
            last = {}
            for op in ops[:F]:
                if not op["dma"]:
                    last[op["eng"]] = op
            for op in last.values():
                op["need_inc"] = True
        sems = {e: ctx.enter_context(nc.semaphore("s_" + e)) for e in COMPUTE}
        dsems = {q: [ctx.enter_context(nc.semaphore("d_%s%d" % (q, i))) for i in range(self.dma_pool)]
                 for q in QUEUES}
        tick = {e: 0 for e in COMPUTE}
        dcount = {q: 0 for q in QUEUES}
        duse = {q: [0] * self.dma_pool for q in QUEUES}
        for op in ops:
            if op["dma"]:
                q = op["eng"]
                s = dcount[q] % self.dma_pool
                dcount[q] += 1
                op["slot_prev"] = duse[q][s] * 16
                duse[q][s] += 1
                op["sem"] = dsems[q][s]
                op["val"] = duse[q][s] * 16
            elif op["need_inc"]:
                tick[op["eng"]] += 1
                op["sem"] = sems[op["eng"]]
                op["val"] = tick[op["eng"]]
        by_eng = {}
        for op in ops:
            by_eng.setdefault(op["eng"], []).append(op)
        fence_waits = []
        for F in self.fences:
            fw = {}
            for op in ops[:F]:
                if op["dma"] or op["need_inc"]:
                    k = id(op["sem"])
                    if k not in fw or fw[k][1] < op["val"]:
                        fw[k] = (op["sem"], op["val"])
            fence_waits.append(list(fw.values()))
        final = []
        for q in QUEUES:
            for s in range(self.dma_pool):
                if duse[q][s]:
                    final.append((dsems[q][s], duse[q][s] * 16))

        def run_engine(eng_name, handle):
            waited = {}
            fi = 0
            for op in by_eng.get(eng_name, []):
                waits = []
                while fi < len(self.fences) and op["idx"] >= self.fences[fi]:
                    waits.extend(fence_waits[fi])
                    fi += 1
                for d in sorted(op["deps"]):
                    dop = ops[d]
                    if self._skip(dop, op):
                        continue
                    waits.append((dop["sem"], dop["val"]))
                if op["dma"] and op["slot_prev"] > 0:
                    waits.append((op["sem"], op["slot_prev"]))
                best = {}
                for sem, val in waits:
                    k = id(sem)
                    if k not in best or best[k][1] < val:
                        best[k] = (sem, val)
                for k, (sem, val) in best.items():
                    if waited.get(k, 0) >= val:
                        continue
                    waited[k] = val
                    handle.wait_ge(sem, val)
                ins = op["fn"](handle)
                if op["dma"]:
                    ins.then_inc(op["sem"], 16)
                elif op["need_inc"]:
                    ins.then_inc(op["sem"], 1)
            if eng_name == "sp":
                for sem, val in final:
                    handle.wait_ge(sem, val)

        with nc.Block() as block:
            @block.sync
            def _(e):
                run_engine("sp", e)

            @block.tensor
            def _(e):
                run_engine("pe", e)

            @block.scalar
            def _(e):
                run_engine("act", e)

            @block.vector
            def _(e):
                run_engine("dve", e)

            @block.gpsimd
            def _(e):
                run_engine("pool", e)


import os
SAMPLE = os.environ.get('KSAMPLE', '1') == '1'
STAGE = int(os.environ.get('KSTAGE', '9'))
NTOK = 5120 if SAMPLE else 1024
NT_ALL = NTOK // TT
NMAIN = 2048 if SAMPLE else 1024
MAIN_TILES = [0, 1, 2, 3] if SAMPLE else [0, 1]
NKEY = 4096 + 256
QSCALE = 128.0 ** -0.5
LAM_INIT0 = 0.8 - 0.6 * math.exp(-0.3 * 0)
SEG = 256


class Prog:
    def __init__(self):
        self.nc = bass.Bass("TRN2", target_bir_lowering=False)
        self.ctx = None
        self.S = Sched(self.nc)
        self.nbank = 0
        self.nslab = 0
        self.cp = 0
        self.rot = {}

    def din(self, name, shape, dt=F32):
        return self.nc.dram_tensor(name, list(shape), dt, kind="ExternalInput").ap()

    def dout(self, name, shape):
        return self.nc.dram_tensor(name, list(shape), F32, kind="ExternalOutput").ap()

    def dscr(self, name, shape, dt):
        return self.nc.dram_tensor(name, list(shape), dt).ap()

    def sb(self, name, shape, dt=F32):
        return self.ctx.enter_context(self.nc.sbuf_tensor(name, list(shape), dt))

    def bank(self):
        i = self.nbank % 8
        self.nbank += 1
        return i

    def rr(self, name, n):
        i = self.rot.get(name, 0)
        self.rot[name] = i + 1
        return i % n

    def copy(self, out, in_, reads, writes, eng=None):
        S = self.S
        if eng is None:
            eng = "dve" if (self.cp % 2 == 0) else "act"
            self.cp += 1
        if eng == "dve":
            S.dve(lambda e: e.tensor_copy(out, in_), reads=reads, writes=writes)
        elif eng == "pool":
            S.pool(lambda e: e.tensor_copy(out, in_), reads=reads, writes=writes)
        else:
            S.act(lambda e: e.activation(out, in_, AF.Copy), reads=reads, writes=writes)

    def build(self):
        nc = self.nc
        S = self.S
        P = self
        TT_ = TT
        xall = P.din("xall", [NTOK, D])
        cond = P.din("cond", [2, D])
        ident_d = P.din("ident", [128, 128])
        jmat_d = P.din("jmat", [128, 128])
        maskB_d = P.din("maskB", [4, 128, 8])
        maskC_d = P.din("maskC", [4, 128, 2])
        ada_w = P.din("ada_w", [2, D, 6 * D])
        ada_b = P.din("ada_b", [2, 6 * D])
        norm_g = P.din("norm_g", [2, 4, D])
        mlp_w1 = P.din("mlp_w1", [2, D, 4 * D])
        mlp_w2 = P.din("mlp_w2", [2, 4 * D, D])
        w_in = P.din("attn_w_in", [D, ATTN_IN])
        w_out = P.din("attn_w_out", [D, D])
        qkn = P.din("attn_qk_norm", [2, 128])
        dlam = P.din("diff_lambda", [4, 128])
        dsub = P.din("diff_subln", [256])
        s_w_in = P.din("ssm_w_in", [D, D])
        s_are = P.din("ssm_a_re", [2, 128, 64])
        s_aim = P.din("ssm_a_im", [2, 128, 64])
        s_ldt = P.din("ssm_log_dt", [2, 128])
        s_bre = P.din("ssm_b_re", [2, 128, 64, 16])
        s_bim = P.din("ssm_b_im", [2, 128, 64, 16])
        s_cre = P.din("ssm_c_re", [2, 128, 16, 64])
        s_cim = P.din("ssm_c_im", [2, 128, 16, 64])
        s_d = P.din("ssm_d", [D])
        s_glu = P.din("ssm_glu_w", [D, D])
        s_wout = P.din("ssm_w_out", [D, D])
        if SAMPLE:
            rope_d = P.din("rope", [2, 4096, 64])
            c_ak = P.din("cache_ak", [256, 256])
            c_av = P.din("cache_av", [256, 256])
            c_bk = P.din("cache_bk", [256, 1024])
            c_bv = P.din("cache_bv", [256, 1024])
            h0_d = P.din("h0", [2, 128, 64, 2])
            sel_d = P.din("sel", [1, 32])
        o_yp = P.dout("o_yp", [1024, D])
        o_ys = P.dout("o_ys", [1024, D])
        o_ak = P.dout("o_ak", [NPT, 256])
        o_av = P.dout("o_av", [NPT, 256])
        o_bk = P.dout("o_bk", [NPT, 1024])
        o_bv = P.dout("o_bv", [NPT, 1024])
        o_st = P.dout("o_st", [4, 2, 64, 256])
        XTs = P.dscr("XTs", [NT_ALL, 128, KC, TT], F32)
        UTs = P.dscr("UTs", [2, KC, 128, NTOK], BF16)
        UTF = P.dscr("UTF", [KC, 128, NMAIN], F32)
        Ys = P.dscr("Ys", [KC, 128, NMAIN], F32)
        if SAMPLE:
            QTs = P.dscr("QTs", [16, 128, 4096], BF16)
            KTs = P.dscr("KTs", [10, 128, NKEY], BF16)
            Vs = P.dscr("Vs", [NKEY, 1280], BF16)
            OTs = P.dscr("OTs", [8, 128, KC, TT], BF16)

        DBG = os.environ.get("KDBG", "0") == "1"
        P.dbg_names = []

        def dbg(name, ap, shape, dt, reads):
            if not DBG:
                return
            d = nc.dram_tensor("dbg_" + name, list(shape), dt, kind="ExternalOutput").ap()
            P.dbg_names.append("dbg_" + name)
            S.dma("sp", ncd(d, ap), reads=reads, writes=[("dbg", name)])
        P.dbg = dbg

        def ncd(out, in_, q="sp"):
            def fn(e):
                with nc.allow_non_contiguous_dma(reason="small/strided layout DMA"):
                    return e.dma_start(out=out, in_=in_)
            return fn

        with ExitStack() as ctx:
            P.ctx = ctx
            ps = [ctx.enter_context(nc.psum_tensor("ps%d" % i, [128, 512], F32)) for i in range(8)]
            psb = [p[:].bitcast(BF16) for p in ps]
            ident = P.sb("ident_sb", [128, 128])
            identb = P.sb("identb", [128, 128], BF16)
            jmat = P.sb("jmat_sb", [128, 128])
            jmatb = P.sb("jmatb", [128, 128], BF16)
            ones_b = P.sb("ones_b", [128, 128], BF16)
            eps_c = P.sb("eps_c", [128, 1])
            condT = P.sb("condT", [128, KC, 2])
            sc = P.sb("sc", [128, KC, 2], BF16)
            adab = P.sb("adab", [128, 2, 96])
            gam = P.sb("gam", [128, 2, 4, KC])
            mod = P.sb("mod", [128, 2, 96, 2])
            DER = P.sb("DER", [128, 2, 2, 4, KC])
            gq = P.sb("gq", [128, 2, 128])
            dl = P.sb("dl", [128, 4, 128])
            dlp = P.sb("dlp", [128, 2, 128])
            lam = P.sb("lam", [128, 4])
            sg = P.sb("sg", [128, 2])
            rstd = P.sb("rstd", [128, TT])
            rc2 = P.sb("rc2", [128, 2, TT])
            hss = P.sb("hss", [128, 8])
            hrs = P.sb("hrs", [128, 8])
            Dcol = P.sb("Dcol", [128, KC])
            MEM = P.sb("MEM", [128, 188 * 256])

            def V(off_kb, shape, dt):
                n = 1
                for s_ in shape[1:]:
                    n *= s_
                nb = n * (4 if dt == F32 else 2)
                a = int(off_kb * 256)
                ap = MEM[:, a:a + nb // 4]
                if dt != F32:
                    ap = ap.bitcast(dt)
                if len(shape) > 2:
                    names = ["d%d" % i for i in range(len(shape) - 1)]
                    pat = "p (" + " ".join(names) + ") -> p " + " ".join(names)
                    kw = {names[i]: shape[i + 1] for i in range(1, len(names))}
                    ap = ap.rearrange(pat, **kw)
                return ap

            X = V(0, [128, KC, TT], F32)
            BF = V(32, [128, KC, TT], F32)
            xtm = V(32, [128, 4, D], F32)
            NTb = V(64, [128, KC, TT], BF16)
            QTt = V(80, [128, 16, TT], BF16)
            KTt = V(96, [128, 10, TT], BF16)
            Vtt = V(106, [128, 4, 1280], BF16)
            OTt = V(116, [128, KC, TT], BF16)
            hT = V(80, [128, 64, TT], BF16)
            gtmp = V(80, [128, KC, TT], F32)
            z2 = V(112, [128, KC, TT], BF16)
            slabs = [V(144 + 16 * i, [128, KC, 512], BF16) for i in range(2)]
            blk32 = [V(176 + 2 * i, [128, 512], F32) for i in range(2)]
            blk16 = [V(180 + i, [128, 512], BF16) for i in range(2)]
            EB = [V(182 + i, [128, 512], BF16) for i in range(2)]
            rtmp = [V(132 + i, [128, 4, 64], F32) for i in range(4)]
            cs_t = V(136, [128, 4, 2, 64], F32)
            hsq = V(187, [128, 512], BF16)

            def bfk(k):
                return ("BF", k)
            BFALL = [bfk(k) for k in range(KC)]

            PRECAST = os.environ.get("KPRECAST", "1") == "1"
            wbf = {}

            def precast(name, w_ap):
                rows, cols = w_ap.shape
                nkg, ncg = rows // 2048, cols // 512
                t_ = P.dscr("wbf_" + name, [nkg, ncg, 128, KC, 512], BF16)
                wbf[name] = t_
                for kg in range(nkg):
                    for cg in range(ncg):
                        src = w_ap[kg * 2048:(kg + 1) * 2048, cg * 512:(cg + 1) * 512].rearrange("(k p) n -> p k n", p=128)
                        S.dma("pool", lambda e, kg=kg, cg=cg, src=src, t_=t_: e.dma_start(out=t_[kg, cg], in_=src), writes=[("wbf", name, kg, cg)])

            def load_w(w_ap, r0, c0, ncols=512, kc=KC, name=None):
                i = P.nslab % 2
                P.nslab += 1
                sl = slabs[i]
                if name is not None and name in wbf:
                    kg, cg = r0 // 2048, c0 // 512
                    t_ = wbf[name]
                    S.dma("sp", lambda e, kg=kg, cg=cg, t_=t_: e.dma_start(out=sl, in_=t_[kg, cg]), reads=[("wbf", name, kg, cg)], writes=[("slab", i)])
                    return sl, ("slab", i)
                src = w_ap[r0:r0 + kc * 128, c0:c0 + ncols].rearrange("(k p) n -> p k n", p=128)
                S.dma("pool", lambda e: e.dma_start(out=sl[:, 0:kc, 0:ncols], in_=src), writes=[("slab", i)])
                return sl, ("slab", i)

            S.dma("sp", lambda e: e.dma_start(out=ident[:], in_=ident_d[:, :]), writes=["ident"])
            S.dma("sp", lambda e: e.dma_start(out=jmat[:], in_=jmat_d[:, :]), writes=["jmat"])
            S.dve(lambda e: e.tensor_copy(identb[:], ident[:]), reads=["ident"], writes=["identb"])
            S.dve(lambda e: e.tensor_copy(jmatb[:], jmat[:]), reads=["jmat"], writes=["jmatb"])
            S.pool(lambda e: e.memset(ones_b[:], 1.0), writes=["ones_b"])
            S.pool(lambda e: e.memset(eps_c[:], EPS), writes=["eps_c"])
            for r in range(2):
                S.dma("sp", ncd(condT[:, :, r], cond[r].rearrange("(c p) -> p c", p=128)), writes=["condT"])
            for l in range(2):
                S.dma("sp", ncd(adab[:, l, :], ada_b[l].rearrange("(j p) -> p j", p=128)), writes=["adab"])
                for f_ in range(4):
                    S.dma("sp", ncd(gam[:, l, f_, :], norm_g[l, f_].rearrange("(c p) -> p c", p=128)), writes=["gam"])
            for r in range(2):
                S.dma("sp", ncd(gq[:, r, :], qkn[r:r + 1, :].partition_broadcast(128)[:, 0, :]), writes=["gq"])
            for r in range(4):
                S.dma("sp", ncd(dl[:, r, :], dlam[r:r + 1, :].partition_broadcast(128)[:, 0, :]), writes=["dl"])
            S.dma("sp", ncd(sg[:], dsub.rearrange("(h p) -> p h", p=128)), writes=["sg"])
            S.dma("sp", ncd(Dcol[:], s_d.rearrange("(c p) -> p c", p=128)), writes=["Dcol"])
            S.act(lambda e: e.activation(sc[:], condT[:], AF.Silu), reads=["condT"], writes=["sc"])
            S.dve(lambda e: e.tensor_tensor(dlp[:, 0, :], dl[:, 0, :], dl[:, 1, :], op=ALU.mult), reads=["dl"], writes=["dlp"])
            S.dve(lambda e: e.tensor_tensor(dlp[:, 1, :], dl[:, 2, :], dl[:, 3, :], op=ALU.mult), reads=["dl"], writes=["dlp"])
            S.dve(lambda e: e.tensor_reduce(out=lam[:, 0:2], in_=dlp[:], op=ALU.add, axis=AX.X), reads=["dlp"], writes=["lam"])
            S.act(lambda e: e.activation(lam[:, 0:2], lam[:, 0:2], AF.Exp), reads=["lam"], writes=["lam"])
            S.dve(lambda e: e.tensor_tensor(lam[:, 2:3], lam[:, 0:1], lam[:, 1:2], op=ALU.subtract), reads=["lam"], writes=["lam"])
            S.dve(lambda e: e.tensor_scalar(lam[:, 3:4], lam[:, 2:3], LAM_INIT0, -1.0, op0=ALU.add, op1=ALU.mult), reads=["lam"], writes=["lam"])
            S.dve(lambda e: e.tensor_scalar(sg[:], sg[:], 1.0 - LAM_INIT0, None, op0=ALU.mult), reads=["sg"], writes=["sg"])

            for l in range(2):
                for sl_i in range(24):
                    sl, sk = load_w(ada_w[l], 0, sl_i * 512)
                    for j in range(4):
                        b = P.bank()
                        for k in range(KC):
                            S.pe(lambda e, b=b, k=k, j=j, sl=sl: e.matmul(ps[b][:, 0:2], sl[:, k, j * 128:(j + 1) * 128], sc[:, k, :],
                                                                       start=(k == 0), stop=(k == KC - 1)),
                                 reads=[sk, "sc"], writes=[("ps", b)])
                        ch = sl_i * 4 + j
                        S.dve(lambda e, b=b, ch=ch, l=l: e.tensor_scalar(mod[:, l, ch, :], ps[b][:, 0:2], adab[:, l, ch:ch + 1], None, op0=ALU.add),
                              reads=[("ps", b), "adab"], writes=["mod"])
                for r in range(2):
                    S.dve(lambda e, l=l, r=r: e.scalar_tensor_tensor(DER[:, l, r, 0, :], mod[:, l, 16:32, r], 1.0, gam[:, l, 0, :], op0=ALU.add, op1=ALU.mult),
                          reads=["mod", "gam"], writes=["DER"])
                    S.dve(lambda e, l=l, r=r: e.tensor_tensor(DER[:, l, r, 1, :], mod[:, l, 32:48, r], gam[:, l, 1, :], op=ALU.mult),
                          reads=["mod", "gam"], writes=["DER"])
                    S.dve(lambda e, l=l, r=r: e.scalar_tensor_tensor(DER[:, l, r, 2, :], mod[:, l, 64:80, r], 1.0, gam[:, l, 2, :], op0=ALU.add, op1=ALU.mult),
                          reads=["mod", "gam"], writes=["DER"])
                    S.dve(lambda e, l=l, r=r: e.tensor_tensor(DER[:, l, r, 3, :], mod[:, l, 80:96, r], gam[:, l, 3, :], op=ALU.mult),
                          reads=["mod", "gam"], writes=["DER"])

            if PRECAST:
                for nm_, w_ in (("a_in", w_in), ("a_out", w_out), ("w1_0", mlp_w1[0]), ("w2_0", mlp_w2[0]), ("s_in", s_w_in),
                                ("s_glu", s_glu), ("s_wout", s_wout), ("w1_1", mlp_w1[1]), ("w2_1", mlp_w2[1])):
                    precast(nm_, w_)

            def stats_rstd(src, skeys, kc=KC, dim=D):
                for k in range(kc):
                    S.act(lambda e, k=k: e.activation(NTb[:, k, :], src[:, k, :], AF.Square), reads=[skeys[k]], writes=[("NT", k)])
                b = P.bank()
                for k in range(kc):
                    S.pe(lambda e, k=k, b=b: e.matmul(ps[b][:, :], ones_b[:], NTb[:, k, :], start=(k == 0), stop=(k == kc - 1)),
                         reads=[("NT", k), "ones_b"], writes=[("ps", b)])
                S.act(lambda e, b=b: e.activation(rstd[:], ps[b][:, :], AF.Sqrt, bias=eps_c[:, 0:1], scale=1.0 / dim),
                      reads=[("ps", b), "eps_c"], writes=["rstd"])
                S.dve(lambda e: e.reciprocal(rstd[:], rstd[:]), reads=["rstd"], writes=["rstd"])

            def modulate(l, r, kind, shift_lo):
                for k in range(KC):
                    S.dve(lambda e, k=k: e.tensor_tensor(BF[:, k, :], X[:, k, :], rstd[:], op=ALU.mult),
                          reads=[("X", k), "rstd"], writes=[bfk(k)])
                    S.act(lambda e, k=k: e.activation(NTb[:, k, :], BF[:, k, :], AF.Identity,
                                                      bias=mod[:, l, shift_lo + k, r:r + 1], scale=DER[:, l, r, kind, k:k + 1]),
                          reads=[bfk(k), "DER", "mod"], writes=[("NT", k)])

            def residual(l, r, kind):
                for k in range(KC):
                    S.pool(lambda e, k=k: e.tensor_tensor(BF[:, k, :], BF[:, k, :], rstd[:], op=ALU.mult),
                           reads=[bfk(k), "rstd"], writes=[bfk(k)])
                    S.dve(lambda e, k=k: e.scalar_tensor_tensor(X[:, k, :], BF[:, k, :], DER[:, l, r, kind, k:k + 1], X[:, k, :],
                                                                op0=ALU.mult, op1=ALU.add),
                          reads=[bfk(k), "DER", ("X", k)], writes=[("X", k)])

            def linear_fm(in_get, in_keys, kcin, w_ap, n_out, evac, name=None):
                for cg in range(n_out // 512):
                    banks = [P.bank() for _ in range(4)]
                    nkg = kcin // 16
                    for kg in range(nkg):
                        sl, sk = load_w(w_ap, kg * 2048, cg * 512, name=name)
                        for j in range(4):
                            for k in range(16):
                                kk = kg * 16 + k
                                S.pe(lambda e, b=banks[j], k=k, j=j, kk=kk, sl=sl, kg=kg: e.matmul(
                                    ps[b][:, :], sl[:, k, j * 128:(j + 1) * 128], in_get(kk),
                                    start=(kg == 0 and k == 0), stop=(kg == nkg - 1 and k == 15)),
                                    reads=[sk, in_keys[kk]], writes=[("ps", banks[j])])
                    for j in range(4):
                        evac(cg * 4 + j, banks[j])

            def mlp(l, r):
                stats_rstd(X, [("X", k) for k in range(KC)])
                modulate(l, r, 2, 48)

                def ev1(j, b):
                    i = P.rr("relu", 2)
                    S.act(lambda e: e.activation(blk32[i][:, :], ps[b][:, :], AF.Relu), reads=[("ps", b)], writes=[("blk32", i)])
                    S.pool(lambda e: e.tensor_tensor(hT[:, j, :], blk32[i][:, :], blk32[i][:, :], op=ALU.mult),
                           reads=[("blk32", i)], writes=[("hT", j)])
                linear_fm(lambda kk: NTb[:, kk, :], [("NT", k) for k in range(KC)], KC, mlp_w1[l], 4 * D, ev1, name="w1_%d" % l)

                def ev2(j, b):
                    P.copy(BF[:, j, :], ps[b][:, :], reads=[("ps", b)], writes=[bfk(j)])
                linear_fm(lambda kk: hT[:, kk, :], [("hT", k) for k in range(64)], 64, mlp_w2[l], D, ev2, name="w2_%d" % l)
                stats_rstd(BF, BFALL)
                residual(l, r, 3)
            def tile_r(t):
                return 0 if t < 2 else 1

            def passA(t):
                r = tile_r(t)
                samp = t >= 2
                for sub in range(4):
                    r0 = t * TT + sub * 128
                    S.dma("sp", lambda e, sub=sub, r0=r0: e.dma_start(out=xtm[:, sub, :], in_=xall[r0:r0 + 128, :]),
                          writes=[bfk(4 * sub + i) for i in range(4)])
                    if samp:
                        tr0 = (t - 2) * TT + sub * 128
                        for cs_i in range(2):
                            S.dma("sp", lambda e, sub=sub, tr0=tr0, cs_i=cs_i: e.dma_start(out=cs_t[:, sub, cs_i, :], in_=rope_d[cs_i, tr0:tr0 + 128, :]),
                                  writes=[("cs", sub)])
                for k in range(KC):
                    b = P.bank()
                    for sub in range(4):
                        S.pe(lambda e, b=b, k=k, sub=sub: e.transpose(ps[b][:, sub * 128:(sub + 1) * 128], xtm[:, sub, k * 128:(k + 1) * 128], ident[:]),
                             reads=[bfk(4 * sub + k // 4), "ident"], writes=[("ps", b)])
                    P.copy(X[:, k, :], ps[b][:, :], reads=[("ps", b)], writes=[("X", k)])
                stats_rstd(X, [("X", k) for k in range(KC)])
                modulate(0, r, 0, 0)
                if samp:
                    S.dma("sp", lambda e: e.dma_start(out=XTs[t], in_=X), reads=[("X", k) for k in range(KC)], writes=[("XTs", t)])

                def rope(src32, nmap, sub, dst16):
                    s3 = src32.rearrange("p (h d) -> p h d", d=128)
                    d3 = dst16.rearrange("p (h d) -> p h d", d=128)
                    cosb = cs_t[:, sub, 0, :].unsqueeze(1).to_broadcast([128, nmap, 64])
                    sinb = cs_t[:, sub, 1, :].unsqueeze(1).to_broadcast([128, nmap, 64])
                    x1 = s3[:, :, 0:64]
                    x2 = s3[:, :, 64:128]
                    T = [rt[:, 0:nmap, :] for rt in rtmp]
                    rk = [("rtmp", i) for i in range(4)]
                    S.dve(lambda e: e.tensor_tensor(T[0], x1, cosb, op=ALU.mult), reads=[skey_cur[0], ("cs", sub)], writes=[rk[0]])
                    S.pool(lambda e: e.tensor_tensor(T[1], x2, sinb, op=ALU.mult), reads=[skey_cur[0], ("cs", sub)], writes=[rk[1]])
                    S.pool(lambda e: e.tensor_tensor(T[2], x1, sinb, op=ALU.mult), reads=[skey_cur[0], ("cs", sub)], writes=[rk[2]])
                    S.dve(lambda e: e.tensor_tensor(T[3], x2, cosb, op=ALU.mult), reads=[skey_cur[0], ("cs", sub)], writes=[rk[3]])
                    S.dve(lambda e: e.tensor_tensor(d3[:, :, 0:64], T[0], T[1], op=ALU.subtract), reads=[rk[0], rk[1]], writes=[dkey_cur[0]])
                    S.pool(lambda e: e.tensor_tensor(d3[:, :, 64:128], T[2], T[3], op=ALU.add), reads=[rk[2], rk[3]], writes=[dkey_cur[0]])

                skey_cur = [None]
                dkey_cur = [None]

                def to_T(src16, skey, nmap, dst, dkeys):
                    b = P.bank()
                    for i in range(nmap):
                        S.pe(lambda e, b=b, i=i: e.transpose(psb[b][:, i * 128:(i + 1) * 128], src16[:, i * 128:(i + 1) * 128], identb[:]),
                             reads=[skey, "identb"], writes=[("ps", b)])
                    P.copy(dst, psb[b][:, 0:nmap * 128].rearrange("p (i t) -> p i t", t=128), reads=[("ps", b)], writes=dkeys)

                for s_i in range(9):
                    sl, sk = load_w(w_in, 0, s_i * 512, name="a_in")
                    for sub in range(4):
                        b = P.bank()
                        for k in range(KC):
                            S.pe(lambda e, b=b, k=k, sub=sub, sl=sl: e.matmul(ps[b][:, :], NTb[:, k, sub * 128:(sub + 1) * 128], sl[:, k, :],
                                                                         start=(k == 0), stop=(k == KC - 1)),
                                 reads=[("NT", k), sk], writes=[("ps", b)])
                        r0 = t * TT + sub * 128
                        i32 = P.rr("blk32", 2)
                        i16 = P.rr("blk16", 2)
                        b32 = blk32[i32]
                        b16 = blk16[i16]
                        k32 = ("blk32", i32)
                        k16 = ("blk16", i16)
                        skey_cur[0] = k32
                        dkey_cur[0] = k16
                        tsl = slice(sub * 128, (sub + 1) * 128)
                        if s_i in (0, 1, 2):
                            nh = 4 if s_i < 2 else 2
                            gi = 0 if s_i < 2 else 1
                            S.act(lambda e, b=b, nh=nh: e.activation(hsq[:, 0:nh * 128], ps[b][:, 0:nh * 128], AF.Square), reads=[("ps", b)], writes=["hsq"])
                            S.dve(lambda e, nh=nh: e.tensor_reduce(out=hss[:, 0:nh], in_=hsq[:, 0:nh * 128].rearrange("p (h d) -> p h d", h=nh), op=ALU.add, axis=AX.X),
                                  reads=["hsq"], writes=["hss"])
                            S.act(lambda e, nh=nh: e.activation(hrs[:, 0:nh], hss[:, 0:nh], AF.Sqrt, bias=eps_c[:, 0:1], scale=1.0 / 128), reads=["hss", "eps_c"], writes=["hrs"])
                            S.dve(lambda e, nh=nh: e.reciprocal(hrs[:, 0:nh], hrs[:, 0:nh]), reads=["hrs"], writes=["hrs"])
                            for h in range(nh):
                                S.dve(lambda e, b=b, h=h, b32=b32, gi=gi: e.scalar_tensor_tensor(b32[:, h * 128:(h + 1) * 128], ps[b][:, h * 128:(h + 1) * 128],
                                                                                               hrs[:, h:h + 1], gq[:, gi, :], op0=ALU.mult, op1=ALU.mult),
                                      reads=[("ps", b), "hrs", "gq"], writes=[k32])
                            if s_i == 2:
                                S.dve(lambda e, b=b, b32=b32: e.tensor_copy(b32[:, 256:512], ps[b][:, 256:512]), reads=[("ps", b), "hrs"], writes=[k32])
                            if samp:
                                rope(b32[:, 0:nh * 128], nh, sub, b16[:, 0:nh * 128])
                            else:
                                P.copy(b16[:, 0:nh * 128], b32[:, 0:nh * 128], reads=[k32], writes=[k16], eng="pool")
                            if s_i < 2:
                                to_T(b16, k16, 4, QTt[:, 4 * s_i:4 * s_i + 4, tsl], [("QT", 4 * s_i + i) for i in range(4)])
                            else:
                                to_T(b16, k16, 2, KTt[:, 0:2, tsl], [("KT", 0), ("KT", 1)])
                                P.copy(Vtt[:, sub, 0:256], b32[:, 256:512], reads=[k32], writes=[("Vt", sub)], eng="pool")
                                if not samp:
                                    S.dma("sp", lambda e, b32=b32, r0=r0: e.dma_start(out=o_ak[r0:r0 + 128, :], in_=b32[:, 0:256]), reads=[k32], writes=[("o_ak", r0)])
                                    S.dma("sp", lambda e, b32=b32, r0=r0: e.dma_start(out=o_av[r0:r0 + 128, :], in_=b32[:, 256:512]), reads=[k32], writes=[("o_av", r0)])
                        else:
                            P.copy(b32[:, :], ps[b][:, :], reads=[("ps", b)], writes=[k32])
                            if s_i in (3, 4, 5, 6):
                                if samp:
                                    rope(b32[:, :], 4, sub, b16[:, :])
                                else:
                                    P.copy(b16[:, :], b32[:, :], reads=[k32], writes=[k16], eng="pool")
                                if s_i in (3, 4):
                                    hm0 = 8 + 4 * (s_i - 3)
                                    to_T(b16, k16, 4, QTt[:, hm0:hm0 + 4, tsl], [("QT", hm0 + i) for i in range(4)])
                                else:
                                    hm0 = 2 + 4 * (s_i - 5)
                                    to_T(b16, k16, 4, KTt[:, hm0:hm0 + 4, tsl], [("KT", hm0 + i) for i in range(4)])
                                    if not samp:
                                        c0 = (s_i - 5) * 512
                                        S.dma("sp", lambda e, b32=b32, r0=r0, c0=c0: e.dma_start(out=o_bk[r0:r0 + 128, c0:c0 + 512], in_=b32[:, :]),
                                              reads=[k32], writes=[("o_bk", s_i, r0)])
                            else:
                                c0 = (s_i - 7) * 512
                                P.copy(Vtt[:, sub, 256 + c0:256 + c0 + 512], b32[:, :], reads=[k32], writes=[("Vt", sub)], eng="pool")
                                if not samp:
                                    S.dma("sp", lambda e, b32=b32, r0=r0, c0=c0: e.dma_start(out=o_bv[r0:r0 + 128, c0:c0 + 512], in_=b32[:, :]),
                                          reads=[k32], writes=[("o_bv", s_i, r0)])
                if samp:
                    tk0 = (t - 2) * TT
                    S.dma("sp", ncd(QTs[:, :, tk0:tk0 + TT].rearrange("h p t -> p h t"), QTt), reads=[("QT", i) for i in range(16)], writes=[("QTs", t)])
                    S.dma("sp", ncd(KTs[:, :, tk0:tk0 + TT].rearrange("h p t -> p h t"), KTt), reads=[("KT", i) for i in range(10)], writes=[("KTs", t)])
                    S.dma("sp", ncd(Vs[tk0:tk0 + TT, :].rearrange("(s p) c -> p s c", p=128), Vtt), reads=[("Vt", i) for i in range(4)], writes=[("Vs", t)])

            ATT_S = [0, 1]

            def att_gqa(qT, qkeys, kT_get, kkeys_get, v_get, vkeys_get, nkt, nq, out_ap, okeys):
                ob, db = 2, 6
                for kt in range(nkt):
                    sbk = ATT_S[P.rr("attS", 2)]
                    ei = P.rr("EB", 2)
                    S.pe(lambda e, sbk=sbk, kt=kt: e.matmul(ps[sbk][:, 0:nq], kT_get(kt), qT, start=True, stop=True),
                         reads=list(qkeys) + list(kkeys_get(kt)), writes=[("ps", sbk)])
                    S.act(lambda e, sbk=sbk, ei=ei: e.activation(EB[ei][:, 0:nq], ps[sbk][:, 0:nq], AF.Exp, scale=QSCALE),
                          reads=[("ps", sbk)], writes=[("EB", ei)])
                    S.pe(lambda e, kt=kt, ei=ei: e.matmul(ps[ob][:, 0:nq], v_get(kt), EB[ei][:, 0:nq], start=(kt == 0), stop=(kt == nkt - 1)),
                         reads=[("EB", ei)] + list(vkeys_get(kt)), writes=[("ps", ob)])
                    S.pe(lambda e, kt=kt, ei=ei: e.matmul(ps[db][:, 0:nq], ones_b[:], EB[ei][:, 0:nq], start=(kt == 0), stop=(kt == nkt - 1)),
                         reads=[("EB", ei), "ones_b"], writes=[("ps", db)])
                S.dve(lambda e: e.reciprocal(rc2[:, 0, 0:nq], ps[db][:, 0:nq]), reads=[("ps", db)], writes=["rc2"])
                S.dve(lambda e: e.tensor_tensor(out_ap, ps[ob][:, 0:nq], rc2[:, 0, 0:nq], op=ALU.mult), reads=[("ps", ob), "rc2"], writes=okeys)

            def att_diff(qT2, qkeys, kT_get, kkeys_get, v_get, vkeys_get, nkt, nq, out_get, okeys):
                for kt in range(nkt):
                    for m in range(2):
                        sbk = ATT_S[P.rr("attS", 2)]
                        ei = P.rr("EB", 2)
                        S.pe(lambda e, sbk=sbk, kt=kt, m=m: e.matmul(ps[sbk][:, 0:nq], kT_get(m, kt), qT2(m), start=True, stop=True),
                             reads=list(qkeys) + list(kkeys_get(kt)), writes=[("ps", sbk)])
                        S.act(lambda e, sbk=sbk, ei=ei: e.activation(EB[ei][:, 0:nq], ps[sbk][:, 0:nq], AF.Exp, scale=QSCALE),
                              reads=[("ps", sbk)], writes=[("EB", ei)])
                        for half in range(2):
                            ob = 2 + 2 * m + half
                            S.pe(lambda e, kt=kt, ei=ei, ob=ob, half=half: e.matmul(ps[ob][:, 0:nq], v_get(kt, half), EB[ei][:, 0:nq], start=(kt == 0), stop=(kt == nkt - 1)),
                                 reads=[("EB", ei)] + list(vkeys_get(kt)), writes=[("ps", ob)])
                        S.pe(lambda e, kt=kt, ei=ei, m=m: e.matmul(ps[6 + m][:, 0:nq], ones_b[:], EB[ei][:, 0:nq], start=(kt == 0), stop=(kt == nkt - 1)),
                             reads=[("EB", ei), "ones_b"], writes=[("ps", 6 + m)])
                S.dve(lambda e: e.reciprocal(rc2[:, 0, 0:nq], ps[6][:, 0:nq]), reads=[("ps", 6)], writes=["rc2"])
                S.dve(lambda e: e.reciprocal(rc2[:, 1, 0:nq], ps[7][:, 0:nq]), reads=[("ps", 7)], writes=["rc2"])
                S.dve(lambda e: e.tensor_scalar(rc2[:, 1, 0:nq], rc2[:, 1, 0:nq], lam[:, 3:4], None, op0=ALU.mult), reads=["rc2", "lam"], writes=["rc2"])
                dfo = [blk32[0], blk32[1]]
                for half in range(2):
                    S.dve(lambda e, half=half: e.tensor_tensor(dfo[half][:, 0:nq], ps[2 + half][:, 0:nq], rc2[:, 0, 0:nq], op=ALU.mult),
                          reads=[("ps", 2 + half), "rc2"], writes=[("blk32", half)])
                    S.dve(lambda e, half=half: e.tensor_tensor(EB[half][:, 0:nq], ps[4 + half][:, 0:nq], rc2[:, 1, 0:nq], op=ALU.mult),
                          reads=[("ps", 4 + half), "rc2"], writes=[("EB", half)])
                    S.pool(lambda e, half=half: e.tensor_tensor(dfo[half][:, 0:nq], dfo[half][:, 0:nq], EB[half][:, 0:nq], op=ALU.add),
                           reads=[("blk32", half), ("EB", half)], writes=[("blk32", half)])
                for half in range(2):
                    S.act(lambda e, half=half: e.activation(blk16[half][:, 0:nq], dfo[half][:, 0:nq], AF.Square), reads=[("blk32", half)], writes=[("blk16", half)])
                sbk = ATT_S[P.rr("attS", 2)]
                for half in range(2):
                    S.pe(lambda e, half=half, sbk=sbk: e.matmul(ps[sbk][:, 0:nq], ones_b[:], blk16[half][:, 0:nq], start=(half == 0), stop=(half == 1)),
                         reads=[("blk16", half), "ones_b"], writes=[("ps", sbk)])
                S.act(lambda e, sbk=sbk: e.activation(rc2[:, 0, 0:nq], ps[sbk][:, 0:nq], AF.Sqrt, bias=eps_c[:, 0:1], scale=1.0 / 256), reads=[("ps", sbk), "eps_c"], writes=["rc2"])
                S.dve(lambda e: e.reciprocal(rc2[:, 0, 0:nq], rc2[:, 0, 0:nq]), reads=["rc2"], writes=["rc2"])
                for half in range(2):
                    S.dve(lambda e, half=half: e.scalar_tensor_tensor(out_get(half), dfo[half][:, 0:nq], sg[:, half:half + 1], rc2[:, 0, 0:nq], op0=ALU.mult, op1=ALU.mult),
                          reads=[("blk32", half), "rc2", "sg"], writes=okeys)

            def attention_prompt_tile():
                for sq_i in range(2):
                    cs = slice(sq_i * 256, (sq_i + 1) * 256)
                    for h in range(8):
                        g = h // 4
                        att_gqa(QTt[:, h, cs], [("QT", h)],
                                lambda kt, g=g, sq_i=sq_i: KTt[:, g, sq_i * 256 + kt * 128: sq_i * 256 + (kt + 1) * 128], lambda kt, g=g: [("KT", g)],
                                lambda kt, g=g, sq_i=sq_i: Vtt[:, 2 * sq_i + kt, g * 128:(g + 1) * 128], lambda kt, sq_i=sq_i: [("Vt", 2 * sq_i + kt)],
                                2, 256, OTt[:, h, cs], [("OT", h)])
                    for h in range(4):
                        att_diff(lambda m, h=h, cs=cs: QTt[:, 8 + 2 * h + m, cs], [("QT", 8 + 2 * h), ("QT", 9 + 2 * h)],
                                 lambda m, kt, h=h, sq_i=sq_i: KTt[:, 2 + 2 * h + m, sq_i * 256 + kt * 128: sq_i * 256 + (kt + 1) * 128],
                                 lambda kt, h=h: [("KT", 2 + 2 * h), ("KT", 3 + 2 * h)],
                                 lambda kt, half, h=h, sq_i=sq_i: Vtt[:, 2 * sq_i + kt, 256 + h * 256 + half * 128: 256 + h * 256 + (half + 1) * 128],
                                 lambda kt, sq_i=sq_i: [("Vt", 2 * sq_i + kt)],
                                 2, 256, lambda half, h=h, cs=cs: OTt[:, 8 + 2 * h + half, cs], [("OT", 8 + 2 * h), ("OT", 9 + 2 * h)])

            def rev_pos(t, sub):
                if t < 2:
                    pr, pos = sub // 2, sub % 2
                    return t * TT + (2 * pr + (1 - pos)) * 128
                slot = (t - 2) // 2
                s8 = ((t - 2) % 2) * 4 + sub
                s8r = 7 - s8
                return 1024 + slot * 1024 + s8r * 128

            def main_index(t):
                return MAIN_TILES.index(t) if t in MAIN_TILES else None

            def passC(t):
                r = tile_r(t)

                def evo(j, b):
                    P.copy(BF[:, j, :], ps[b][:, :], reads=[("ps", b)], writes=[bfk(j)])
                linear_fm(lambda kk: OTt[:, kk, :], [("OT", k) for k in range(KC)], KC, w_out, D, evo, name="a_out")
                stats_rstd(BF, BFALL)
                residual(0, r, 1)
                P.mark("mlp")
                switch(K_ATT, K_HT)
                mlp(0, r)
                switch(K_HT, K_UST)
                P.mark("l1inproj")
                stats_rstd(X, [("X", k) for k in range(KC)])
                modulate(1, r, 0, 0)
                mi = main_index(t)
                if mi is not None:
                    S.dma("sp", lambda e: e.dma_start(out=XTs[t], in_=X), reads=[("X", k) for k in range(KC)], writes=[("XTs", t)])
                for cg in range(4):
                    sl, sk = load_w(s_w_in, 0, cg * 512, name="s_in")
                    for sub in range(4):
                        b = P.bank()
                        for k in range(KC):
                            S.pe(lambda e, b=b, k=k, sub=sub, sl=sl: e.matmul(ps[b][:, :], NTb[:, k, sub * 128:(sub + 1) * 128], sl[:, k, :],
                                                                         start=(k == 0), stop=(k == KC - 1)),
                                 reads=[("NT", k), sk], writes=[("ps", b)])
                        i32 = P.rr("blk32", 2)
                        i16 = P.rr("blk16", 2)
                        b32, k32 = blk32[i32], ("blk32", i32)
                        b16, k16 = blk16[i16], ("blk16", i16)
                        S.act(lambda e, b=b, b32=b32: e.activation(b32[:, :], ps[b][:, :], AF.Copy), reads=[("ps", b)], writes=[k32])
                        S.pool(lambda e, b32=b32, b16=b16: e.tensor_copy(b16[:, :], b32[:, :]), reads=[k32], writes=[k16])
                        gpos = t * TT + sub * 128
                        rpos = rev_pos(t, sub)
                        bt = P.bank()
                        for i in range(4):
                            S.pe(lambda e, bt=bt, i=i, b32=b32: e.transpose(ps[bt][:, i * 128:(i + 1) * 128], b32[:, i * 128:(i + 1) * 128], ident[:]),
                                 reads=[k32, "ident"], writes=[("ps", bt)])
                        ui = P.rr("ust", 2)
                        S.act(lambda e, bt=bt, ui=ui: e.activation(ust32[ui][:, :, :], ps[bt][:, :].rearrange("p (i t) -> p i t", t=128), AF.Copy),
                              reads=[("ps", bt)], writes=[("ust32", ui)])
                        S.dve(lambda e, ui=ui: e.tensor_copy(ust16[ui][:, 0, :, :], ust32[ui][:, :, :]),
                              reads=[("ust32", ui)], writes=[("ust16", ui, 0)])
                        S.dma("sp", ncd(UTs[0, 4 * cg:4 * cg + 4, :, gpos:gpos + 128].rearrange("c p t -> p c t"), ust16[ui][:, 0, :, :]),
                              reads=[("ust16", ui, 0)], writes=[("UTs", 0, cg, gpos)])
                        if mi is not None:
                            mpos = mi * TT + sub * 128
                            S.dma("sp", ncd(UTF[4 * cg:4 * cg + 4, :, mpos:mpos + 128].rearrange("c p t -> p c t"), ust32[ui][:, :, :]),
                                  reads=[("ust32", ui)], writes=[("UTF", cg, mpos)])
                        br = P.bank()
                        for i in range(4):
                            S.pe(lambda e, br=br, i=i, b16=b16: e.matmul(ps[br][:, i * 128:(i + 1) * 128], b16[:, i * 128:(i + 1) * 128], jmatb[:], start=True, stop=True),
                                 reads=[k16, "jmatb"], writes=[("ps", br)])
                        S.act(lambda e, br=br, ui=ui: e.activation(ust16[ui][:, 1, :, :], ps[br][:, :].rearrange("p (i t) -> p i t", t=128), AF.Copy),
                              reads=[("ps", br)], writes=[("ust16", ui, 1)])
                        S.dma("sp", ncd(UTs[1, 4 * cg:4 * cg + 4, :, rpos:rpos + 128].rearrange("c p t -> p c t"), ust16[ui][:, 1, :, :]),
                              reads=[("ust16", ui, 1)], writes=[("UTs", 1, cg, rpos)])

            ust16 = [V(80 + 2 * i, [128, 2, 4, 128], BF16) for i in range(2)]
            ust32 = [V(84 + 2 * i, [128, 4, 128], F32) for i in range(2)]

            def passE(t):
                r = tile_r(t)
                mi = main_index(t)
                switch(K_HT, K_GT)
                S.dma("sp", lambda e: e.dma_start(out=X, in_=XTs[t]), reads=[("XTs", t)], writes=[("X", k) for k in range(KC)])
                S.dma("sp", ncd(BF, Ys[:, :, mi * TT:(mi + 1) * TT].rearrange("c p t -> p c t")), reads=["Ys"], writes=BFALL)
                for k in range(KC):
                    gk = ("gtmp", k)
                    S.pool(lambda e, k=k: e.tensor_tensor(gtmp[:, k, :], BF[:, k, :], BF[:, k, :], op=ALU.mult), reads=[bfk(k)], writes=[gk])
                    S.dve(lambda e, k=k: e.tensor_scalar(gtmp[:, k, :], gtmp[:, k, :], 0.044715, 1.0, op0=ALU.mult, op1=ALU.add), reads=[gk], writes=[gk])
                    S.pool(lambda e, k=k: e.tensor_tensor(gtmp[:, k, :], gtmp[:, k, :], BF[:, k, :], op=ALU.mult), reads=[gk, bfk(k)], writes=[gk])
                    S.act(lambda e, k=k: e.activation(gtmp[:, k, :], gtmp[:, k, :], AF.Sigmoid, scale=2.0 * math.sqrt(2.0 / math.pi)), reads=[gk], writes=[gk])
                    S.dve(lambda e, k=k: e.tensor_tensor(BF[:, k, :], BF[:, k, :], gtmp[:, k, :], op=ALU.mult), reads=[gk, bfk(k)], writes=[bfk(k)])
                    S.act(lambda e, k=k: e.activation(NTb[:, k, :], BF[:, k, :], AF.Copy), reads=[bfk(k)], writes=[("NT", k)])

                def evg(j, b):
                    i = P.rr("relu", 2)
                    S.act(lambda e: e.activation(blk32[i][:, :], ps[b][:, :], AF.Sigmoid), reads=[("ps", b)], writes=[("blk32", i)])
                    S.dve(lambda e: e.tensor_tensor(z2[:, j, :], BF[:, j, :], blk32[i][:, :], op=ALU.mult), reads=[("blk32", i), bfk(j)], writes=[("z2", j)])
                linear_fm(lambda kk: NTb[:, kk, :], [("NT", k) for k in range(KC)], KC, s_glu, D, evg, name="s_glu")

                def evo(j, b):
                    P.copy(BF[:, j, :], ps[b][:, :], reads=[("ps", b)], writes=[bfk(j)])
                linear_fm(lambda kk: z2[:, kk, :], [("z2", k) for k in range(KC)], KC, s_wout, D, evo, name="s_wout")
                stats_rstd(BF, BFALL)
                residual(1, r, 1)
                switch(K_GT, K_HT)
                mlp(1, r)
                dst = o_yp if t < 2 else o_ys
                base = t * TT if t < 2 else (t - 2) * TT
                for sub in range(4):
                    for kq in range(4):
                        b = P.bank()
                        for i in range(4):
                            k = kq * 4 + i
                            S.pe(lambda e, b=b, i=i, k=k, sub=sub: e.transpose(ps[b][:, i * 128:(i + 1) * 128], X[:, k, sub * 128:(sub + 1) * 128], ident[:]),
                                 reads=[("X", k), "ident"], writes=[("ps", b)])
                        oi = P.rr("blk32", 2)
                        P.copy(blk32[oi][:, :], ps[b][:, :], reads=[("ps", b)], writes=[("blk32", oi)])
                        r0 = base + sub * 128
                        S.dma("sp", lambda e, oi=oi, r0=r0, kq=kq, dst=dst: e.dma_start(out=dst[r0:r0 + 128, kq * 512:(kq + 1) * 512], in_=blk32[oi][:, :]),
                              reads=[("blk32", oi)], writes=[("oy", t, sub, kq)])

            def passD():
                PI = math.pi
                BnT = [V(4 * i, [128, 64, 16], F32) for i in range(4)]
                btmp = [V(16 + 4 * i, [128, 64, 16], F32) for i in range(2)]
                SO = V(24, [128, 4, 2, 64, 2], F32)
                H0 = V(28, [128, 2, 64, 2], F32)
                CnT = [V(32 + 4 * i, [128, 16, 64], F32) for i in range(4)]
                nslot = [0]

                def slot():
                    i = nslot[0]
                    nslot[0] += 1
                    assert i < 120
                    return V(48 + 0.25 * i, [128, 64], F32), ("prm", i)
                Ec = V(78, [128, 8, 256], F32)
                Es = V(86, [128, 8, 256], F32)
                U = V(94, [128, NTOK], BF16)
                HT = V(104, [128, 4, 2, NMAIN], BF16)
                Y0 = V(136, [128, 2, NMAIN], F32)
                UTFt = V(152, [128, NMAIN], F32)
                bri = [[V(160 + 4 * i + 2 * j, [128, 512], F32) for j in range(2)] for i in range(2)]
                tb = [V(168 + 2 * i, [128, 512], F32) for i in range(4)]
                tt1 = V(168, [128, 8, 128], F32)
                tt2 = V(172, [128, 8, 128], F32)
                wre = V(176, [128, 512], F32)
                wim = V(178, [128, 512], F32)
                qre = V(180, [128, 512], F32)
                qim = V(182, [128, 512], F32)
                bex = V(184, [128, 4, 128], F32)
                pads = V(186, [128, 4, 128], BF16)
                ytm = V(187, [128, 2, 128], F32)
                maskB = P.sb("maskB_sb", [128, 4, 8])
                maskC = P.sb("maskC_sb", [128, 4, 2])
                cst = P.sb("cst", [128, 4])
                selt = P.sb("selt", [128, 32])
                SL = P.sb("SL", [128, 4, 2])
                ini = P.sb("ini", [128, 4, 2])
                acc = P.sb("acc", [128, 4])
                S.dma("sp", ncd(maskB[:], maskB_d.rearrange("q p g -> p q g")), writes=["maskB"])
                S.dma("sp", ncd(maskC[:], maskC_d.rearrange("q p g -> p q g")), writes=["maskC"])
                S.pool(lambda e: e.memset(cst[:, 0:1], PI / 2), writes=["cst"])
                if SAMPLE:
                    S.dma("sp", ncd(selt[:], sel_d[0:1, :].partition_broadcast(128)[:, 0, :]), writes=["selt"])
                    for d_ in range(2):
                        S.dma("sp", ncd(H0[:, d_, :, :], h0_d[d_].rearrange("(st gl) n r -> (gl n) st r", gl=2)), writes=["H0"])
                prm = {}
                for d_ in range(2):
                    for ri, src in enumerate((s_bre, s_bim)):
                        S.dma("sp", ncd(BnT[d_ * 2 + ri], src[d_].rearrange("(st gl) n c -> (gl n) st c", gl=2)), writes=[("Bn", d_, ri)])
                    for ri, src in enumerate((s_cre, s_cim)):
                        S.dma("sp", ncd(CnT[d_ * 2 + ri], src[d_].rearrange("(ct g) c n -> (g c) ct n", g=8)), writes=[("Cn", d_, ri)])

                def ew(eng, fn, rk, wk):
                    getattr(S, eng)(fn, reads=rk, writes=wk)

                def tt(out, a, b, op, eng="dve"):
                    ew(eng, lambda e: e.tensor_tensor(out[0], a[0], b[0], op=op), [a[1], b[1]], [out[1]])

                def csq(cr, ci, nr, ni, tmp):
                    tt(tmp, ci, ci, ALU.mult)
                    tt(ni, cr, ci, ALU.mult)
                    tt(nr, cr, cr, ALU.mult)
                    tt(nr, nr, tmp, ALU.subtract)
                    ew("dve", lambda e: e.tensor_scalar(ni[0], ni[0], 2.0, None, op0=ALU.mult), [ni[1]], [ni[1]])

                for d_ in range(2):
                    are, aim, ldt = slot(), slot(), slot()
                    S.dma("sp", ncd(are[0], s_are[d_].rearrange("(st gl) n -> (gl n) st", gl=2)), writes=[are[1]])
                    S.dma("sp", ncd(aim[0], s_aim[d_].rearrange("(st gl) n -> (gl n) st", gl=2)), writes=[aim[1]])
                    for gl in range(2):
                        S.dma("sp", ncd(ldt[0][gl * 64:(gl + 1) * 64, :], s_ldt[d_].rearrange("(st gl) -> gl st", gl=2)[gl:gl + 1, :].partition_broadcast(64)[:, 0, :]),
                              writes=[ldt[1]])
                    dt_ = slot()
                    ew("act", lambda e, dt_=dt_, ldt=ldt: e.activation(dt_[0], ldt[0], AF.Exp), [ldt[1]], [dt_[1]])
                    rr_, th = slot(), slot()
                    tt(rr_, are, dt_, ALU.mult)
                    ew("act", lambda e, rr_=rr_: e.activation(rr_[0], rr_[0], AF.Exp), [rr_[1]], [rr_[1]])
                    tt(th, aim, dt_, ALU.mult)
                    kf, ki, tmp, xk = slot(), slot(), slot(), slot()
                    Pc, Ps = [], []
                    for k in range(9):
                        pc_, ps_ = slot(), slot()
                        Pc.append(pc_)
                        Ps.append(ps_)
                        sc2 = float(1 << k)
                        ew("dve", lambda e, xk=xk, th=th, sc2=sc2: e.tensor_scalar(xk[0], th[0], sc2, None, op0=ALU.mult), [th[1]], [xk[1]])
                        ew("dve", lambda e, kf=kf, xk=xk: e.tensor_scalar(kf[0], xk[0], 1.0 / (2 * PI), None, op0=ALU.mult), [xk[1]], [kf[1]])
                        ew("dve", lambda e, kf=kf, ki=ki: e.tensor_copy(ki[0].bitcast(I32), kf[0]), [kf[1]], [ki[1]])
                        ew("dve", lambda e, kf=kf, ki=ki: e.tensor_copy(kf[0], ki[0].bitcast(I32)), [ki[1]], [kf[1]])
                        ew("dve", lambda e, kf=kf, xk=xk: e.scalar_tensor_tensor(xk[0], kf[0], -2 * PI, xk[0], op0=ALU.mult, op1=ALU.add), [kf[1], xk[1]], [xk[1]])
                        ew("dve", lambda e, xk=xk: e.tensor_scalar(xk[0], xk[0], PI, -PI, op0=ALU.min, op1=ALU.max), [xk[1]], [xk[1]])
                        ew("act", lambda e, ps_=ps_, xk=xk: e.activation(ps_[0], xk[0], AF.Sin), [xk[1]], [ps_[1]])
                        ew("act", lambda e, tmp=tmp, xk=xk: e.activation(tmp[0], xk[0], AF.Abs), [xk[1]], [tmp[1]])
                        ew("act", lambda e, pc_=pc_, tmp=tmp: e.activation(pc_[0], tmp[0], AF.Sin, bias=cst[:, 0:1], scale=-1.0), [tmp[1], "cst"], [pc_[1]])
                    abr, abi = slot(), slot()
                    tt(abr, rr_, Pc[0], ALU.mult)
                    tt(abi, rr_, Ps[0], ALU.mult)
                    nr, den, t2, fr, fi = slot(), slot(), slot(), slot(), slot()
                    ew("dve", lambda e, nr=nr, abr=abr: e.tensor_scalar(nr[0], abr[0], -1.0, None, op0=ALU.add), [abr[1]], [nr[1]])
                    tt(den, are, are, ALU.mult)
                    tt(t2, aim, aim, ALU.mult)
                    tt(den, den, t2, ALU.add)
                    ew("dve", lambda e, den=den: e.reciprocal(den[0], den[0]), [den[1]], [den[1]])
                    tt(fr, nr, are, ALU.mult)
                    tt(t2, abi, aim, ALU.mult)
                    tt(fr, fr, t2, ALU.add)
                    tt(fr, fr, den, ALU.mult)
                    tt(fi, abi, are, ALU.mult)
                    tt(t2, nr, aim, ALU.mult)
                    tt(fi, fi, t2, ALU.subtract)
                    tt(fi, fi, den, ALU.mult)
                    bre_, bim_ = BnT[d_ * 2], BnT[d_ * 2 + 1]
                    kbr, kbi = ("Bn", d_, 0), ("Bn", d_, 1)
                    frb = fr[0].unsqueeze(2).to_broadcast([128, 64, 16])
                    fib = fi[0].unsqueeze(2).to_broadcast([128, 64, 16])
                    ew("pool", lambda e, bre_=bre_: e.tensor_copy(btmp[0], bre_), [kbr], ["btmp0"])
                    ew("dve", lambda e, bre_=bre_, frb=frb: e.tensor_tensor(bre_, bre_, frb, op=ALU.mult), [kbr, fr[1], "btmp0"], [kbr])
                    ew("dve", lambda e, bim_=bim_, fib=fib: e.tensor_tensor(btmp[1], bim_, fib, op=ALU.mult), [kbi, fi[1]], ["btmp1"])
                    ew("dve", lambda e, bre_=bre_: e.tensor_tensor(bre_, bre_, btmp[1], op=ALU.subtract), [kbr, "btmp1"], [kbr])
                    ew("dve", lambda e, bim_=bim_, frb=frb: e.tensor_tensor(bim_, bim_, frb, op=ALU.mult), [kbi, fr[1], "btmp1"], [kbi])
                    ew("dve", lambda e, fib=fib: e.tensor_tensor(btmp[1], btmp[0], fib, op=ALU.mult), ["btmp0", fi[1], kbi], ["btmp1"])
                    ew("dve", lambda e, bim_=bim_: e.tensor_tensor(bim_, bim_, btmp[1], op=ALU.add), [kbi, "btmp1"], [kbi])
                    prm[d_] = dict(r=rr_, Pc=Pc, Ps=Ps)
                    if d_ == 0:
                        for nm, sl_ in (("r", rr_), ("pc0", Pc[0]), ("ps0", Ps[0]), ("fr", fr), ("fi", fi), ("dt", dt_), ("th", th), ("are", are), ("aim", aim)):
                            P.dbg(nm, sl_[0], [128, 64], F32, [sl_[1]])
                        P.dbg("bbr", BnT[0], [128, 64, 16], F32, [("Bn", 0, 0)])
                    if SAMPLE:
                        cur = (abr, abi)
                        pp = [(slot(), slot()), (slot(), slot())]
                        for k in range(10):
                            nx = pp[k % 2]
                            csq(cur[0], cur[1], nx[0], nx[1], tmp)
                            cur = nx
                        A1 = cur
                        A2 = (slot(), slot())
                        csq(A1[0], A1[1], A2[0], A2[1], tmp)
                        A3 = (slot(), slot())
                        tt(A3[0], A2[0], A1[0], ALU.mult)
                        tt(tmp, A2[1], A1[1], ALU.mult)
                        tt(A3[0], A3[0], tmp, ALU.subtract)
                        tt(A3[1], A2[0], A1[1], ALU.mult)
                        tt(tmp, A2[1], A1[0], ALU.mult)
                        tt(A3[1], A3[1], tmp, ALU.add)
                        coef = []
                        for src in range(4):
                            cr_, ci_, nci_ = slot(), slot(), slot()
                            base = d_ * 16 + src * 4
                            sc_ = lambda e_: selt[:, base + e_: base + e_ + 1]
                            for (o_, parts) in ((cr_, (A1[0], A2[0], A3[0])), (ci_, (A1[1], A2[1], A3[1]))):
                                ew("dve", lambda e, o_=o_, p0=parts[0], s1=sc_(1): e.tensor_scalar(o_[0], p0[0], s1, None, op0=ALU.mult), [parts[0][1], "selt"], [o_[1]])
                                ew("dve", lambda e, o_=o_, p1=parts[1], s2_=sc_(2): e.scalar_tensor_tensor(o_[0], p1[0], s2_, o_[0], op0=ALU.mult, op1=ALU.add), [parts[1][1], "selt", o_[1]], [o_[1]])
                                ew("dve", lambda e, o_=o_, p2=parts[2], s3=sc_(3): e.scalar_tensor_tensor(o_[0], p2[0], s3, o_[0], op0=ALU.mult, op1=ALU.add), [parts[2][1], "selt", o_[1]], [o_[1]])
                            ew("dve", lambda e, cr_=cr_, s0=sc_(0): e.tensor_scalar(cr_[0], cr_[0], s0, None, op0=ALU.add), [cr_[1], "selt"], [cr_[1]])
                            ew("dve", lambda e, ci_=ci_, nci_=nci_: e.tensor_scalar(nci_[0], ci_[0], -1.0, None, op0=ALU.mult), [ci_[1]], [nci_[1]])
                            coef.append((cr_, ci_, nci_))
                        prm[d_]["coef"] = coef

                order = [0, 1] + ([4, 5, 6, 7, 8, 9, 2, 3] if SAMPLE else [])
                TK = [("t", i) for i in range(4)]

                for ctp in range(8):
                    for d_ in range(2):
                        pr = prm[d_]
                        st0 = ctp * 8
                        S.pool(lambda e: e.memset(Ec[:, :, 0:1], 1.0), writes=["Ec"])
                        S.pool(lambda e: e.memset(Es[:, :, 0:1], 0.0), writes=["Es"])
                        for k in range(8):
                            n = 1 << k
                            pcb = pr["Pc"][k][0][:, st0:st0 + 8].unsqueeze(2).to_broadcast([128, 8, n])
                            psb_ = pr["Ps"][k][0][:, st0:st0 + 8].unsqueeze(2).to_broadcast([128, 8, n])
                            pk = [pr["Pc"][k][1], pr["Ps"][k][1]]
                            S.pool(lambda e, n=n, pcb=pcb: e.tensor_tensor(tt1[:, :, 0:n], Ec[:, :, 0:n], pcb, op=ALU.mult), reads=["Ec"] + pk, writes=TK[0:2])
                            S.pool(lambda e, n=n, psb_=psb_: e.tensor_tensor(tt2[:, :, 0:n], Es[:, :, 0:n], psb_, op=ALU.mult), reads=["Es"] + pk, writes=TK[2:4])
                            S.dve(lambda e, n=n: e.tensor_tensor(Ec[:, :, n:2 * n], tt1[:, :, 0:n], tt2[:, :, 0:n], op=ALU.subtract), reads=TK, writes=["Ec"])
                            S.pool(lambda e, n=n, psb_=psb_: e.tensor_tensor(tt1[:, :, 0:n], Ec[:, :, 0:n], psb_, op=ALU.mult), reads=["Ec"] + pk, writes=TK[0:2])
                            S.pool(lambda e, n=n, pcb=pcb: e.tensor_tensor(tt2[:, :, 0:n], Es[:, :, 0:n], pcb, op=ALU.mult), reads=["Es"] + pk, writes=TK[2:4])
                            S.dve(lambda e, n=n: e.tensor_tensor(Es[:, :, n:2 * n], tt1[:, :, 0:n], tt2[:, :, 0:n], op=ALU.add), reads=TK, writes=["Es"])
                        if ctp == 0 and d_ == 0:
                            P.dbg("Ec", Ec, [128, 8, 256], F32, ["Ec"])
                            P.dbg("Es", Es, [128, 8, 256], F32, ["Es"])
                        for ci in range(2):
                            ct = 2 * ctp + ci
                            S.dma("sp", lambda e, d_=d_, ct=ct: e.dma_start(out=U, in_=UTs[d_, ct]), reads=[("UTs", d_)], writes=["U"])
                            if ct == 0:
                                P.dbg("U%d" % d_, U, [128, NTOK], BF16, ["U"])
                            for q in range(4):
                                st = 4 * ct + q
                                j8 = 4 * ci + q
                                for ri in range(2):
                                    bn = BnT[d_ * 2 + ri]
                                    S.dve(lambda e, bn=bn, ri=ri, st=st, q=q: e.tensor_tensor(
                                        bex[:, ri, :].rearrange("p (g c) -> p g c", c=16),
                                        bn[:, st, :].unsqueeze(1).to_broadcast([128, 8, 16]),
                                        maskB[:, q, :].unsqueeze(2).to_broadcast([128, 8, 16]), op=ALU.mult),
                                        reads=[("Bn", d_, ri), "maskB"], writes=[("bex", ri)])
                                    cn = CnT[d_ * 2 + ri]
                                    S.dve(lambda e, cn=cn, ri=ri, ct=ct, q=q: e.tensor_tensor(
                                        bex[:, 2 + ri, :].rearrange("p (g n) -> p g n", n=64),
                                        cn[:, ct, :].unsqueeze(1).to_broadcast([128, 2, 64]),
                                        maskC[:, q, :].unsqueeze(2).to_broadcast([128, 2, 64]), op=ALU.mult),
                                        reads=[("Cn", d_, ri), "maskC"], writes=[("bex", 2 + ri)])
                                bp = P.bank()
                                for i4 in range(4):
                                    S.pe(lambda e, bp=bp, i4=i4: e.transpose(ps[bp][:, i4 * 128:(i4 + 1) * 128], bex[:, i4, :], ident[:]),
                                         reads=[("bex", i4), "ident"], writes=[("ps", bp)])
                                S.act(lambda e, bp=bp: e.activation(pads[:, 0:3, :], ps[bp][:, 0:384].rearrange("p (a b) -> p a b", b=128), AF.Copy),
                                      reads=[("ps", bp)], writes=["pads"])
                                S.act(lambda e, bp=bp: e.activation(pads[:, 3, :], ps[bp][:, 384:512], AF.Copy, scale=-1.0),
                                      reads=[("ps", bp)], writes=["pads"])
                                if st == 0 and d_ == 0:
                                    P.dbg("pads", pads, [128, 4, 128], BF16, ["pads"])
                                rcol = pr["r"][0][:, st:st + 1]
                                rkey = pr["r"][1]
                                cos255 = Ec[:, j8, 255:256]
                                sin255 = Es[:, j8, 255:256]
                                cosT = Ec[:, j8, :].unsqueeze(1).to_broadcast([128, 2, 256])
                                sinT = Es[:, j8, :].unsqueeze(1).to_broadcast([128, 2, 256])
                                p8c = pr["Pc"][8][0][:, st:st + 1]
                                p8s = pr["Ps"][8][0][:, st:st + 1]
                                p8k = [pr["Pc"][8][1], pr["Ps"][8][1]]
                                v3 = lambda ap: ap.rearrange("p (s t) -> p s t", t=256)
                                prev_last = [None]

                                def cmul_to(o_r, o_i, a_r, a_i, b_r, b_i, rk, wk):
                                    S.dve(lambda e: e.tensor_scalar(acc[:, 0:1], a_i, b_i, None, op0=ALU.mult), reads=rk, writes=["acc0"])
                                    S.dve(lambda e: e.tensor_scalar(acc[:, 1:2], a_i, b_r, None, op0=ALU.mult), reads=rk, writes=["acc1"])
                                    S.dve(lambda e: e.scalar_tensor_tensor(o_r, a_r, b_r, acc[:, 0:1], op0=ALU.mult, op1=ALU.subtract), reads=rk + ["acc0"], writes=wk)
                                    S.dve(lambda e: e.scalar_tensor_tensor(o_i, a_r, b_i, acc[:, 1:2], op0=ALU.mult, op1=ALU.add), reads=rk + ["acc1"], writes=wk)

                                for t in order:
                                    col0 = t * TT
                                    mi = main_index(t)
                                    samp = t >= 2
                                    slot_i = (t - 2) // 2 if samp else None
                                    half_i = (t - 2) % 2 if samp else None
                                    bi = P.rr("bri", 2)
                                    b_re, b_im = P.bank(), P.bank()
                                    for ri, bb_ in enumerate((b_re, b_im)):
                                        S.pe(lambda e, ri=ri, bb_=bb_, col0=col0: e.matmul(ps[bb_][:, :], pads[:, ri, :], U[:, col0:col0 + TT], start=True, stop=True),
                                             reads=["pads", "U"], writes=[("ps", bb_)])
                                        S.act(lambda e, ri=ri, bb_=bb_, bi=bi: e.activation(bri[bi][ri], ps[bb_][:, :], AF.Copy), reads=[("ps", bb_)], writes=[("bri", bi, ri)])
                                    bre_, bim_ = bri[bi][0], bri[bi][1]
                                    kre, kim = ("bri", bi, 0), ("bri", bi, 1)
                                    EK = ["Ec", "Es"]
                                    cos2 = Ec[:, j8, :]
                                    sin2 = Es[:, j8, :]
                                    for sg_i in range(2):
                                        cs_ = slice(sg_i * 256, (sg_i + 1) * 256)
                                        S.pool(lambda e, bre_=bre_, cs_=cs_, cos2=cos2: e.tensor_tensor(tb[0][:, cs_], bre_[:, cs_], cos2, op=ALU.mult), reads=[kre] + EK, writes=[TK[0]])
                                        S.pool(lambda e, bim_=bim_, cs_=cs_, sin2=sin2: e.tensor_tensor(tb[1][:, cs_], bim_[:, cs_], sin2, op=ALU.mult), reads=[kim] + EK, writes=[TK[1]])
                                        S.pool(lambda e, bim_=bim_, cs_=cs_, cos2=cos2: e.tensor_tensor(tb[2][:, cs_], bim_[:, cs_], cos2, op=ALU.mult), reads=[kim] + EK, writes=[TK[2]])
                                        S.pool(lambda e, bre_=bre_, cs_=cs_, sin2=sin2: e.tensor_tensor(tb[3][:, cs_], bre_[:, cs_], sin2, op=ALU.mult), reads=[kre] + EK, writes=[TK[3]])
                                    S.dve(lambda e: e.tensor_tensor(wre, tb[0], tb[1], op=ALU.add), reads=TK[0:2], writes=["wre"])
                                    S.dve(lambda e: e.tensor_tensor(wim, tb[2], tb[3], op=ALU.subtract), reads=TK[2:4], writes=["wim"])
                                    for sg_i in range(2):
                                        cs_ = slice(sg_i * 256, (sg_i + 1) * 256)
                                        init_r, init_i = 0.0, 0.0
                                        ik = []
                                        if samp:
                                            gseg = half_i * 2 + sg_i
                                            ii = P.rr("ini", 2)
                                            if gseg > 0:
                                                pl = prev_last[0]
                                                cmul_to(ini[:, ii, 0:1], ini[:, ii, 1:2], pl[0], pl[1], p8c, p8s, [pl[2]] + p8k, [("ini", ii)])
                                                init_r, init_i = ini[:, ii, 0:1], ini[:, ii, 1:2]
                                                ik = [("ini", ii)]
                                            elif slot_i == 0:
                                                S.dve(lambda e: e.memset(acc[:, 2:4], 0.0), writes=["acc23"])
                                                for src in range(4):
                                                    cr_, ci_, nci_ = pr["coef"][src]
                                                    if src == 0:
                                                        vr, vi, vk = H0[:, d_, st, 0:1], H0[:, d_, st, 1:2], "H0"
                                                    else:
                                                        vr, vi, vk = SL[:, src, 0:1], SL[:, src, 1:2], ("SL", src)
                                                    ck = [cr_[1], ci_[1], nci_[1], vk, "acc23"]
                                                    S.dve(lambda e, vr=vr, cr_=cr_, st=st: e.scalar_tensor_tensor(acc[:, 2:3], vr, cr_[0][:, st:st + 1], acc[:, 2:3], op0=ALU.mult, op1=ALU.add), reads=ck, writes=["acc23"])
                                                    S.dve(lambda e, vi=vi, nci_=nci_, st=st: e.scalar_tensor_tensor(acc[:, 2:3], vi, nci_[0][:, st:st + 1], acc[:, 2:3], op0=ALU.mult, op1=ALU.add), reads=ck, writes=["acc23"])
                                                    S.dve(lambda e, vr=vr, ci_=ci_, st=st: e.scalar_tensor_tensor(acc[:, 3:4], vr, ci_[0][:, st:st + 1], acc[:, 3:4], op0=ALU.mult, op1=ALU.add), reads=ck, writes=["acc23"])
                                                    S.dve(lambda e, vi=vi, cr_=cr_, st=st: e.scalar_tensor_tensor(acc[:, 3:4], vi, cr_[0][:, st:st + 1], acc[:, 3:4], op0=ALU.mult, op1=ALU.add), reads=ck, writes=["acc23"])
                                                cmul_to(ini[:, ii, 0:1], ini[:, ii, 1:2], acc[:, 2:3], acc[:, 3:4],
                                                        pr["Pc"][0][0][:, st:st + 1], pr["Ps"][0][0][:, st:st + 1], ["acc23", pr["Pc"][0][1], pr["Ps"][0][1]], [("ini", ii)])
                                                init_r, init_i = ini[:, ii, 0:1], ini[:, ii, 1:2]
                                                ik = [("ini", ii)]
                                        S.dve(lambda e, cs_=cs_, init_r=init_r, rcol=rcol: e.tensor_tensor_scan(qre[:, cs_], rcol.to_broadcast([128, 256]), wre[:, cs_], init_r, ALU.mult, ALU.add),
                                              reads=["wre", rkey] + ik, writes=["qre"])
                                        S.dve(lambda e, cs_=cs_, init_i=init_i, rcol=rcol: e.tensor_tensor_scan(qim[:, cs_], rcol.to_broadcast([128, 256]), wim[:, cs_], init_i, ALU.mult, ALU.add),
                                              reads=["wim", rkey] + ik, writes=["qim"])
                                        if st == 0 and d_ == 0 and t == 0 and sg_i == 1:
                                            P.dbg("wre", wre, [128, 512], F32, ["wre"])
                                            P.dbg("qre", qre, [128, 512], F32, ["qre"])
                                            P.dbg("bre", bre_, [128, 512], F32, [kre])
                                            P.dbg("bim", bim_, [128, 512], F32, [kim])
                                            P.dbg("wim", wim, [128, 512], F32, ["wim"])
                                            P.dbg("qim", qim, [128, 512], F32, ["qim"])
                                            P.dbg("t0", tb[0], [128, 512], F32, [TK[0]])
                                            P.dbg("t1", tb[1], [128, 512], F32, [TK[1]])
                                        lastc = sg_i * 256 + 255
                                        if samp:
                                            li = P.rr("lastq", 2)
                                            S.dve(lambda e, li=li, lastc=lastc: e.tensor_copy(ini[:, 2 + li, 0:1], qre[:, lastc:lastc + 1]), reads=["qre"], writes=[("lastq", li)])
                                            S.dve(lambda e, li=li, lastc=lastc: e.tensor_copy(ini[:, 2 + li, 1:2], qim[:, lastc:lastc + 1]), reads=["qim"], writes=[("lastq", li)])
                                            prev_last[0] = (ini[:, 2 + li, 0:1], ini[:, 2 + li, 1:2], ("lastq", li))
                                            if slot_i != 0 and gseg == 3:
                                                cmul_to(SL[:, slot_i, 0:1], SL[:, slot_i, 1:2], ini[:, 2 + li, 0:1], ini[:, 2 + li, 1:2], cos255, sin255,
                                                        [("lastq", li), "Ec", "Es"], [("SL", slot_i)])
                                        else:
                                            seq = t * 2 + sg_i
                                            cmul_to(SO[:, seq, d_, st, 0:1], SO[:, seq, d_, st, 1:2], qre[:, lastc:lastc + 1], qim[:, lastc:lastc + 1], cos255, sin255,
                                                    ["qre", "qim", "Ec", "Es"], ["SO"])
                                    if mi is not None:
                                        mc = slice(mi * TT, (mi + 1) * TT)
                                        for sg_i in range(2):
                                            cs_ = slice(sg_i * 256, (sg_i + 1) * 256)
                                            S.pool(lambda e, cs_=cs_, cos2=cos2: e.tensor_tensor(tb[0][:, cs_], qre[:, cs_], cos2, op=ALU.mult), reads=["qre"] + EK, writes=[TK[0]])
                                            S.pool(lambda e, cs_=cs_, sin2=sin2: e.tensor_tensor(tb[1][:, cs_], qim[:, cs_], sin2, op=ALU.mult), reads=["qim"] + EK, writes=[TK[1]])
                                            S.pool(lambda e, cs_=cs_, sin2=sin2: e.tensor_tensor(tb[2][:, cs_], qre[:, cs_], sin2, op=ALU.mult), reads=["qre"] + EK, writes=[TK[2]])
                                            S.pool(lambda e, cs_=cs_, cos2=cos2: e.tensor_tensor(tb[3][:, cs_], qim[:, cs_], cos2, op=ALU.mult), reads=["qim"] + EK, writes=[TK[3]])
                                        S.dve(lambda e, mc=mc, q=q: e.tensor_tensor(HT[:, q, 0, mc], tb[0], tb[1], op=ALU.subtract), reads=TK[0:2], writes=[("HT", q)])
                                        S.dve(lambda e, mc=mc, q=q: e.tensor_tensor(HT[:, q, 1, mc], tb[2], tb[3], op=ALU.add), reads=TK[2:4], writes=[("HT", q)])
                                S.pool(lambda e, q=q: e.tensor_copy(cpads[:, q, :, :], pads[:, 2:4, :]), reads=["pads"], writes=[("cpads", q)])
                            HK = [("HT", q) for q in range(4)]
                            CK = [("cpads", q) for q in range(4)]
                            for mi, t in enumerate(MAIN_TILES):
                                if d_ == 0:
                                    b = P.bank()
                                    n_ = 0
                                    for q in range(4):
                                        for ri in range(2):
                                            S.pe(lambda e, b=b, q=q, ri=ri, mi=mi, n_=n_: e.matmul(ps[b][:, :], cpads[:, q, ri, :], HT[:, q, ri, mi * TT:(mi + 1) * TT],
                                                                                            start=(n_ == 0), stop=(n_ == 7)),
                                                 reads=HK + CK, writes=[("ps", b)])
                                            n_ += 1
                                    P.copy(Y0[:, ci, mi * TT:(mi + 1) * TT], ps[b][:, :], reads=[("ps", b)], writes=[("Y0", ci, mi)])
                                else:
                                    for sub in range(4):
                                        pos = mi * TT + sub * 128
                                        gp = rev_pos(t, sub)
                                        gt = gp // TT
                                        fpos = main_index(gt) * TT + gp % TT
                                        b = P.bank()
                                        n_ = 0
                                        for q in range(4):
                                            for ri in range(2):
                                                S.pe(lambda e, b=b, q=q, ri=ri, pos=pos, n_=n_: e.matmul(ps[b][:, 0:128], HT[:, q, ri, pos:pos + 128], cpads[:, q, ri, :],
                                                                                                  start=(n_ == 0), stop=(n_ == 7)),
                                                     reads=HK + CK, writes=[("ps", b)])
                                                n_ += 1
                                        yi = P.rr("ytm", 2)
                                        S.act(lambda e, b=b, yi=yi: e.activation(ytm[:, yi, :], ps[b][:, 0:128], AF.Copy), reads=[("ps", b)], writes=[("ytm", yi)])
                                        b2 = P.bank()
                                        S.pe(lambda e, b2=b2, yi=yi: e.matmul(ps[b2][:, 0:128], ytm[:, yi, :], jmat[:], start=True, stop=True),
                                             reads=[("ytm", yi), "jmat"], writes=[("ps", b2)])
                                        S.dve(lambda e, b2=b2, fpos=fpos, ci=ci: e.tensor_tensor(Y0[:, ci, fpos:fpos + 128], Y0[:, ci, fpos:fpos + 128], ps[b2][:, 0:128], op=ALU.add),
                                              reads=[("ps", b2), ("Y0", ci, fpos // TT)], writes=[("Y0", ci, fpos // TT)])
                            if d_ == 1:
                                S.dma("sp", lambda e, ct=ct: e.dma_start(out=UTFt, in_=UTF[ct]), reads=["UTF"], writes=["UTFt"])
                                yk = [("Y0", ci, m_) for m_ in range(len(MAIN_TILES))]
                                S.dve(lambda e, ci=ci, ct=ct: e.scalar_tensor_tensor(Y0[:, ci, :], UTFt, Dcol[:, ct:ct + 1], Y0[:, ci, :], op0=ALU.mult, op1=ALU.add),
                                      reads=["UTFt", "Dcol"] + yk, writes=yk)
                                S.dma("sp", lambda e, ci=ci, ct=ct: e.dma_start(out=Ys[ct], in_=Y0[:, ci, :]), reads=yk, writes=["Ys"])
                for seq in range(4):
                    for d_ in range(2):
                        b = P.bank()
                        for ri in range(2):
                            S.pe(lambda e, b=b, ri=ri, seq=seq, d_=d_: e.transpose(ps[b][0:64, ri * 128:(ri + 1) * 128], SO[:, seq, d_, :, ri], ident[:]),
                                 reads=["SO", "ident"], writes=[("ps", b)])
                        oi = 0
                        S.dve(lambda e, b=b, oi=oi: e.tensor_copy(sost[oi][0:64, :].rearrange("p (a r) -> p a r", r=2),
                                                                   ps[b][0:64, 0:256].rearrange("p (r a) -> p a r", r=2)), reads=[("ps", b)], writes=[("sost", oi)])
                        S.dma("sp", lambda e, oi=oi, seq=seq, d_=d_: e.dma_start(out=o_st[seq, d_], in_=sost[oi][0:64, :]), reads=[("sost", oi)], writes=[("o_st", seq, d_)])

            cpads = V(29, [128, 4, 2, 128], BF16)
            sost = [V(31, [128, 256], F32)]

            dummy = P.sb("dummy_bar", [128, 8])
            K_ATT = [("QT", i) for i in range(16)] + [("KT", i) for i in range(10)] + [("Vt", i) for i in range(4)] + [("OT", i) for i in range(16)] + [("rtmp", i) for i in range(4)] + [("cs", i) for i in range(4)]
            K_HT = [("hT", i) for i in range(64)]
            K_UST = [("ust16", i, j) for i in range(2) for j in range(2)] + [("ust32", i) for i in range(2)]
            K_GT = [("gtmp", i) for i in range(16)] + [("z2", i) for i in range(16)]
            K_X = [("X", k) for k in range(KC)]
            K_B = ["KTu", "Vu"] + [("Qb", i) for i in range(2)] + [("Ob", i) for i in range(2)]

            def switch(*groups):
                keys = []
                for g in groups:
                    keys += list(g)
                n = P.rr("dummy", 8)
                S.dve(lambda e, n=n: e.memset(dummy[:, n:n + 1], 0.0), writes=keys + [("dummy", n)])

            def passB():
                switch(K_X, BFALL, K_B)
                KTu = V(0, [128, 2, NKEY], BF16)
                Vu = V(17, [128, 34, 256], BF16)
                Qb = [V(34 + 4 * i, [128, 4, TT], BF16) for i in range(2)]
                Ob = [V(42 + 4 * i, [128, 4, TT], BF16) for i in range(2)]
                S.dma("pool", lambda e: e.dma_start(out=Vs[4096:4352, 0:256], in_=c_av[:, :]), writes=["Vs_c"])
                S.dma("pool", lambda e: e.dma_start(out=Vs[4096:4352, 256:1280], in_=c_bv[:, :]), writes=["Vs_c"])
                ck = Vu[:, 0:10, :].rearrange("p a b -> p (a b)").rearrange("p (k c) -> p k c", c=1280)
                S.dma("pool", lambda e: e.dma_start(out=ck[:, :, 0:256], in_=c_ak.rearrange("(k p) c -> p k c", p=128)), writes=["Vu"])
                S.dma("pool", lambda e: e.dma_start(out=ck[:, :, 256:1280], in_=c_bk.rearrange("(k p) c -> p k c", p=128)), writes=["Vu"])
                for kt in range(2):
                    for grp in range(3):
                        m0 = grp * 4
                        nm = 4 if grp < 2 else 2
                        b = P.bank()
                        for i in range(nm):
                            S.pe(lambda e, b=b, i=i, kt=kt, m0=m0: e.transpose(psb[b][:, i * 128:(i + 1) * 128], ck[:, kt, (m0 + i) * 128:(m0 + i + 1) * 128], identb[:]),
                                 reads=["Vu", "identb"], writes=[("ps", b)])
                        oi = P.rr("Ob", 2)
                        P.copy(Ob[oi][:, 0:nm, 0:128], psb[b][:, 0:nm * 128].rearrange("p (i t) -> p i t", t=128), reads=[("ps", b)], writes=[("Ob", oi)])
                        S.dma("sp", ncd(KTs[m0:m0 + nm, :, 4096 + kt * 128:4096 + (kt + 1) * 128].rearrange("h p t -> p h t"), Ob[oi][:, 0:nm, 0:128]),
                              reads=[("Ob", oi)], writes=["KTs_c"])
                NKT = NKEY // 128
                for unit in range(6):
                    gqa = unit < 2
                    if gqa:
                        g = unit
                        S.dma("sp", lambda e, g=g: e.dma_start(out=KTu[:, 0, :], in_=KTs[g]), reads=["KTs_c"], writes=["KTu"])
                        S.dma("sp", ncd(Vu[:, :, 0:128], Vs[:, g * 128:(g + 1) * 128].rearrange("(kt p) c -> p kt c", p=128)), reads=["Vs_c"], writes=["Vu"])
                        hm0, nm = 4 * g, 4
                    else:
                        h = unit - 2
                        for m in range(2):
                            S.dma("sp", lambda e, h=h, m=m: e.dma_start(out=KTu[:, m, :], in_=KTs[2 + 2 * h + m]), reads=["KTs_c"], writes=["KTu"])
                        S.dma("sp", ncd(Vu[:, :, :], Vs[:, 256 + 256 * h:512 + 256 * h].rearrange("(kt p) c -> p kt c", p=128)), reads=["Vs_c"], writes=["Vu"])
                        hm0, nm = 8 + 2 * h, 2
                    for qb in range(8):
                        qi = P.rr("Qb", 2)
                        oi = P.rr("Ob", 2)
                        S.dma("sp", ncd(Qb[qi][:, 0:nm, :], QTs[hm0:hm0 + nm, :, qb * TT:(qb + 1) * TT].rearrange("h p t -> p h t")), writes=[("Qb", qi)])
                        if gqa:
                            for h4 in range(4):
                                att_gqa(Qb[qi][:, h4, :], [("Qb", qi)],
                                        lambda kt: KTu[:, 0, kt * 128:(kt + 1) * 128], lambda kt: ["KTu"],
                                        lambda kt: Vu[:, kt, 0:128], lambda kt: ["Vu"],
                                        NKT, TT, Ob[oi][:, h4, :], [("Ob", oi)])
                        else:
                            att_diff(lambda m, qi=qi: Qb[qi][:, m, :], [("Qb", qi)],
                                     lambda m, kt: KTu[:, m, kt * 128:(kt + 1) * 128], lambda kt: ["KTu"],
                                     lambda kt, half: Vu[:, kt, half * 128:(half + 1) * 128], lambda kt: ["Vu"],
                                     NKT, TT, lambda half, oi=oi: Ob[oi][:, half, :], [("Ob", oi)])
                        S.dma("sp", lambda e, oi=oi, qb=qb, hm0=hm0, nm=nm: e.dma_start(out=OTs[qb][:, hm0:hm0 + nm, :], in_=Ob[oi][:, 0:nm, :]),
                              reads=[("Ob", oi)], writes=[("OTs", qb, unit)])
                switch(K_B, K_X, BFALL)

            def mark(name):
                if os.environ.get("KMARK"):
                    print("MARK", name, len(S.ops), flush=True)
            P.mark = mark
            mark("start_tiles")
            for t in (0, 1):
                switch(K_UST, K_HT, K_ATT)
                mark("passA%d" % t)
                passA(t)
                mark("att%d" % t)
                attention_prompt_tile()
                mark("passC%d" % t)
                passC(t)
            mark("end_prompt_l0")
            if SAMPLE:
                for t in range(2, NT_ALL):
                    switch(K_UST, K_HT, K_ATT)
                    passA(t)
                passB()
                for t in range(2, NT_ALL):
                    switch(K_UST, K_HT, K_ATT)
                    S.dma("sp", lambda e, t=t: e.dma_start(out=X, in_=XTs[t]), reads=[("XTs", t)], writes=K_X)
                    S.dma("sp", lambda e, t=t: e.dma_start(out=OTt, in_=OTs[t - 2]), reads=[("OTs", t - 2, u) for u in range(6)], writes=[("OT", k) for k in range(KC)])
                    passC(t)
            if STAGE >= 2:
                S.fence()
                passD()
            if STAGE >= 3:
                S.fence()
                for t in MAIN_TILES:
                    passE(t)
            S.emit(ctx)
        return nc


_CACHE = {}


def _get_prog():
    if "nc" not in _CACHE:
        _CACHE["nc"] = Prog().build()
    return _CACHE["nc"]


def _rope_tables():
    pos = np.arange(4096)
    row = (pos // 64).astype(np.float32)
    col = (pos % 64).astype(np.float32)
    inv = (np.float32(10000.0) ** (-np.arange(32, dtype=np.float32) / np.float32(32))).astype(np.float32)
    ang = np.concatenate([row[:, None] * inv[None, :], col[:, None] * inv[None, :]], axis=-1).astype(np.float32)
    return np.cos(ang).astype(np.float32), np.sin(ang).astype(np.float32)


def kernel(x_prompt, x_sample, c, cache_a_k, cache_a_v, cache_b_k, cache_b_v, state_ssm, c_ctx,
           ada_w, ada_b, norm_g, mlp_w1, mlp_w2, attn_w_in, attn_w_out, attn_qk_norm, diff_lambda,
           diff_subln, ssm_w_in, ssm_a_re, ssm_a_im, ssm_log_dt, ssm_b_re, ssm_b_im, ssm_c_re,
           ssm_c_im, ssm_d, ssm_glu_w, ssm_w_out):
    f = lambda a: np.ascontiguousarray(np.asarray(a, dtype=np.float32))
    nc = _get_prog()
    x_prompt = f(x_prompt)
    x_sample = f(x_sample)
    c_ctx = f(c_ctx)
    c = f(c)
    ident = np.eye(128, dtype=np.float32)
    jmat = np.ascontiguousarray(ident[::-1])
    maskB = np.zeros((4, 128, 8), np.float32)
    maskC = np.zeros((4, 128, 2), np.float32)
    for q in range(4):
        for gl in range(2):
            maskB[q, gl * 64:(gl + 1) * 64, 2 * q + gl] = 1.0
            g8 = 2 * q + gl
            maskC[q, g8 * 16:(g8 + 1) * 16, gl] = 1.0
    shared = dict(
        ident=ident, jmat=jmat, maskB=maskB, maskC=maskC,
        ada_w=f(ada_w), ada_b=f(ada_b), norm_g=f(norm_g), mlp_w1=f(mlp_w1), mlp_w2=f(mlp_w2),
        attn_w_in=f(attn_w_in).reshape(D, ATTN_IN), attn_w_out=f(attn_w_out).reshape(D, D),
        attn_qk_norm=f(attn_qk_norm).reshape(2, 128), diff_lambda=f(diff_lambda).reshape(4, 128),
        diff_subln=f(diff_subln).reshape(256), ssm_w_in=f(ssm_w_in).reshape(D, D),
        ssm_a_re=f(ssm_a_re).reshape(2, 128, 64), ssm_a_im=f(ssm_a_im).reshape(2, 128, 64),
        ssm_log_dt=f(ssm_log_dt).reshape(2, 128), ssm_b_re=f(ssm_b_re).reshape(2, 128, 64, 16),
        ssm_b_im=f(ssm_b_im).reshape(2, 128, 64, 16), ssm_c_re=f(ssm_c_re).reshape(2, 128, 16, 64),
        ssm_c_im=f(ssm_c_im).reshape(2, 128, 16, 64), ssm_d=f(ssm_d).reshape(D),
        ssm_glu_w=f(ssm_glu_w).reshape(D, D), ssm_w_out=f(ssm_w_out).reshape(D, D))
    cos_t, sin_t = _rope_tables()
    in_maps = []
    orders = []
    for core in range(8):
        m = dict(shared)
        b, j = core // 4, core % 4
        xp = x_prompt[core * 4:(core + 1) * 4].reshape(NPT, D)
        m["cond"] = np.stack([c_ctx, c[b]], 0)
        if SAMPLE:
            others = [i for i in range(4) if i != j]
            order = [j] + others
            orders.append(order)
            idx = np.concatenate([np.arange(ch * 1024, (ch + 1) * 1024) for ch in order])
            m["xall"] = np.ascontiguousarray(np.concatenate([xp, x_sample[b][idx]], 0))
            m["rope"] = np.ascontiguousarray(np.stack([cos_t[idx], sin_t[idx]], 0))
            m["cache_ak"] = f(cache_a_k)[b, 0].reshape(256, 256)
            m["cache_av"] = f(cache_a_v)[b, 0].reshape(256, 256)
            m["cache_bk"] = f(cache_b_k)[b, 0].reshape(256, 1024)
            m["cache_bv"] = f(cache_b_v)[b, 0].reshape(256, 1024)
            m["h0"] = np.ascontiguousarray(f(state_ssm)[b, 0])
            sel = np.zeros((2, 4, 4), np.float32)
            sel[0, 0, j] = 1.0
            sel[1, 0, 3 - j] = 1.0
            for s_, i in enumerate(others, start=1):
                if i < j:
                    sel[0, s_, j - 1 - i] = 1.0
                if i > j:
                    sel[1, s_, i - j - 1] = 1.0
            m["sel"] = sel.reshape(1, 32)
        else:
            m["xall"] = xp
        in_maps.append(m)
    res = run_bass_kernel_spmd(nc, in_maps, core_ids=list(range(8)))
    R = res.results
    _CACHE["dbg"] = {k: v for k, v in R[0].items() if k.startswith("dbg_")}
    cat = lambda name: np.concatenate([R[i][name] for i in range(8)], 0)
    y_p = cat("o_yp").reshape(32, 256, D)
    ak = cat("o_ak").reshape(32, 1, 256, 2, 128)
    av = cat("o_av").reshape(32, 1, 256, 2, 128)
    bk = cat("o_bk").reshape(32, 1, 256, 4, 2, 128)
    bv = cat("o_bv").reshape(32, 1, 256, 4, 256)
    st = cat("o_st").reshape(32, 1, 2, 128, 64, 2)
    y_s = np.zeros((2, 4096, D), np.float32)
    for core in range(8):
        b, j = core // 4, core % 4
        y_s[b, j * 1024:(j + 1) * 1024] = R[core]["o_ys"]
    return (y_p, y_s, ak, av, bk, bv, st)
```
